# Optimizing a Trainium2 kernel written in Bass

```python
import jax, jax.numpy as jnp
from jax import lax
import numpy as np

D_MODEL = 2048
BATCH = 8
SEQ = 2048
DEPTH = 4
DEC_BATCH = 16
DEC_SEQ = 16
PAST_LEN = 2048

CHUNK = 64
Q_BLOCK = 128
N_HEADS = 8
NOPE_DIM = 128
ROPE_DIM = 64
V_DIM = 128
QK_DIM = NOPE_DIM + ROPE_DIM
Q_LORA = 512
KV_LORA = 512
D_ATTN = N_HEADS * V_DIM
D_CONV = D_MODEL - D_ATTN
D_MIX = D_ATTN + D_CONV
CONV_W = 3
D_FF = 4 * D_MODEL
IN_COLS = Q_LORA + KV_LORA + ROPE_DIM + 3 * D_CONV
SPLITS = (Q_LORA, Q_LORA + KV_LORA, Q_LORA + KV_LORA + ROPE_DIM,
          Q_LORA + KV_LORA + ROPE_DIM + D_CONV, Q_LORA + KV_LORA + ROPE_DIM + 2 * D_CONV)
ROPE_BASE = 10000.0
EPS = 1e-6
ATTN_SCALE = QK_DIM ** -0.5

kernel_name = "hymba_mla_shortconv_adaln_stream_step"


def rms_norm(x, gain=None):
    xf = x.astype(jnp.float32)
    y = xf * lax.rsqrt(jnp.mean(xf * xf, axis=-1, keepdims=True) + EPS)
    if gain is not None:
        y = y * gain.astype(jnp.float32)
    return y.astype(x.dtype)


def rope_tables(pos):
    half = ROPE_DIM // 2
    inv = ROPE_BASE ** (-jnp.arange(half, dtype=jnp.float32) / half)
    ang = pos.astype(jnp.float32)[:, None] * inv[None, :]
    return jnp.cos(ang), jnp.sin(ang)


def rope(x, cos, sin):
    xf = x.astype(jnp.float32)
    x1, x2 = xf[..., :ROPE_DIM // 2], xf[..., ROPE_DIM // 2:]
    out = jnp.concatenate([x1 * cos - x2 * sin, x2 * cos + x1 * sin], axis=-1)
    return out.astype(x.dtype)


def modulation(c, w_ada, b_ada):
    m = jax.nn.silu(c) @ w_ada + b_ada
    return jnp.split(m[:, None, :], 6, axis=-1)


def expand_latent(c_kv, w_ukv):
    b, t, _ = c_kv.shape
    kv = (c_kv @ w_ukv).reshape(b, t, N_HEADS, NOPE_DIM + V_DIM)
    return kv[..., :NOPE_DIM], kv[..., NOPE_DIM:]


def attend(q_nope, q_pe, k_nope, k_pe, v, mask):
    s = (jnp.einsum('bqhd,bkhd->bhqk', q_nope, k_nope)
         + jnp.einsum('bqhr,bkr->bhqk', q_pe, k_pe)).astype(jnp.float32) * ATTN_SCALE
    if mask is not None:
        s = jnp.where(mask, s, -jnp.inf)
    p = jax.nn.softmax(s, axis=-1).astype(v.dtype)
    return jnp.einsum('bhqk,bkhd->bqhd', p, v)


def prompt_attention(q_nope, q_pe, k_nope, k_pe, v):
    b, s = q_nope.shape[:2]
    key_chunk = jnp.arange(s) // CHUNK

    def block(i):
        start = i * Q_BLOCK
        qn = lax.dynamic_slice_in_dim(q_nope, start, Q_BLOCK, axis=1)
        qp = lax.dynamic_slice_in_dim(q_pe, start, Q_BLOCK, axis=1)
        q_chunk = (start + jnp.arange(Q_BLOCK)) // CHUNK
        mask = key_chunk[None, :] <= q_chunk[:, None]
        return attend(qn, qp, k_nope, k_pe, v, mask[None, None])

    out = lax.map(block, jnp.arange(s // Q_BLOCK))
    return jnp.moveaxis(out, 0, 1).reshape(b, s, D_ATTN)


def short_conv(u, prev, w):
    t = u.shape[1]
    full = jnp.concatenate([prev.astype(u.dtype), u], axis=1)
    y = w[0] * full[:, 0:t]
    for k in range(1, CONV_W):
        y = y + w[k] * full[:, k:k + t]
    return y, full[:, t:]


def trunk_layer(x, c, cos, sin, past_ckv, past_kpe, conv_prev,
                w_ada, b_ada, w_in, q_norm, w_uq, kv_norm, w_ukv, conv_w,
                attn_out_norm, conv_out_norm, w_out, w_up, w_down):
    b, t, _ = x.shape
    sh_a, sc_a, g_a, sh_m, sc_m, g_m = modulation(c, w_ada, b_ada)
    h = rms_norm(x) * (1 + sc_a) + sh_a
    z = h @ w_in
    q_lat, kv_lat, k_pe, gate_b, gate_c, u = jnp.split(z, SPLITS, axis=-1)
    q = (rms_norm(q_lat, q_norm) @ w_uq).reshape(b, t, N_HEADS, QK_DIM)
    q_nope = q[..., :NOPE_DIM]
    q_pe = rope(q[..., NOPE_DIM:], cos[:, None, :], sin[:, None, :])
    c_kv = rms_norm(kv_lat, kv_norm)
    k_pe = rope(k_pe, cos, sin)
    if past_ckv is None:
        k_nope, v = expand_latent(c_kv, w_ukv)
        attn = prompt_attention(q_nope, q_pe, k_nope, k_pe, v)
    else:
        ckv_all = jnp.concatenate([past_ckv.astype(c_kv.dtype), c_kv], axis=1)
        kpe_all = jnp.concatenate([past_kpe.astype(k_pe.dtype), k_pe], axis=1)
        k_nope, v = expand_latent(ckv_all, w_ukv)
        attn = attend(q_nope, q_pe, k_nope, kpe_all, v, None).reshape(b, t, D_ATTN)
    conv_y, conv_state = short_conv(gate_c * u, conv_prev, conv_w)
    conv_o = gate_b * conv_y
    mix = jnp.concatenate([rms_norm(attn, attn_out_norm),
                           rms_norm(conv_o, conv_out_norm)], axis=-1) @ w_out
    x = x + g_a * mix
    h = rms_norm(x) * (1 + sc_m) + sh_m
    x = x + g_m * (jnp.square(jax.nn.relu(h @ w_up)) @ w_down)
    return x, c_kv, k_pe, conv_state


def setup_inputs(seed: int = 0) -> dict:
    key = jax.random.key(seed)
    ks = jax.random.split(key, 24)
    f32 = jnp.float32
    nrm = lambda k, shape, s: jax.random.normal(k, shape, f32) * s
    return {
        "x_prompt": nrm(ks[0], (BATCH, SEQ, D_MODEL), 1.0),
        "x_sample": nrm(ks[1], (DEC_BATCH, DEC_SEQ, D_MODEL), 1.0),
        "c_prompt": nrm(ks[2], (BATCH, D_MODEL), 1.0),
        "c_sample": nrm(ks[3], (DEC_BATCH, D_MODEL), 1.0),
        "cache_kv_latent": nrm(ks[4], (DEPTH, DEC_BATCH, PAST_LEN, KV_LORA), 1.0),
        "cache_k_rope": nrm(ks[5], (DEPTH, DEC_BATCH, PAST_LEN, ROPE_DIM), 1.0),
        "state_conv": nrm(ks[6], (DEPTH, DEC_BATCH, CONV_W - 1, D_CONV), 1.0),
        "w_ada": nrm(ks[7], (DEPTH, D_MODEL, 6 * D_MODEL), 0.3 * D_MODEL ** -0.5),
        "b_ada": nrm(ks[8], (DEPTH, 6 * D_MODEL), 0.01),
        "w_in": nrm(ks[9], (DEPTH, D_MODEL, IN_COLS), D_MODEL ** -0.5),
        "q_norm": 1.0 + nrm(ks[10], (DEPTH, Q_LORA), 0.01),
        "w_uq": nrm(ks[11], (DEPTH, Q_LORA, N_HEADS * QK_DIM), Q_LORA ** -0.5),
        "kv_norm": 1.0 + nrm(ks[12], (DEPTH, KV_LORA), 0.01),
        "w_ukv": nrm(ks[13], (DEPTH, KV_LORA, N_HEADS * (NOPE_DIM + V_DIM)), KV_LORA ** -0.5),
        "conv_w": nrm(ks[14], (DEPTH, CONV_W, D_CONV), CONV_W ** -0.5),
        "attn_out_norm": 1.0 + nrm(ks[15], (DEPTH, D_ATTN), 0.01),
        "conv_out_norm": 1.0 + nrm(ks[16], (DEPTH, D_CONV), 0.01),
        "w_out": nrm(ks[17], (DEPTH, D_MIX, D_MODEL), D_MIX ** -0.5),
        "w_up": nrm(ks[18], (DEPTH, D_MODEL, D_FF), D_MODEL ** -0.5),
        "w_down": nrm(ks[19], (DEPTH, D_FF, D_MODEL), D_FF ** -0.5),
        "final_norm": 1.0 + nrm(ks[20], (D_MODEL,), 0.01),
    }


def reference(x_prompt, x_sample, c_prompt, c_sample, cache_kv_latent, cache_k_rope, state_conv,
              w_ada, b_ada, w_in, q_norm, w_uq, kv_norm, w_ukv, conv_w,
              attn_out_norm, conv_out_norm, w_out, w_up, w_down, final_norm):
    b_p, s_p, _ = x_prompt.shape
    b_s, s_s, _ = x_sample.shape
    past_len = cache_kv_latent.shape[2]
    cos_p, sin_p = rope_tables(jnp.arange(s_p))
    cos_s, sin_s = rope_tables(past_len + jnp.arange(s_s))
    zero_conv = jnp.zeros((b_p, CONV_W - 1, D_CONV), x_prompt.dtype)

    xp, xs = x_prompt, x_sample
    ckv_p, kpe_p, conv_p, ckv_s, kpe_s, conv_s = [], [], [], [], [], []
    for l in range(DEPTH):
        params = (w_ada[l], b_ada[l], w_in[l], q_norm[l], w_uq[l], kv_norm[l], w_ukv[l],
                  conv_w[l], attn_out_norm[l], conv_out_norm[l], w_out[l], w_up[l], w_down[l])
        xp, a, k, cs = trunk_layer(xp, c_prompt, cos_p, sin_p, None, None, zero_conv, *params)
        ckv_p.append(a); kpe_p.append(k); conv_p.append(cs)
        xs, a, k, cs = trunk_layer(xs, c_sample, cos_s, sin_s, cache_kv_latent[l],
                                   cache_k_rope[l], state_conv[l], *params)
        ckv_s.append(a); kpe_s.append(k); conv_s.append(cs)

    y_prompt = rms_norm(xp, final_norm)
    y_sample = rms_norm(xs, final_norm)
    return (y_prompt, y_sample,
            jnp.stack(ckv_p), jnp.stack(kpe_p), jnp.stack(conv_p),
            jnp.stack(ckv_s), jnp.stack(kpe_s), jnp.stack(conv_s))
```

```python
import contextlib
import numpy as np
import concourse.bass as bass
import concourse.mybir as mybir
from concourse.bass_utils import run_bass_kernel_spmd

F32, BF16 = mybir.dt.float32, mybir.dt.bfloat16
AF = mybir.ActivationFunctionType
ALU = mybir.AluOpType

L = 4
D = 2048
SEQ = 2048
NH = 8
TB = 512
EPS = 1e-6
ATTN_SCALE = 192.0 ** -0.5
INC = 4160
DFF = 8192
NTAB = 2048 + 32


class Sched:
    def __init__(self, nc, es):
        self.nc = nc
        self.es = es
        self.eng = {'pe': nc.tensor, 'act': nc.scalar, 'dve': nc.vector, 'pool': nc.gpsimd, 'sp': nc.sync}
        self.h = {}
        self.cnt = {}
        for e in self.eng:
            self.h[e] = es.enter_context(nc.semaphore("s_" + e))
            self.cnt[e] = 0
        self.seen = {e: {} for e in self.eng}
        self.snap = {}
        self.bufs = {}
        self.nwaits = 0

    def _wait(self, e, tok):
        name, val = tok
        if name == 'pe' and e == 'pe':
            return
        if self.seen[e].get(name, 0) >= val:
            return
        self.eng[e].wait_ge(self.h[name], val)
        self.nwaits += 1
        self.seen[e][name] = val
        sn = self.snap.get(tok)
        if sn:
            se = self.seen[e]
            for k, v in sn.items():
                if se.get(k, 0) < v:
                    se[k] = v

    def _deps(self, e, r, w):
        deps = set()
        for k in r:
            b = self.bufs.get(k)
            if b and b[0]:
                deps.add(b[0])
        for k in w:
            b = self.bufs.get(k)
            if b:
                if b[0]:
                    deps.add(b[0])
                for n, v in b[1].items():
                    deps.add((n, v))
        for t in sorted(deps):
            self._wait(e, t)

    def _record(self, tok, r, w):
        for k in r:
            b = self.bufs.setdefault(k, [None, {}])
            if b[1].get(tok[0], 0) < tok[1]:
                b[1][tok[0]] = tok[1]
        for k in w:
            self.bufs[k] = [tok, {}]

    def op(self, e, fn, r=(), w=(), inc=True):
        self._deps(e, r, w)
        if e in ('dve', 'act') and self.cnt[e] > 0 and _CACHE.get('selfser', 1):
            self._wait(e, (e, self.cnt[e]))
        ins = fn()
        if inc:
            self.cnt[e] += 1
            ins.then_inc(self.h[e], 1)
            tok = (e, self.cnt[e])
            self.snap[tok] = dict(self.seen[e])
        else:
            tok = (e, self.cnt[e] + 1)
        self._record(tok, r, w)
        return ins

    def dma(self, q, out, in_, sem, r=(), w=(), nonc=False):
        for k in w:
            b = self.bufs.get(k)
            if b and b[0] and b[0][0] == sem and not b[1]:
                b[0] = None
        self._deps(q, r, w)
        if sem not in self.h:
            self.h[sem] = self.es.enter_context(self.nc.semaphore("d_" + sem))
            self.cnt[sem] = 0
        if nonc:
            with self.nc.allow_non_contiguous_dma(reason="small strided vector"):
                ins = self.eng[q].dma_start(out=out, in_=in_)
        else:
            ins = self.eng[q].dma_start(out=out, in_=in_)
        self.cnt[sem] += 16
        ins.then_inc(self.h[sem], 16)
        tok = (sem, self.cnt[sem])
        self.snap[tok] = dict(self.seen[q])
        self._record(tok, r, w)

    def finish(self, e='sp'):
        for name, h in self.h.items():
            if name in self.eng:
                continue
            self._wait(e, (name, self.cnt[name]))
        for name in ('act', 'dve', 'pe'):
            if self.cnt[name] > 0:
                self._wait(e, (name, self.cnt[name]))


class _Stop(Exception):
    pass


class Seg:
    def __init__(self, c0, n, e):
        self.c0, self.n, self.e = c0, n, e


def build_program(NL=L, NBLK=4, SAMPLE=True):
    nc = bass.Bass("TRN2", target_bir_lowering=False)
    dt_in = lambda n, s: nc.dram_tensor(n, s, F32, kind="ExternalInput").ap()
    dt_out = lambda n, s: nc.dram_tensor(n, s, F32, kind="ExternalOutput").ap()
    xp = dt_in("xp", [SEQ, D])
    xs = dt_in("xs", [32, D])
    vecs = dt_in("vecs", [640, 128])
    kvn = dt_in("kvn", [NL, 512])
    pkv = dt_in("pkv", [NL, 2, 2048, 512])
    pkr = dt_in("pkr", [NL, 2, 2048, 64])
    sconv = dt_in("sconv", [NL, 2, 2, 1024])
    tabF = dt_in("tabF", [128, NTAB])
    tabT = dt_in("tabT", [NTAB, 128])
    w_ada = dt_in("w_ada", [NL, D, 6 * D])
    w_in = dt_in("w_in", [NL, D, INC])
    w_uq = dt_in("w_uq", [NL, 512, 1536])
    w_ukv = dt_in("w_ukv", [NL, 512, 2048])
    w_out = dt_in("w_out", [NL, D, D])
    w_up = dt_in("w_up", [NL, D, DFF])
    w_down = dt_in("w_down", [NL, DFF, D])
    o_yp = dt_out("o_yp", [SEQ, D])
    o_ys = dt_out("o_ys", [32, D])
    o_ckvp = dt_out("o_ckvp", [L, SEQ, 512])
    o_krp = dt_out("o_krp", [L, SEQ, 64])
    o_convp = dt_out("o_convp", [L, 2, 1024])
    o_ckvs = dt_out("o_ckvs", [L, 32, 512])
    o_krs = dt_out("o_krs", [L, 32, 64])
    o_convs = dt_out("o_convs", [L, 2, 2, 1024])

    with contextlib.ExitStack() as es:
        sbt = lambda n, s, d: es.enter_context(nc.sbuf_tensor(n, s, d))
        S = Sched(nc, es)
        cache = sbt("cache", [128, L * 4 * 1536], BF16)
        ownckv = sbt("ownckv", [128, 4, 512], BF16)
        krc = sbt("krc", [128, L * 1536], BF16)
        ownkr = sbt("ownkr", [128, 512], BF16)
        xT = sbt("xT", [128, 16, 512], F32)
        R1 = sbt("R1", [128, 16, 512], BF16)
        convn = sbt("convn", [128, 8, 512], BF16)
        R46 = sbt("R46", [128, 4096], F32)
        ring = sbt("ring", [128, 4, 4096], BF16)
        sq = sbt("sq", [128, 2, 512], BF16)
        rst = sbt("rst", [128, 2, 512], F32)
        fsc = sbt("fsc", [128, 3, 512], F32)
        ql = sbt("ql", [128, 4, 512], BF16)
        pT = sbt("pT", [128, 2, 512], BF16)
        cst = sbt("cst", [128, 512], F32)
        krt = sbt("krt", [128, 3, 128], F32)
        ttab = sbt("ttab", [128, 128], F32)
        csT = sbt("csT", [128, 512], F32)
        kvt = sbt("kvt", [128, 512], F32)
        ident = sbt("ident", [128, 128], F32)
        ones = sbt("ones", [128, 128], BF16)
        onesr = sbt("onesr", [1, 4], BF16)
        epsc = sbt("epsc", [128, 1], F32)
        vT = sbt("vT", [128, 640], F32)
        sT = sbt("sT", [128, 16, 3], BF16)
        mod = sbt("mod", [128, L, 3, 96], F32)
        hal = sbt("hal", [128, L, 8, 2], F32)
        cs2 = sbt("cs2", [128, 2, 8], F32)
        sm = sbt("sm", [128, 8], F32)
        kown = sbt("kown", [128, 32], BF16)
        vown = sbt("vown", [16, 128], BF16)
        rtmp = sbt("rtmp", [128, 512], BF16)
        ps = [es.enter_context(nc.psum_tensor("ps%d" % i, [128, 512], F32)) for i in range(8)]

        st = {'gb': 0, 'piece': 0, 'alt': 0, 'sq': 0, 'f': 0, 'pt': 0}

        def gbank():
            st['gb'] = (st['gb'] + 1) % 4
            return st['gb']

        def alt():
            st['alt'] ^= 1
            return 'act' if st['alt'] else 'dve'

        def mm(out, lhsT, rhs, start, stop, r, w, inc=None):
            if inc is None:
                inc = stop
            S.op('pe', lambda: nc.tensor.matmul(out, lhsT=lhsT, rhs=rhs, start=start, stop=stop), r=r, w=w, inc=inc)

        def tr(out, in_, rows, r, w, inc=True):
            if _CACHE.get('mmtr', 1):
                S.op('pe', lambda: nc.tensor.matmul(out, lhsT=in_, rhs=ident[0:rows, 0:rows], start=True, stop=True), r=r, w=w, inc=inc)
            else:
                S.op('pe', lambda: nc.tensor.transpose(out=out, in_=in_, identity=ident[0:rows, 0:rows]), r=r, w=w, inc=inc)

        def act(out, in_, func, r, w, bias=None, scale=None, accum=None):
            kw = {}
            if bias is not None:
                kw['bias'] = bias
            if scale is not None:
                kw['scale'] = scale
            if accum is not None:
                kw['accum_out'] = accum
            S.op('act', lambda: nc.scalar.activation(out=out, in_=in_, func=func, **kw), r=r, w=w)

        def cp(e, out, in_, r, w):
            if e == 'act':
                S.op('act', lambda: nc.scalar.activation(out=out, in_=in_, func=AF.Copy), r=r, w=w)
            else:
                S.op('dve', lambda: nc.vector.tensor_scalar(out=out, in0=in_, scalar1=1.0, scalar2=None, op0=ALU.mult), r=r, w=w)

        def tt(out, a, b, op, r, w):
            S.op('dve', lambda: nc.vector.tensor_tensor(out=out, in0=a, in1=b, op=op), r=r, w=w)

        def stt(out, in0, scalar, in1, op0, op1, r, w):
            S.op('dve', lambda: nc.vector.scalar_tensor_tensor(out=out, in0=in0, scalar=scalar, in1=in1, op0=op0, op1=op1), r=r, w=w)

        def ts(out, in0, s1, s2, op0, op1, r, w):
            if s2 is None:
                S.op('dve', lambda: nc.vector.tensor_scalar(out=out, in0=in0, scalar1=s1, scalar2=None, op0=op0), r=r, w=w)
            else:
                S.op('dve', lambda: nc.vector.tensor_scalar(out=out, in0=in0, scalar1=s1, scalar2=s2, op0=op0, op1=op1), r=r, w=w)

        def fscr():
            st['f'] = (st['f'] + 1) % 3
            return st['f']

        def new_piece():
            s = st['piece'] % st.get('nslots', 4)
            st['piece'] += 1
            return s

        def slot(s):
            if s < 4:
                return ring[:, s, :]
            return cache[:, 8192 + (s - 4) * 4096:8192 + (s - 3) * 4096]

        def load_std(W, kc0, nk, c0, ncols):
            s = new_piece()
            dst = slot(s)[:, 0:nk * ncols].rearrange("p (k n) -> p k n", k=nk)
            src = W[kc0 * 128:(kc0 + nk) * 128, c0:c0 + ncols].rearrange("(k p) n -> p k n", p=128)
            S.dma('pool', dst, src, sem="w%d" % s, w=["w%d" % s])
            return s

        def pc(s, k, ncols, m0, mw=128):
            return slot(s)[:, k * ncols + m0:k * ncols + m0 + mw]

        def chk(name):
            if _CACHE.get('stop') == name:
                raise _Stop()

        try:
            S.op('pool', lambda: nc.gpsimd.memset(ident[:], 0.0), w=['ident'])
            S.op('pool', lambda: nc.gpsimd.affine_select(out=ident[:], in_=ident[:], pattern=[[-1, 128]], compare_op=ALU.not_equal, fill=1.0, base=0, channel_multiplier=1), r=['ident'], w=['ident'])
            S.op('pool', lambda: nc.gpsimd.memset(ones[:], 1.0), w=['ones'])
            S.op('pool', lambda: nc.gpsimd.memset(onesr[:], 1.0), w=['onesr'])
            S.op('pool', lambda: nc.gpsimd.memset(epsc[:], EPS), w=['epsc'])
            S.op('pool', lambda: nc.gpsimd.memset(hal[:], 0.0), w=['hal'])
            for _i in range(_CACHE.get('padcopy', 0)):
                S.op('dve', lambda: nc.vector.tensor_copy(out=sm[:, 4:5], in_=sm[:, 5:6]), w=[])
            for _i in range(_CACHE.get('pad', 0)):
                S.op('dve', lambda: nc.vector.memset(sm[:, 4:5], 0.0), w=[])
            ctok = ('pool', S.cnt['pool'])
            for e in ('pe', 'act', 'dve', 'sp'):
                S._wait(e, ctok)
            xst = R46[:, 0:2048]
            for i in range(5):
                S.dma('sp', R46[:, i * 128:(i + 1) * 128], vecs[i * 128:(i + 1) * 128, :], sem="vecs%d" % i, w=['g%d' % i])
            for i in range(5):
                tr(ps[0][:, 0:128], R46[:, i * 128:(i + 1) * 128], 128, r=['g%d' % i], w=['ps0'])
                cp('dve', vT[:, i * 128:(i + 1) * 128], ps[0][:, 0:128], r=['ps0'], w=['vT'])
            for e in range(3):
                act(sT[:, :, e], vT[:, 576 + e * 16:576 + e * 16 + 16], AF.Silu, r=['vT'], w=['sT'])
            S._wait('pe', ('act', S.cnt['act']))
            S._wait('dve', ('act', S.cnt['act']))
            S._wait('act', ('dve', S.cnt['dve']))
            S._wait('pe', ('dve', S.cnt['dve']))

            chk('vt')
            for l in range(NL):
                for cg in range(24):
                    sA = load_std(w_ada[l], 0, 8, cg * 512, 512)
                    sB = load_std(w_ada[l], 8, 8, cg * 512, 512)
                    b = gbank()
                    for j in range(4):
                        m = cg * 4 + j
                        for kc in range(16):
                            s_, k_ = (sA, kc) if kc < 8 else (sB, kc - 8)
                            mm(ps[b][:, 3 * j:3 * j + 3], pc(s_, k_, 512, j * 128), sT[:, kc, :], start=(kc == 0), stop=(kc == 15),
                               r=['w%d' % s_], w=['ps%d' % b], inc=(kc == 15 and j == 3))
                    for j in range(4):
                        m = cg * 4 + j
                        kind = m // 16
                        ts(mod[:, l, :, m], ps[b][:, 3 * j:3 * j + 3], vT[:, l * 96 + m:l * 96 + m + 1],
                           1.0 if kind in (1, 4) else 0.0, ALU.add, ALU.add, r=['ps%d' % b], w=['mod'])
            S._wait('act', ('dve', S.cnt['dve']))

            chk('mod')
            MSH_A, MSC_A, MG_A, MSH_M, MSC_M, MG_M = 0, 1, 2, 3, 4, 5

            def modv(l, e, kind, c):
                return mod[:, l, e, kind * 16 + c:kind * 16 + c + 1]

            def ssq_accum(bank, src_ap, src_keys, T, first, last):
                i = st['sq'] = (st['sq'] + 1) % 2
                act(sq[:, i, 0:T], src_ap, AF.Square, r=src_keys, w=['sq%d' % i])
                mm(ps[bank][:, 0:T], ones[:], sq[:, i, 0:T], start=first, stop=last, r=['sq%d' % i], w=['ps%d' % bank], inc=True)

            def rstd_from(bank, n, slot, T):
                f = fscr()
                act(fsc[:, f, 0:T], ps[bank][:, 0:T], AF.Ln, r=['ps%d' % bank], w=['f%d' % f], scale=1.0 / n, bias=epsc[:, 0:1])
                act(rst[:, slot, 0:T], fsc[:, f, 0:T], AF.Exp, r=['f%d' % f], w=['rst%d' % slot], scale=-0.5)

            def norm_mod(l, blk, kind_sh, kind_sc):
                T = blk['T']
                b = gbank()
                for kc in range(16):
                    ssq_accum(b, xT[:, kc, 0:T], ['x%d' % kc], T, kc == 0, kc == 15)
                rstd_from(b, D, 0, T)
                for kc in range(16):
                    f = fscr()
                    tt(fsc[:, f, 0:T], xT[:, kc, 0:T], rst[:, 0, 0:T], ALU.mult, r=['x%d' % kc, 'rst0'], w=['f%d' % f])
                    for sg in blk['segs']:
                        act(R1[:, kc, sg.c0:sg.c0 + sg.n], fsc[:, f, sg.c0:sg.c0 + sg.n], AF.Identity, r=['f%d' % f], w=['r%d' % kc],
                            scale=modv(l, sg.e, kind_sc, kc), bias=modv(l, sg.e, kind_sh, kc))

            def attention(l, blk, grp, wk):
                q0, nq = grp['q0'], grp['nq']
                nkb = grp['nkb']
                for h in range(NH):
                    g, hh = h // 4, h % 4
                    wkeys = ['w%d' % wk[g]]
                    hb = h % 2
                    kexp = R46[:, hb * 2048:hb * 2048 + 1024].bitcast(BF16)
                    vexp = R46[:, hb * 2048 + 1024:hb * 2048 + 2048].bitcast(BF16)
                    kk = ['g%d' % (hb * 8 + i) for i in range(4)]
                    vk = ['g%d' % (hb * 8 + 4 + i) for i in range(4)]
                    for kb in range(nkb):
                        b = gbank()
                        for kc in range(4):
                            ap_, keys_ = grp['ckv'](kc, kb * 512, 512)
                            mm(ps[b][:, 0:512], pc(wk[g], kc, 1024, hh * 256), ap_, start=(kc == 0), stop=(kc == 3), r=wkeys + keys_, w=['ps%d' % b])
                        cp(alt(), kexp[:, kb * 512:(kb + 1) * 512], ps[b][:, 0:512], r=['ps%d' % b], w=kk)
                        b = gbank()
                        for t4 in range(4):
                            for kc in range(4):
                                ap_, keys_ = grp['ckv'](kc, kb * 512 + t4 * 128, 128)
                                mm(ps[b][:, t4 * 128:(t4 + 1) * 128], ap_, pc(wk[g], kc, 1024, hh * 256 + 128), start=(kc == 0), stop=(kc == 3),
                                   r=wkeys + keys_, w=['ps%d' % b], inc=(kc == 3 and t4 == 3))
                        cp(alt(), vexp[:, kb * 512:(kb + 1) * 512], ps[b][:, 0:512], r=['ps%d' % b], w=vk)
                    own_n = grp['own_n']
                    if own_n:
                        b = gbank()
                        for kc in range(4):
                            mm(ps[b][:, 0:own_n], pc(wk[g], kc, 1024, hh * 256), ownckv[:, kc, q0:q0 + own_n], start=(kc == 0), stop=(kc == 3), r=wkeys + ['ock'], w=['ps%d' % b])
                        cp(alt(), kown[:, 0:own_n], ps[b][:, 0:own_n], r=['ps%d' % b], w=['kown'])
                        b = gbank()
                        for kc in range(4):
                            mm(ps[b][0:own_n, 0:128], ownckv[:, kc, q0:q0 + own_n], pc(wk[g], kc, 1024, hh * 256 + 128), start=(kc == 0), stop=(kc == 3), r=wkeys + ['ock'], w=['ps%d' % b])
                        cp(alt(), vown[0:own_n, :], ps[b][0:own_n, 0:128], r=['ps%d' % b], w=['vown'])
                    tiles = []
                    for kt in range(nkb * 4):
                        if grp['diag'] and kt >= (nkb - 1) * 4:
                            j = kt - (nkb - 1) * 4
                            tiles.append((kt, 128, q0 + j * 128, nq - j * 128, True))
                        else:
                            tiles.append((kt, 128, q0, nq, False))
                    if own_n:
                        tiles.append((-1, own_n, q0, nq, False))
                    qn_ap = lambda c0, n: R1[:, 2 * h, c0:c0 + n]
                    qp_ap = lambda c0, n: R1[:, 2 * h + 1, c0:c0 + n]
                    qkeys = ['r%d' % (2 * h), 'r%d' % (2 * h + 1)]
                    for ti, (kt, nk, c0, n, dg) in enumerate(tiles):
                        sb_ = 4 + (ti % 2)
                        if kt >= 0:
                            kl = kexp[:, kt * 128:(kt + 1) * 128]
                            krl, krkeys = grp['kr'](kt * 128, 128)
                            vl = vexp[:, kt * 128:(kt + 1) * 128]
                            rk, rv = kk, vk
                        else:
                            kl = kown[:, 0:nk]
                            krl, krkeys = ownkr[:, q0:q0 + nk], ['okr']
                            vl = vown[0:nk, :]
                            rk, rv = ['kown'], ['vown']
                        mm(ps[sb_][0:nk, 0:n], kl, qn_ap(c0, n), start=True, stop=False, r=rk + qkeys, w=['ps%d' % sb_], inc=False)
                        mm(ps[sb_][0:nk, 0:n], krl, qp_ap(c0, n), start=False, stop=True, r=krkeys + qkeys, w=['ps%d' % sb_], inc=True)
                        pi = st['pt'] = (st['pt'] + 1) % 2
                        act(pT[0:nk, pi, 0:n], ps[sb_][0:nk, 0:n], AF.Exp, r=['ps%d' % sb_], w=['pT%d' % pi], scale=ATTN_SCALE)
                        if dg:
                            S.op('dve', lambda: nc.vector.memset(pT[64:128, pi, 0:64], 0.0), r=[], w=['pT%d' % pi])
                        first = (ti == 0)
                        last = (ti == len(tiles) - 1)
                        mm(ps[6][:, c0 - q0:c0 - q0 + n], vl, pT[0:nk, pi, 0:n], start=first, stop=last, r=rv + ['pT%d' % pi], w=['ps6'], inc=False)
                        mm(ps[7][:, c0 - q0:c0 - q0 + n], ones[0:nk, :], pT[0:nk, pi, 0:n], start=first, stop=last, r=['pT%d' % pi], w=['ps7'], inc=True)
                    f = fscr()
                    S.op('dve', lambda: nc.vector.reciprocal(out=fsc[:, f, 0:nq], in_=ps[7][:, 0:nq]), r=['ps7'], w=['f%d' % f])
                    tt(R1[:, 2 * h, q0:q0 + nq], ps[6][:, 0:nq], fsc[:, f, 0:nq], ALU.mult, r=['ps6', 'f%d' % f], w=['r%d' % (2 * h)])

            def layer_block(l, blk):
                T = blk['T']
                segs = blk['segs']
                tiles = blk['tiles']
                tok0 = blk['tok0']
                samp = blk['samp']
                b_idx = blk['b']
                Wl = w_in[l]
                S.dma('sp', kvt[:], kvn[l:l + 1, :].partition_broadcast(128), sem="kvt", w=['kvt'])
                if l == 0:
                    S.dma('sp', csT[:, 0:T], tabF[:, tok0:tok0 + T], sem="csT", w=['csT'])
                norm_mod(l, blk, MSH_A, MSC_A)
                chk('norm1')
                sA = load_std(Wl, 0, 8, 0, 512)
                sB = load_std(Wl, 8, 8, 0, 512)
                for m in range(4):
                    b = gbank()
                    for kc in range(16):
                        s_, k_ = (sA, kc) if kc < 8 else (sB, kc - 8)
                        mm(ps[b][:, 0:T], pc(s_, k_, 512, m * 128), R1[:, kc, 0:T], start=(kc == 0), stop=(kc == 15), r=['w%d' % s_, 'r%d' % kc], w=['ps%d' % b])
                    cp('dve', ql[:, m, 0:T], ps[b][:, 0:T], r=['ps%d' % b], w=['ql%d' % m])
                    ssq_accum(6, ql[:, m, 0:T], ['ql%d' % m], T, m == 0, m == 3)
                chk('qlat')
                sA = load_std(Wl, 0, 8, 512, 512)
                sB = load_std(Wl, 8, 8, 512, 512)
                for (r0, rows) in tiles:
                    b = gbank()
                    for kc in range(16):
                        s_, k_ = (sA, kc) if kc < 8 else (sB, kc - 8)
                        mm(ps[b][0:rows, 0:512], R1[:, kc, r0:r0 + rows], pc(s_, k_, 512, 0, 512), start=(kc == 0), stop=(kc == 15), r=['w%d' % s_, 'r%d' % kc], w=['ps%d' % b])
                    f = fscr()
                    S.op('dve', lambda: nc.vector.memset(sm[:, 0:1], 0.0), w=['sm'])
                    act(fsc[0:rows, f, 0:512], ps[b][0:rows, 0:512], AF.Square, r=['ps%d' % b], w=['f%d' % f, 'sm'], accum=sm[0:rows, 0:1])
                    act(sm[0:rows, 1:2], sm[0:rows, 0:1], AF.Ln, r=['sm'], w=['sm'], scale=1.0 / 512, bias=epsc[0:rows, 0:1])
                    act(sm[0:rows, 2:3], sm[0:rows, 1:2], AF.Exp, r=['sm'], w=['sm'], scale=-0.5)
                    stt(cst[0:rows, :], ps[b][0:rows, 0:512], sm[0:rows, 2:3], kvt[0:rows, :], ALU.mult, ALU.mult, r=['ps%d' % b, 'sm', 'kvt'], w=['cst'])
                    if samp:
                        S.dma('sp', o_ckvs[l, r0:r0 + rows, :], cst[0:rows, :], sem="cst_o", r=['cst'])
                    else:
                        S.dma('sp', o_ckvp[l, tok0 + r0:tok0 + r0 + rows, :], cst[0:rows, :], sem="cst_o", r=['cst'])
                    b2 = gbank()
                    for c in range(4):
                        tr(ps[b2][:, c * 128:c * 128 + rows], cst[0:rows, c * 128:(c + 1) * 128], rows, r=['cst'], w=['ps%d' % b2], inc=(c == 3))
                    for c in range(4):
                        cp(alt(), ownckv[:, c, r0:r0 + rows], ps[b2][:, c * 128:c * 128 + rows], r=['ps%d' % b2], w=['ock'])
                chk('kv')
                s = new_piece()
                S.dma('pool', slot(s)[:, 0:1024].rearrange("p (k n) -> p k n", k=16), Wl[:, 1024:1088].rearrange("(k p) n -> p k n", p=128), sem="w%d" % s, w=['w%d' % s])
                for (r0, rows) in tiles:
                    b = gbank()
                    S.dma('sp', ttab[0:rows, :], tabT[tok0 + r0:tok0 + r0 + rows, :], sem="ttab", w=['ttab'])
                    for kc in range(16):
                        mm(ps[b][0:rows, 0:64], R1[:, kc, r0:r0 + rows], pc(s, kc, 64, 0, 64), start=(kc == 0), stop=(kc == 15), r=['w%d' % s, 'r%d' % kc], w=['ps%d' % b])
                    tt(krt[0:rows, 0, 0:64], ps[b][0:rows, 0:64], ttab[0:rows, 0:64], ALU.mult, r=['ps%d' % b, 'ttab'], w=['krtA'])
                    tt(krt[0:rows, 1, 0:32], ps[b][0:rows, 32:64], ttab[0:rows, 64:96], ALU.mult, r=['ps%d' % b, 'ttab'], w=['krtB'])
                    tt(krt[0:rows, 1, 32:64], ps[b][0:rows, 0:32], ttab[0:rows, 96:128], ALU.mult, r=['ps%d' % b, 'ttab', 'krtB'], w=['krtB'])
                    tt(krt[0:rows, 2, 0:64], krt[0:rows, 0, 0:64], krt[0:rows, 1, 0:64], ALU.add, r=['krtA', 'krtB'], w=['krtC'])
                    tt(krt[0:rows, 2, 64:128], krt[0:rows, 0, 0:64], krt[0:rows, 1, 0:64], ALU.add, r=['krtA', 'krtB', 'krtC'], w=['krtC'])
                    if samp:
                        S.dma('sp', o_krs[l, r0:r0 + rows, :], krt[0:rows, 2, 0:64], sem="krt_o", r=['krtC'])
                    else:
                        S.dma('sp', o_krp[l, tok0 + r0:tok0 + r0 + rows, :], krt[0:rows, 2, 0:64], sem="krt_o", r=['krtC'])
                    b2 = gbank()
                    tr(ps[b2][:, 0:rows], krt[0:rows, 2, :], rows, r=['krtC'], w=['ps%d' % b2])
                    cp(alt(), ownkr[:, r0:r0 + rows], ps[b2][:, 0:rows], r=['ps%d' % b2], w=['okr'])
                chk('kpe')
                gcs = R46[:, 0:1024].bitcast(BF16)
                CUW = 516
                cu = R46[:, 1024:1024 + 4 * CUW]
                cukeys = ['g%d' % i for i in range(4, 13)]
                for half in range(2):
                    sA = load_std(Wl, 0, 8, 2112 + half * 512, 512)
                    sB = load_std(Wl, 8, 8, 2112 + half * 512, 512)
                    for j in range(4):
                        b = gbank()
                        for kc in range(16):
                            s_, k_ = (sA, kc) if kc < 8 else (sB, kc - 8)
                            mm(ps[b][:, 0:T], pc(s_, k_, 512, j * 128), R1[:, kc, 0:T], start=(kc == 0), stop=(kc == 15), r=['w%d' % s_, 'r%d' % kc], w=['ps%d' % b])
                        cp('act', gcs[:, j * 512:j * 512 + T], ps[b][:, 0:T], r=['ps%d' % b], w=['g%d' % j])
                    sA = load_std(Wl, 0, 8, 3136 + half * 512, 512)
                    sB = load_std(Wl, 8, 8, 3136 + half * 512, 512)
                    for j in range(4):
                        c = half * 4 + j
                        b = gbank()
                        for kc in range(16):
                            s_, k_ = (sA, kc) if kc < 8 else (sB, kc - 8)
                            mm(ps[b][:, 0:T], pc(s_, k_, 512, j * 128), R1[:, kc, 0:T], start=(kc == 0), stop=(kc == 15), r=['w%d' % s_, 'r%d' % kc], w=['ps%d' % b])
                        for si, sg in enumerate(segs):
                            so = j * CUW + si * (2 + sg.n)
                            if samp:
                                cp('dve', cu[:, so:so + 2], blk['halo'][si][:, :, c], r=['halo%d' % si], w=cukeys)
                            else:
                                cp('dve', cu[:, so:so + 2], hal[:, l, c, :], r=['hal'], w=cukeys)
                            tt(cu[:, so + 2:so + 2 + sg.n], ps[b][:, sg.c0:sg.c0 + sg.n], gcs[:, j * 512 + sg.c0:j * 512 + sg.c0 + sg.n], ALU.mult,
                               r=['ps%d' % b, 'g%d' % j], w=cukeys)
                            if samp or b_idx == 3:
                                cp('dve', cs2[:, :, c] if not samp else blk['cs2'][si][:, :, c], cu[:, so + sg.n:so + sg.n + 2], r=cukeys, w=['cs2_%d' % si])
                            if not samp:
                                cp('dve', hal[:, l, c, :], cu[:, so + sg.n:so + sg.n + 2], r=cukeys, w=['hal'])
                    sA = load_std(Wl, 0, 8, 1088 + half * 512, 512)
                    sB = load_std(Wl, 8, 8, 1088 + half * 512, 512)
                    for j in range(4):
                        c = half * 4 + j
                        b = gbank()
                        for kc in range(16):
                            s_, k_ = (sA, kc) if kc < 8 else (sB, kc - 8)
                            mm(ps[b][:, 0:T], pc(s_, k_, 512, j * 128), R1[:, kc, 0:T], start=(kc == 0), stop=(kc == 15), r=['w%d' % s_, 'r%d' % kc], w=['ps%d' % b])
                        f = fscr()
                        for si, sg in enumerate(segs):
                            so = j * CUW + si * (2 + sg.n)
                            wv = lambda k: vT[:, 464 + l * 24 + k * 8 + c:464 + l * 24 + k * 8 + c + 1]
                            y = fsc[:, f, sg.c0:sg.c0 + sg.n]
                            ts(y, cu[:, so:so + sg.n], wv(0), None, ALU.mult, None, r=cukeys, w=['f%d' % f])
                            stt(y, cu[:, so + 1:so + 1 + sg.n], wv(1), y, ALU.mult, ALU.add, r=cukeys + ['f%d' % f], w=['f%d' % f])
                            stt(y, cu[:, so + 2:so + 2 + sg.n], wv(2), y, ALU.mult, ALU.add, r=cukeys + ['f%d' % f], w=['f%d' % f])
                            tt(convn[:, c, sg.c0:sg.c0 + sg.n], ps[b][:, sg.c0:sg.c0 + sg.n], y, ALU.mult, r=['ps%d' % b, 'f%d' % f], w=['cn%d' % c])
                        ssq_accum(7, convn[:, c, 0:T], ['cn%d' % c], T, c == 0, c == 7)
                if samp or b_idx == 3:
                    for si, sg in enumerate(segs):
                        src = blk['cs2'][si] if samp else cs2
                        for t in range(2):
                            dst = (o_convs[l, si, t] if samp else o_convp[l, t]).rearrange("(c p) -> p c", p=128)
                            S.dma('sp', dst, src[:, t, :], sem="cs2_o%d" % si, r=['cs2_%d' % si], nonc=True)
                rstd_from(7, 1024, 1, T)
                for c in range(8):
                    stt(convn[:, c, 0:T], convn[:, c, 0:T], vT[:, 432 + l * 8 + c:432 + l * 8 + c + 1], rst[:, 1, 0:T], ALU.mult, ALU.mult,
                        r=['cn%d' % c, 'rst1'], w=['cn%d' % c])
                chk('conv')
                rstd_from(6, 512, 0, T)
                for m in range(4):
                    stt(ql[:, m, 0:T], ql[:, m, 0:T], vT[:, 384 + l * 4 + m:384 + l * 4 + m + 1], rst[:, 0, 0:T], ALU.mult, ALU.mult,
                        r=['ql%d' % m, 'rst0'], w=['ql%d' % m])
                for g in range(2):
                    s = new_piece()
                    dst = slot(s).rearrange("p (k h c) -> p k h c", k=4, h=4)
                    srcv = w_uq[l].rearrange("(k p) (h c) -> p k h c", p=128, c=192)
                    for kc in range(4):
                        S.dma('pool', dst[:, kc, :, 0:192], srcv[:, kc, 4 * g:4 * g + 4, :], sem="w%d" % s, w=['w%d' % s])
                        S.dma('pool', dst[:, kc, :, 192:224], srcv[:, kc, 4 * g:4 * g + 4, 160:192], sem="w%d" % s, w=['w%d' % s])
                        S.dma('pool', dst[:, kc, :, 224:256], srcv[:, kc, 4 * g:4 * g + 4, 128:160], sem="w%d" % s, w=['w%d' % s])
                    for hh in range(4):
                        h = 4 * g + hh
                        for part in range(2):
                            b = gbank()
                            for kc in range(4):
                                mm(ps[b][:, 0:T], slot(s)[:, kc * 1024 + hh * 256 + part * 128:kc * 1024 + hh * 256 + part * 128 + 128], ql[:, kc, 0:T],
                                   start=(kc == 0), stop=(kc == 3), r=['w%d' % s, 'ql%d' % kc], w=['ps%d' % b])
                            if part == 0:
                                cp('act', R1[:, 2 * h, 0:T], ps[b][:, 0:T], r=['ps%d' % b], w=['r%d' % (2 * h)])
                            else:
                                tt(R1[:, 2 * h + 1, 0:T], ps[b][:, 0:T], csT[:, 0:T], ALU.mult, r=['ps%d' % b, 'csT'], w=['r%d' % (2 * h + 1)])
                chk('q')
                if not samp:
                    wk = [load_std(w_ukv[l], 0, 4, g * 1024, 1024) for g in range(2)]

                    def ckv_acc(kc, t0, n):
                        if t0 >= b_idx * 512:
                            return ownckv[:, kc, t0 - b_idx * 512:t0 - b_idx * 512 + n], ['ock']
                        o = (l * 4 + kc) * 1536 + t0
                        return cache[:, o:o + n], ['cache']

                    def kr_acc(t0, n):
                        if t0 >= b_idx * 512:
                            return ownkr[:, t0 - b_idx * 512:t0 - b_idx * 512 + n], ['okr']
                        return krc[:, l * 1536 + t0:l * 1536 + t0 + n], ['krc']
                    attention(l, blk, dict(q0=0, nq=512, nkb=b_idx + 1, ckv=ckv_acc, kr=kr_acc, own_n=0, diag=True), wk)
                    if b_idx < 3:
                        for kc in range(4):
                            o = (l * 4 + kc) * 1536 + b_idx * 512
                            cp(alt(), cache[:, o:o + 512], ownckv[:, kc, :], r=['ock'], w=['cache'])
                        cp(alt(), krc[:, l * 1536 + b_idx * 512:l * 1536 + b_idx * 512 + 512], ownkr[:, :], r=['okr'], w=['krc'])
                else:
                    for si, sg in enumerate(segs):
                        for q4 in range(4):
                            stg = R46[:, (q4 % 2) * 2048:(q4 % 2) * 2048 + 2048]
                            sk = ['g%d' % ((q4 % 2) * 8 + i) for i in range(8)]
                            S.dma('sp', stg.rearrange("p (t n) -> p t n", t=4), pkv[l, si, q4 * 512:(q4 + 1) * 512, :].rearrange("(t p) n -> p t n", p=128),
                                  sem="pst%d" % (q4 % 2), w=sk)
                            for t4 in range(4):
                                b = gbank()
                                for c in range(4):
                                    tr(ps[b][:, c * 128:(c + 1) * 128], stg[:, t4 * 512 + c * 128:t4 * 512 + (c + 1) * 128], 128, r=sk, w=['ps%d' % b], inc=(c == 3))
                                for c in range(4):
                                    o = c * 2048 + q4 * 512 + t4 * 128
                                    cp(alt(), cache[:, o:o + 128], ps[b][:, c * 128:(c + 1) * 128], r=['ps%d' % b], w=['cache'])
                        for q4 in range(4):
                            stg = R46[:, (q4 % 2) * 2048:(q4 % 2) * 2048 + 512]
                            sk = ['g%d' % ((q4 % 2) * 8 + i) for i in range(2)]
                            for dup in range(2):
                                S.dma('sp', stg.rearrange("p (t n) -> p t n", t=4)[:, :, dup * 64:(dup + 1) * 64],
                                      pkr[l, si, q4 * 512:(q4 + 1) * 512, :].rearrange("(t p) n -> p t n", p=128), sem="pst%d" % (q4 % 2), w=sk)
                            b = gbank()
                            for t4 in range(4):
                                tr(ps[b][:, t4 * 128:(t4 + 1) * 128], stg[:, t4 * 128:(t4 + 1) * 128], 128, r=sk, w=['ps%d' % b], inc=(t4 == 3))
                            cp(alt(), krc[:, q4 * 512:(q4 + 1) * 512], ps[b][:, 0:512], r=['ps%d' % b], w=['krc'])
                        wk = [load_std(w_ukv[l], 0, 4, g * 1024, 1024) for g in range(2)]

                        def ckv_acc(kc, t0, n):
                            return cache[:, kc * 2048 + t0:kc * 2048 + t0 + n], ['cache']

                        def kr_acc(t0, n):
                            return krc[:, t0:t0 + n], ['krc']
                        attention(l, blk, dict(q0=sg.c0, nq=sg.n, nkb=4, ckv=ckv_acc, kr=kr_acc, own_n=sg.n, diag=False), wk)
                chk('attn')
                b = gbank()
                for h in range(NH):
                    ssq_accum(b, R1[:, 2 * h, 0:T], ['r%d' % (2 * h)], T, h == 0, h == NH - 1)
                rstd_from(b, 1024, 0, T)
                for h in range(NH):
                    stt(R1[:, 2 * h, 0:T], R1[:, 2 * h, 0:T], vT[:, 400 + l * 8 + h:400 + l * 8 + h + 1], rst[:, 0, 0:T], ALU.mult, ALU.mult,
                        r=['r%d' % (2 * h), 'rst0'], w=['r%d' % (2 * h)])
                for cg in range(4):
                    sA = load_std(w_out[l], 0, 8, cg * 512, 512)
                    sB = load_std(w_out[l], 8, 8, cg * 512, 512)
                    for j in range(4):
                        m = cg * 4 + j
                        b = gbank()
                        for kc in range(16):
                            if kc < 8:
                                mm(ps[b][:, 0:T], pc(sA, kc, 512, j * 128), R1[:, 2 * kc, 0:T], start=(kc == 0), stop=False, r=['w%d' % sA, 'r%d' % (2 * kc)], w=['ps%d' % b])
                            else:
                                mm(ps[b][:, 0:T], pc(sB, kc - 8, 512, j * 128), convn[:, kc - 8, 0:T], start=False, stop=(kc == 15), r=['w%d' % sB, 'cn%d' % (kc - 8)], w=['ps%d' % b])
                        for sg in segs:
                            stt(xT[:, m, sg.c0:sg.c0 + sg.n], ps[b][:, sg.c0:sg.c0 + sg.n], modv(l, sg.e, MG_A, m), xT[:, m, sg.c0:sg.c0 + sg.n], ALU.mult, ALU.add,
                                r=['ps%d' % b, 'x%d' % m], w=['x%d' % m])
                chk('wout')
                norm_mod(l, blk, MSH_M, MSC_M)
                for g in range(8):
                    ub = (g % 2) * 2048
                    up = R46[:, ub:ub + 2048].bitcast(BF16)
                    uk = ['g%d' % ((g % 2) * 8 + i) for i in range(8)]
                    for half in range(2):
                        sA = load_std(w_up[l], 0, 8, g * 1024 + half * 512, 512)
                        sB = load_std(w_up[l], 8, 8, g * 1024 + half * 512, 512)
                        for j in range(4):
                            b = gbank()
                            for kc in range(16):
                                s_, k_ = (sA, kc) if kc < 8 else (sB, kc - 8)
                                mm(ps[b][:, 0:T], pc(s_, k_, 512, j * 128), R1[:, kc, 0:T], start=(kc == 0), stop=(kc == 15), r=['w%d' % s_, 'r%d' % kc], w=['ps%d' % b])
                            act(rtmp[:, 0:T], ps[b][:, 0:T], AF.Relu, r=['ps%d' % b], w=['rtmp'])
                            jj = half * 4 + j
                            tt(up[:, jj * 512:jj * 512 + T], rtmp[:, 0:T], rtmp[:, 0:T], ALU.mult, r=['rtmp'], w=uk)
                    for cg in range(4):
                        s = load_std(w_down[l], g * 8, 8, cg * 512, 512)
                        for j in range(4):
                            m = cg * 4 + j
                            b = gbank()
                            for kc in range(8):
                                mm(ps[b][:, 0:T], pc(s, kc, 512, j * 128), up[:, kc * 512:kc * 512 + T], start=(kc == 0), stop=(kc == 7), r=['w%d' % s] + uk, w=['ps%d' % b])
                            for sg in segs:
                                stt(xT[:, m, sg.c0:sg.c0 + sg.n], ps[b][:, sg.c0:sg.c0 + sg.n], modv(l, sg.e, MG_M, m), xT[:, m, sg.c0:sg.c0 + sg.n], ALU.mult, ALU.add,
                                    r=['ps%d' % b, 'x%d' % m], w=['x%d' % m])

            def run_block(blk, xsrc, ydst):
                T = blk['T']
                tiles = blk['tiles']
                gk = ['g%d' % i for i in range(16)]
                for ti, (r0, rows) in enumerate(tiles):
                    stg = R46[:, (ti % 2) * 2048:(ti % 2) * 2048 + 2048]
                    sk = ['g%d' % ((ti % 2) * 8 + i) for i in range(8)]
                    S.dma('sp', stg[0:rows, :], xsrc[r0:r0 + rows, :], sem="xst%d" % (ti % 2), w=sk)
                    chk('xl1')
                    for c4 in range(4):
                        b = gbank()
                        if _CACHE.get('xv', 0) == 3:
                            b = 1
                        for c in range(4):
                            cc = c4 * 4 + c
                            tr(ps[b][:, c * 128:c * 128 + rows], stg[0:rows, cc * 128:(cc + 1) * 128], rows, r=sk, w=['ps%d' % b], inc=(c == 3))
                        chk('xl2')
                        chk('xt%d_%d' % (ti, c4))
                        xv = _CACHE.get('xv', 7)
                        for c in range(4):
                            cc = c4 * 4 + c
                            if xv == 1 and c4 == 1 and c > 0:
                                continue
                            if xv == 2:
                                cc = c
                            if xv == 5 and c4 == 1 and c != 1:
                                continue
                            en = alt()
                            if (xv == 4 and c4 == 1) or xv == 7:
                                en = 'act'
                            if xv == 5 and c4 == 1:
                                en = 'dve'
                            if xv == 6:
                                cp(en, fsc[:, 0, c * 128:c * 128 + rows], ps[b][:, c * 128:c * 128 + rows], r=['ps%d' % b], w=['x%d' % cc])
                            else:
                                cp(en, xT[:, cc, r0:r0 + rows], ps[b][:, c * 128:c * 128 + rows], r=['ps%d' % b], w=['x%d' % cc])
                        if _CACHE.get('serialtr'):
                            S._wait('pe', ('act', S.cnt['act']))
                            S._wait('pe', ('dve', S.cnt['dve']))
                        chk('xl3')
                        chk('xg%d_%d' % (ti, c4))
                chk('xload')
                for l in range(NL):
                    layer_block(l, blk)
                chk('layers')
                b = gbank()
                for kc in range(16):
                    ssq_accum(b, xT[:, kc, 0:T], ['x%d' % kc], T, kc == 0, kc == 15)
                rstd_from(b, D, 0, T)
                for kc in range(16):
                    stt(xT[:, kc, 0:T], xT[:, kc, 0:T], vT[:, 560 + kc:561 + kc], rst[:, 0, 0:T], ALU.mult, ALU.mult, r=['x%d' % kc, 'rst0'], w=['x%d' % kc])
                for ti, (r0, rows) in enumerate(tiles):
                    stg = R46[:, (ti % 2) * 2048:(ti % 2) * 2048 + 2048]
                    sk = ['g%d' % ((ti % 2) * 8 + i) for i in range(8)]
                    for c4 in range(4):
                        b = gbank()
                        for c in range(4):
                            cc = c4 * 4 + c
                            tr(ps[b][0:rows, c * 128:(c + 1) * 128], xT[:, cc, r0:r0 + rows], 128, r=['x%d' % cc], w=['ps%d' % b], inc=(c == 3))
                        cp(alt(), stg[0:rows, c4 * 512:(c4 + 1) * 512], ps[b][0:rows, 0:512], r=['ps%d' % b], w=sk)
                    S.dma('sp', ydst[r0:r0 + rows, :], stg[0:rows, :], sem="yst%d" % (ti % 2), r=sk)

            for bi in range(NBLK):
                blk = dict(T=512, segs=[Seg(0, 512, 0)], tiles=[(i * 128, 128) for i in range(4)], tok0=bi * 512, samp=False, b=bi)
                run_block(blk, xp[bi * 512:(bi + 1) * 512, :], o_yp[bi * 512:(bi + 1) * 512, :])
            halo_t = [sbt("halo%d" % i, [128, 2, 8], F32) for i in range(2)]
            cs2_t = [sbt("cs2s%d" % i, [128, 2, 8], F32) for i in range(2)]
            sblk = dict(T=32, segs=[Seg(0, 16, 1), Seg(16, 16, 2)], tiles=[(0, 32)], tok0=2048, samp=True, b=4, halo=halo_t, cs2=cs2_t)
            orig_layer_block = layer_block

            def layer_block_s(l, blk):
                for si in range(2):
                    for t in range(2):
                        S.dma('sp', halo_t[si][:, t, :], sconv[l, si, t].rearrange("(c p) -> p c", p=128), sem="halo%d" % si, w=['halo%d' % si], nonc=True)
                orig_layer_block(l, blk)
            layer_block = layer_block_s
            if SAMPLE:
                if _CACHE.get('nslots_s', 8) > 4:
                    for en_ in ('pe', 'act', 'dve'):
                        if S.cnt[en_] > 0:
                            S._wait('pool', (en_, S.cnt[en_]))
                    st['nslots'] = _CACHE.get('nslots_s', 8)
                run_block(sblk, xs, o_ys)
        except _Stop:
            pass
        S.finish('sp')
        _CACHE['counts'] = dict(S.cnt)
        _CACHE['nwaits'] = S.nwaits
    return nc


_CACHE = {}


def _tables():
    half = 32
    inv = (10000.0 ** (-np.arange(half, dtype=np.float32) / half)).astype(np.float32)
    pos = np.concatenate([np.arange(2048), 2048 + np.arange(16), 2048 + np.arange(16)]).astype(np.float32)
    ang = pos[:, None] * inv[None, :]
    cos, sin = np.cos(ang).astype(np.float32), np.sin(ang).astype(np.float32)
    tabT = np.concatenate([cos, cos, -sin, sin], axis=1).astype(np.float32)
    tabF = np.ascontiguousarray(tabT.T)
    return tabF, np.ascontiguousarray(tabT)


def kernel(x_prompt, x_sample, c_prompt, c_sample, cache_kv_latent, cache_k_rope, state_conv,
           w_ada, b_ada, w_in, q_norm, w_uq, kv_norm, w_ukv, conv_w,
           attn_out_norm, conv_out_norm, w_out, w_up, w_down, final_norm):
    f = lambda a: np.ascontiguousarray(np.asarray(a, dtype=np.float32))
    x_prompt, x_sample, c_prompt, c_sample = f(x_prompt), f(x_sample), f(c_prompt), f(c_sample)
    cache_kv_latent, cache_k_rope, state_conv = f(cache_kv_latent), f(cache_k_rope), f(state_conv)
    w_ada, b_ada, w_in, q_norm, w_uq, kv_norm, w_ukv = f(w_ada), f(b_ada), f(w_in), f(q_norm), f(w_uq), f(kv_norm), f(w_ukv)
    conv_w, attn_out_norm, conv_out_norm, w_out, w_up, w_down, final_norm = f(conv_w), f(attn_out_norm), f(conv_out_norm), f(w_out), f(w_up), f(w_down), f(final_norm)
    if 'nc' not in _CACHE:
        _CACHE['nc'] = build_program()
    nc = _CACHE['nc']
    tabF, tabT = _tables()
    common = np.concatenate([b_ada.reshape(384, 128), q_norm.reshape(16, 128), attn_out_norm.reshape(32, 128),
                             conv_out_norm.reshape(32, 128), conv_w.reshape(96, 128), final_norm.reshape(16, 128)], axis=0)
    in_maps = []
    for c in range(8):
        vecs = np.zeros((640, 128), np.float32)
        vecs[0:576] = common
        vecs[576:592] = c_prompt[c].reshape(16, 128)
        vecs[592:608] = c_sample[2 * c].reshape(16, 128)
        vecs[608:624] = c_sample[2 * c + 1].reshape(16, 128)
        in_maps.append(dict(
            xp=x_prompt[c], xs=np.ascontiguousarray(x_sample[2 * c:2 * c + 2].reshape(32, D)), vecs=vecs, kvn=kv_norm,
            pkv=np.ascontiguousarray(cache_kv_latent[:, 2 * c:2 * c + 2]), pkr=np.ascontiguousarray(cache_k_rope[:, 2 * c:2 * c + 2]),
            sconv=np.ascontiguousarray(state_conv[:, 2 * c:2 * c + 2]), tabF=tabF, tabT=tabT,
            w_ada=w_ada, w_in=w_in, w_uq=w_uq, w_ukv=w_ukv, w_out=w_out, w_up=w_up, w_down=w_down))
    res = run_bass_kernel_spmd(nc, in_maps, core_ids=list(range(8)))
    R = res.results
    y_p = np.stack([R[c]["o_yp"] for c in range(8)], axis=0)
    y_s = np.concatenate([R[c]["o_ys"].reshape(2, 16, D) for c in range(8)], axis=0)
    ckv_p = np.stack([R[c]["o_ckvp"] for c in range(8)], axis=1)
    kr_p = np.stack([R[c]["o_krp"] for c in range(8)], axis=1)
    conv_p = np.stack([R[c]["o_convp"] for c in range(8)], axis=1)
    ckv_s = np.concatenate([R[c]["o_ckvs"].reshape(L, 2, 16, 512) for c in range(8)], axis=1)
    kr_s = np.concatenate([R[c]["o_krs"].reshape(L, 2, 16, 64) for c in range(8)], axis=1)
    conv_s = np.concatenate([R[c]["o_convs"] for c in range(8)], axis=1)
    return (y_p.astype(np.float32), y_s.astype(np.float32), ckv_p.astype(np.float32), kr_p.astype(np.float32),
            conv_p.astype(np.float32), ckv_s.astype(np.float32), kr_s.astype(np.float32), conv_s.astype(np.float32))
```

```python
import contextlib
import numpy as np
import concourse.bass as bass
import concourse.mybir as mybir
from concourse.bass_utils import run_bass_kernel_spmd

F32, BF16 = mybir.dt.float32, mybir.dt.bfloat16
AF = mybir.ActivationFunctionType
ALU = mybir.AluOpType

L = 4
D = 2048
SEQ = 2048
NH = 8
TB = 512
EPS = 1e-6
ATTN_SCALE = 192.0 ** -0.5
INC = 4160
DFF = 8192
NTAB = 2048 + 32


class Sched:
    def __init__(self, nc, es):
        self.nc = nc
        self.es = es
        self.eng = {'pe': nc.tensor, 'act': nc.scalar, 'dve': nc.vector, 'pool': nc.gpsimd, 'sp': nc.sync}
        self.h = {}
        self.cnt = {}
        for e in self.eng:
            self.h[e] = es.enter_context(nc.semaphore("s_" + e))
            self.cnt[e] = 0
        self.seen = {e: {} for e in self.eng}
        self.snap = {}
        self.bufs = {}
        self.nwaits = 0

    def _wait(self, e, tok):
        name, val = tok
        if name == 'pe' and e == 'pe':
            return
        if self.seen[e].get(name, 0) >= val:
            return
        self.eng[e].wait_ge(self.h[name], val)
        self.nwaits += 1
        self.seen[e][name] = val
        sn = self.snap.get(tok)
        if sn:
            se = self.seen[e]
            for k, v in sn.items():
                if se.get(k, 0) < v:
                    se[k] = v

    def _deps(self, e, r, w):
        deps = set()
        for k in r:
            b = self.bufs.get(k)
            if b and b[0]:
                deps.add(b[0])
        for k in w:
            b = self.bufs.get(k)
            if b:
                if b[0]:
                    deps.add(b[0])
                for n, v in b[1].items():
                    deps.add((n, v))
        for t in sorted(deps):
            self._wait(e, t)

    def _record(self, tok, r, w):
        for k in r:
            b = self.bufs.setdefault(k, [None, {}])
            if b[1].get(tok[0], 0) < tok[1]:
                b[1][tok[0]] = tok[1]
        for k in w:
            self.bufs[k] = [tok, {}]

    def op(self, e, fn, r=(), w=(), inc=True):
        self._deps(e, r, w)
        if e in ('dve', 'act') and self.cnt[e] > 0 and _CACHE.get('selfser', 1):
            self._wait(e, (e, self.cnt[e]))
        ins = fn()
        if inc:
            self.cnt[e] += 1
            ins.then_inc(self.h[e], 1)
            tok = (e, self.cnt[e])
            self.snap[tok] = dict(self.seen[e])
        else:
            tok = (e, self.cnt[e] + 1)
        self._record(tok, r, w)
        return ins

    def dma(self, q, out, in_, sem, r=(), w=(), nonc=False):
        for k in w:
            b = self.bufs.get(k)
            if b and b[0] and b[0][0] == sem and not b[1]:
                b[0] = None
        self._deps(q, r, w)
        if sem not in self.h:
            self.h[sem] = self.es.enter_context(self.nc.semaphore("d_" + sem))
            self.cnt[sem] = 0
        if nonc:
            with self.nc.allow_non_contiguous_dma(reason="small strided vector"):
                ins = self.eng[q].dma_start(out=out, in_=in_)
        else:
            ins = self.eng[q].dma_start(out=out, in_=in_)
        self.cnt[sem] += 16
        ins.then_inc(self.h[sem], 16)
        tok = (sem, self.cnt[sem])
        self.snap[tok] = dict(self.seen[q])
        self._record(tok, r, w)

    def finish(self, e='sp'):
        for name, h in self.h.items():
            if name in self.eng:
                continue
            self._wait(e, (name, self.cnt[name]))
        for name in ('act', 'dve', 'pe'):
            if self.cnt[name] > 0:
                self._wait(e, (name, self.cnt[name]))


class _Stop(Exception):
    pass


class Seg:
    def __init__(self, c0, n, e):
        self.c0, self.n, self.e = c0, n, e


def build_program(NL=L, NBLK=4, SAMPLE=True):
    nc = bass.Bass("TRN2", target_bir_lowering=False)
    dt_in = lambda n, s: nc.dram_tensor(n, s, F32, kind="ExternalInput").ap()
    dt_out = lambda n, s: nc.dram_tensor(n, s, F32, kind="ExternalOutput").ap()
    xp = dt_in("xp", [SEQ, D])
    xs = dt_in("xs", [32, D])
    vecs = dt_in("vecs", [640, 128])
    kvn = dt_in("kvn", [NL, 512])
    pkv = dt_in("pkv", [NL, 2, 2048, 512])
    pkr = dt_in("pkr", [NL, 2, 2048, 64])
    sconv = dt_in("sconv", [NL, 2, 2, 1024])
    tabF = dt_in("tabF", [128, NTAB])
    tabT = dt_in("tabT", [NTAB, 128])
    w_ada = dt_in("w_ada", [NL, D, 6 * D])
    w_in = dt_in("w_in", [NL, D, INC])
    w_uq = dt_in("w_uq", [NL, 512, 1536])
    w_ukv = dt_in("w_ukv", [NL, 512, 2048])
    w_out = dt_in("w_out", [NL, D, D])
    w_up = dt_in("w_up", [NL, D, DFF])
    w_down = dt_in("w_down", [NL, DFF, D])
    o_yp = dt_out("o_yp", [SEQ, D])
    o_ys = dt_out("o_ys", [32, D])
    o_ckvp = dt_out("o_ckvp", [L, SEQ, 512])
    o_krp = dt_out("o_krp", [L, SEQ, 64])
    o_convp = dt_out("o_convp", [L, 2, 1024])
    o_ckvs = dt_out("o_ckvs", [L, 32, 512])
    o_krs = dt_out("o_krs", [L, 32, 64])
    o_convs = dt_out("o_convs", [L, 2, 2, 1024])

    with contextlib.ExitStack() as es:
        sbt = lambda n, s, d: es.enter_context(nc.sbuf_tensor(n, s, d))
        S = Sched(nc, es)
        cache = sbt("cache", [128, L * 4 * 1536], BF16)
        ownckv = sbt("ownckv", [128, 4, 512], BF16)
        krc = sbt("krc", [128, L * 1536], BF16)
        ownkr = sbt("ownkr", [128, 512], BF16)
        xT = sbt("xT", [128, 16, 512], F32)
        R1 = sbt("R1", [128, 16, 512], BF16)
        convn = sbt("convn", [128, 8, 512], BF16)
        R46 = sbt("R46", [128, 4096], F32)
        ring = sbt("ring", [128, 4, 4096], BF16)
        sq = sbt("sq", [128, 2, 512], BF16)
        rst = sbt("rst", [128, 2, 512], F32)
        fsc = sbt("fsc", [128, 3, 512], F32)
        ql = sbt("ql", [128, 4, 512], BF16)
        pT = sbt("pT", [128, 2, 512], BF16)
        cst = sbt("cst", [128, 512], F32)
        krt = sbt("krt", [128, 3, 128], F32)
        ttab = sbt("ttab", [128, 128], F32)
        csT = sbt("csT", [128, 512], F32)
        kvt = sbt("kvt", [128, 512], F32)
        ident = sbt("ident", [128, 128], F32)
        ones = sbt("ones", [128, 128], BF16)
        onesr = sbt("onesr", [1, 4], BF16)
        epsc = sbt("epsc", [128, 1], F32)
        vT = sbt("vT", [128, 640], F32)
        sT = sbt("sT", [128, 16, 3], BF16)
        mod = sbt("mod", [128, L, 3, 96], F32)
        hal = sbt("hal", [128, L, 8, 2], F32)
        cs2 = sbt("cs2", [128, 2, 8], F32)
        sm = sbt("sm", [128, 8], F32)
        kown = sbt("kown", [128, 32], BF16)
        vown = sbt("vown", [16, 128], BF16)
        rtmp = sbt("rtmp", [128, 512], BF16)
        tmS = sbt("tmS", [32, 512], F32)
        ps = [es.enter_context(nc.psum_tensor("ps%d" % i, [128, 512], F32)) for i in range(8)]

        st = {'gb': 0, 'piece': 0, 'alt': 0, 'sq': 0, 'f': 0, 'pt': 0}

        def gbank():
            st['gb'] = (st['gb'] + 1) % 4
            return st['gb']

        def alt():
            st['alt'] ^= 1
            return 'act' if st['alt'] else 'dve'

        def mm(out, lhsT, rhs, start, stop, r, w, inc=None):
            if inc is None:
                inc = stop
            S.op('pe', lambda: nc.tensor.matmul(out, lhsT=lhsT, rhs=rhs, start=start, stop=stop), r=r, w=w, inc=inc)

        def tr(out, in_, rows, r, w, inc=True):
            if _CACHE.get('mmtr', 1):
                S.op('pe', lambda: nc.tensor.matmul(out, lhsT=in_, rhs=ident[0:rows, 0:rows], start=True, stop=True), r=r, w=w, inc=inc)
            else:
                S.op('pe', lambda: nc.tensor.transpose(out=out, in_=in_, identity=ident[0:rows, 0:rows]), r=r, w=w, inc=inc)

        def act(out, in_, func, r, w, bias=None, scale=None, accum=None):
            kw = {}
            if bias is not None:
                kw['bias'] = bias
            if scale is not None:
                kw['scale'] = scale
            if accum is not None:
                kw['accum_out'] = accum
            S.op('act', lambda: nc.scalar.activation(out=out, in_=in_, func=func, **kw), r=r, w=w)

        def cp(e, out, in_, r, w):
            if e == 'act':
                S.op('act', lambda: nc.scalar.activation(out=out, in_=in_, func=AF.Copy), r=r, w=w)
            else:
                S.op('dve', lambda: nc.vector.tensor_scalar(out=out, in0=in_, scalar1=1.0, scalar2=None, op0=ALU.mult), r=r, w=w)

        def tt(out, a, b, op, r, w):
            S.op('dve', lambda: nc.vector.tensor_tensor(out=out, in0=a, in1=b, op=op), r=r, w=w)

        def stt(out, in0, scalar, in1, op0, op1, r, w):
            S.op('dve', lambda: nc.vector.scalar_tensor_tensor(out=out, in0=in0, scalar=scalar, in1=in1, op0=op0, op1=op1), r=r, w=w)

        def ts(out, in0, s1, s2, op0, op1, r, w):
            if s2 is None:
                S.op('dve', lambda: nc.vector.tensor_scalar(out=out, in0=in0, scalar1=s1, scalar2=None, op0=op0), r=r, w=w)
            else:
                S.op('dve', lambda: nc.vector.tensor_scalar(out=out, in0=in0, scalar1=s1, scalar2=s2, op0=op0, op1=op1), r=r, w=w)

        def fscr():
            st['f'] = (st['f'] + 1) % 3
            return st['f']

        def new_piece():
            s = st['piece'] % st.get('nslots', 4)
            st['piece'] += 1
            return s

        def slot(s):
            if s < 4:
                return ring[:, s, :]
            return cache[:, 8192 + (s - 4) * 4096:8192 + (s - 3) * 4096]

        def load_std(W, kc0, nk, c0, ncols):
            s = new_piece()
            dst = slot(s)[:, 0:nk * ncols].rearrange("p (k n) -> p k n", k=nk)
            src = W[kc0 * 128:(kc0 + nk) * 128, c0:c0 + ncols].rearrange("(k p) n -> p k n", p=128)
            S.dma('pool', dst, src, sem="w%d" % s, w=["w%d" % s])
            return s

        def pc(s, k, ncols, m0, mw=128):
            return slot(s)[:, k * ncols + m0:k * ncols + m0 + mw]

        def chk(name):
            if _CACHE.get('stop') == name:
                raise _Stop()

        try:
            S.op('pool', lambda: nc.gpsimd.memset(ident[:], 0.0), w=['ident'])
            S.op('pool', lambda: nc.gpsimd.affine_select(out=ident[:], in_=ident[:], pattern=[[-1, 128]], compare_op=ALU.not_equal, fill=1.0, base=0, channel_multiplier=1), r=['ident'], w=['ident'])
            S.op('pool', lambda: nc.gpsimd.memset(ones[:], 1.0), w=['ones'])
            S.op('pool', lambda: nc.gpsimd.memset(onesr[:], 1.0), w=['onesr'])
            S.op('pool', lambda: nc.gpsimd.memset(epsc[:], EPS), w=['epsc'])
            S.op('pool', lambda: nc.gpsimd.memset(hal[:], 0.0), w=['hal'])
            for _i in range(_CACHE.get('padcopy', 0)):
                S.op('dve', lambda: nc.vector.tensor_copy(out=sm[:, 4:5], in_=sm[:, 5:6]), w=[])
            for _i in range(_CACHE.get('pad', 0)):
                S.op('dve', lambda: nc.vector.memset(sm[:, 4:5], 0.0), w=[])
            ctok = ('pool', S.cnt['pool'])
            for e in ('pe', 'act', 'dve', 'sp'):
                S._wait(e, ctok)
            xst = R46[:, 0:2048]
            for i in range(5):
                S.dma('sp', R46[:, i * 128:(i + 1) * 128], vecs[i * 128:(i + 1) * 128, :], sem="vecs%d" % i, w=['g%d' % i])
            for i in range(5):
                tr(ps[0][:, 0:128], R46[:, i * 128:(i + 1) * 128], 128, r=['g%d' % i], w=['ps0'])
                cp('dve', vT[:, i * 128:(i + 1) * 128], ps[0][:, 0:128], r=['ps0'], w=['vT'])
            for e in range(3):
                act(sT[:, :, e], vT[:, 576 + e * 16:576 + e * 16 + 16], AF.Silu, r=['vT'], w=['sT'])
            S._wait('pe', ('act', S.cnt['act']))
            S._wait('dve', ('act', S.cnt['act']))
            S._wait('act', ('dve', S.cnt['dve']))
            S._wait('pe', ('dve', S.cnt['dve']))

            chk('vt')
            for l in range(NL):
                for cg in range(24):
                    sA = load_std(w_ada[l], 0, 8, cg * 512, 512)
                    sB = load_std(w_ada[l], 8, 8, cg * 512, 512)
                    b = gbank()
                    for j in range(4):
                        m = cg * 4 + j
                        for kc in range(16):
                            s_, k_ = (sA, kc) if kc < 8 else (sB, kc - 8)
                            mm(ps[b][:, 3 * j:3 * j + 3], pc(s_, k_, 512, j * 128), sT[:, kc, :], start=(kc == 0), stop=(kc == 15),
                               r=['w%d' % s_], w=['ps%d' % b], inc=(kc == 15 and j == 3))
                    for j in range(4):
                        m = cg * 4 + j
                        kind = m // 16
                        ts(mod[:, l, :, m], ps[b][:, 3 * j:3 * j + 3], vT[:, l * 96 + m:l * 96 + m + 1],
                           1.0 if kind in (1, 4) else 0.0, ALU.add, ALU.add, r=['ps%d' % b], w=['mod'])
            S._wait('act', ('dve', S.cnt['dve']))

            chk('mod')
            MSH_A, MSC_A, MG_A, MSH_M, MSC_M, MG_M = 0, 1, 2, 3, 4, 5

            def modv(l, e, kind, c):
                return mod[:, l, e, kind * 16 + c:kind * 16 + c + 1]

            def ssq_accum(bank, src_ap, src_keys, T, first, last):
                i = st['sq'] = (st['sq'] + 1) % 2
                act(sq[:, i, 0:T], src_ap, AF.Square, r=src_keys, w=['sq%d' % i])
                mm(ps[bank][:, 0:T], ones[:], sq[:, i, 0:T], start=first, stop=last, r=['sq%d' % i], w=['ps%d' % bank], inc=True)

            def rstd_from(bank, n, slot, T):
                f = fscr()
                act(fsc[:, f, 0:T], ps[bank][:, 0:T], AF.Ln, r=['ps%d' % bank], w=['f%d' % f], scale=1.0 / n, bias=epsc[:, 0:1])
                act(rst[:, slot, 0:T], fsc[:, f, 0:T], AF.Exp, r=['f%d' % f], w=['rst%d' % slot], scale=-0.5)

            def norm_mod(l, blk, kind_sh, kind_sc):
                T = blk['T']
                b = gbank()
                for kc in range(16):
                    ssq_accum(b, xT[:, kc, 0:T], ['x%d' % kc], T, kc == 0, kc == 15)
                rstd_from(b, D, 0, T)
                for kc in range(16):
                    f = fscr()
                    tt(fsc[:, f, 0:T], xT[:, kc, 0:T], rst[:, 0, 0:T], ALU.mult, r=['x%d' % kc, 'rst0'], w=['f%d' % f])
                    for sg in blk['segs']:
                        act(R1[:, kc, sg.c0:sg.c0 + sg.n], fsc[:, f, sg.c0:sg.c0 + sg.n], AF.Identity, r=['f%d' % f], w=['r%d' % kc],
                            scale=modv(l, sg.e, kind_sc, kc), bias=modv(l, sg.e, kind_sh, kc))

            def attention(l, blk, grp, wk):
                q0, nq = grp['q0'], grp['nq']
                nkb = grp['nkb']
                for h in range(NH):
                    g, hh = h // 4, h % 4
                    wkeys = ['w%d' % wk[g]]
                    hb = h % 2
                    kexp = R46[:, hb * 2048:hb * 2048 + 1024].bitcast(BF16)
                    vexp = R46[:, hb * 2048 + 1024:hb * 2048 + 2048].bitcast(BF16)
                    kk = ['g%d' % (hb * 8 + i) for i in range(4)]
                    vk = ['g%d' % (hb * 8 + 4 + i) for i in range(4)]
                    for kb in range(nkb):
                        b = gbank()
                        for kc in range(4):
                            ap_, keys_ = grp['ckv'](kc, kb * 512, 512)
                            mm(ps[b][:, 0:512], pc(wk[g], kc, 1024, hh * 256), ap_, start=(kc == 0), stop=(kc == 3), r=wkeys + keys_, w=['ps%d' % b])
                        cp(alt(), kexp[:, kb * 512:(kb + 1) * 512], ps[b][:, 0:512], r=['ps%d' % b], w=kk)
                        b = gbank()
                        for t4 in range(4):
                            for kc in range(4):
                                ap_, keys_ = grp['ckv'](kc, kb * 512 + t4 * 128, 128)
                                mm(ps[b][:, t4 * 128:(t4 + 1) * 128], ap_, pc(wk[g], kc, 1024, hh * 256 + 128), start=(kc == 0), stop=(kc == 3),
                                   r=wkeys + keys_, w=['ps%d' % b], inc=(kc == 3 and t4 == 3))
                        cp(alt(), vexp[:, kb * 512:(kb + 1) * 512], ps[b][:, 0:512], r=['ps%d' % b], w=vk)
                    own_n = grp['own_n']
                    if own_n:
                        b = gbank()
                        for kc in range(4):
                            mm(ps[b][:, 0:own_n], pc(wk[g], kc, 1024, hh * 256), ownckv[:, kc, q0:q0 + own_n], start=(kc == 0), stop=(kc == 3), r=wkeys + ['ock'], w=['ps%d' % b])
                        cp(alt(), kown[:, 0:own_n], ps[b][:, 0:own_n], r=['ps%d' % b], w=['kown'])
                        b = gbank()
                        for kc in range(4):
                            mm(ps[b][0:own_n, 0:128], ownckv[:, kc, q0:q0 + own_n], pc(wk[g], kc, 1024, hh * 256 + 128), start=(kc == 0), stop=(kc == 3), r=wkeys + ['ock'], w=['ps%d' % b])
                        cp(alt(), vown[0:own_n, :], ps[b][0:own_n, 0:128], r=['ps%d' % b], w=['vown'])
                    tiles = []
                    for kt in range(nkb * 4):
                        if grp['diag'] and kt >= (nkb - 1) * 4:
                            j = kt - (nkb - 1) * 4
                            tiles.append((kt, 128, q0 + j * 128, nq - j * 128, True))
                        else:
                            tiles.append((kt, 128, q0, nq, False))
                    if own_n:
                        tiles.append((-1, own_n, q0, nq, False))
                    qn_ap = lambda c0, n: R1[:, 2 * h, c0:c0 + n]
                    qp_ap = lambda c0, n: R1[:, 2 * h + 1, c0:c0 + n]
                    qkeys = ['r%d' % (2 * h), 'r%d' % (2 * h + 1)]
                    for ti, (kt, nk, c0, n, dg) in enumerate(tiles):
                        sb_ = 4 + (ti % 2)
                        if kt >= 0:
                            kl = kexp[:, kt * 128:(kt + 1) * 128]
                            krl, krkeys = grp['kr'](kt * 128, 128)
                            vl = vexp[:, kt * 128:(kt + 1) * 128]
                            rk, rv = kk, vk
                        else:
                            kl = kown[:, 0:nk]
                            krl, krkeys = ownkr[:, q0:q0 + nk], ['okr']
                            vl = vown[0:nk, :]
                            rk, rv = ['kown'], ['vown']
                        mm(ps[sb_][0:nk, 0:n], kl, qn_ap(c0, n), start=True, stop=False, r=rk + qkeys, w=['ps%d' % sb_], inc=False)
                        mm(ps[sb_][0:nk, 0:n], krl, qp_ap(c0, n), start=False, stop=True, r=krkeys + qkeys, w=['ps%d' % sb_], inc=True)
                        pi = st['pt'] = (st['pt'] + 1) % 2
                        act(pT[0:nk, pi, 0:n], ps[sb_][0:nk, 0:n], AF.Exp, r=['ps%d' % sb_], w=['pT%d' % pi], scale=ATTN_SCALE)
                        if dg:
                            S.op('dve', lambda: nc.vector.memset(pT[64:128, pi, 0:64], 0.0), r=[], w=['pT%d' % pi])
                        first = (ti == 0)
                        last = (ti == len(tiles) - 1)
                        mm(ps[6][:, c0 - q0:c0 - q0 + n], vl, pT[0:nk, pi, 0:n], start=first, stop=last, r=rv + ['pT%d' % pi], w=['ps6'], inc=False)
                        mm(ps[7][:, c0 - q0:c0 - q0 + n], ones[0:nk, :], pT[0:nk, pi, 0:n], start=first, stop=last, r=['pT%d' % pi], w=['ps7'], inc=True)
                    f = fscr()
                    S.op('dve', lambda: nc.vector.reciprocal(out=fsc[:, f, 0:nq], in_=ps[7][:, 0:nq]), r=['ps7'], w=['f%d' % f])
                    tt(R1[:, 2 * h, q0:q0 + nq], ps[6][:, 0:nq], fsc[:, f, 0:nq], ALU.mult, r=['ps6', 'f%d' % f], w=['r%d' % (2 * h)])

            def proj_group(blk, klist, consume):
                T = blk['T']
                nk = len(klist)
                keyl = lambda k: list(k) if isinstance(k, (list, tuple)) else [k]
                if not blk['samp']:
                    for j in range(4):
                        b = gbank()
                        for i, (s_, k_, a_ap, a_key) in enumerate(klist):
                            mm(ps[b][:, 0:T], pc(s_, k_, 512, j * 128), a_ap, start=(i == 0), stop=(i == nk - 1), r=['w%d' % s_] + keyl(a_key), w=['ps%d' % b])
                        consume(j, ps[b][:, 0:T], 'ps%d' % b)
                else:
                    b = gbank()
                    for i, (s_, k_, a_ap, a_key) in enumerate(klist):
                        mm(ps[b][0:T, 0:512], a_ap, pc(s_, k_, 512, 0, 512), start=(i == 0), stop=(i == nk - 1), r=['w%d' % s_] + keyl(a_key), w=['ps%d' % b])
                    cp('act', tmS[0:T, :], ps[b][0:T, 0:512], r=['ps%d' % b], w=['tmS'])
                    b2 = gbank()
                    for j in range(4):
                        tr(ps[b2][:, j * T:(j + 1) * T], tmS[0:T, j * 128:(j + 1) * 128], T, r=['tmS'], w=['ps%d' % b2], inc=(j == 3))
                    for j in range(4):
                        consume(j, ps[b2][:, j * T:(j + 1) * T], 'ps%d' % b2)

            def layer_block(l, blk):
                T = blk['T']
                segs = blk['segs']
                tiles = blk['tiles']
                tok0 = blk['tok0']
                samp = blk['samp']
                b_idx = blk['b']
                Wl = w_in[l]
                S.dma('sp', kvt[:], kvn[l:l + 1, :].partition_broadcast(128), sem="kvt", w=['kvt'])
                if l == 0:
                    S.dma('sp', csT[:, 0:T], tabF[:, tok0:tok0 + T], sem="csT", w=['csT'])
                norm_mod(l, blk, MSH_A, MSC_A)
                chk('norm1')
                sA = load_std(Wl, 0, 8, 0, 512)
                sB = load_std(Wl, 8, 8, 0, 512)
                for m in range(4):
                    b = gbank()
                    for kc in range(16):
                        s_, k_ = (sA, kc) if kc < 8 else (sB, kc - 8)
                        mm(ps[b][:, 0:T], pc(s_, k_, 512, m * 128), R1[:, kc, 0:T], start=(kc == 0), stop=(kc == 15), r=['w%d' % s_, 'r%d' % kc], w=['ps%d' % b])
                    cp('dve', ql[:, m, 0:T], ps[b][:, 0:T], r=['ps%d' % b], w=['ql%d' % m])
                    ssq_accum(6, ql[:, m, 0:T], ['ql%d' % m], T, m == 0, m == 3)
                chk('qlat')
                sA = load_std(Wl, 0, 8, 512, 512)
                sB = load_std(Wl, 8, 8, 512, 512)
                for (r0, rows) in tiles:
                    b = gbank()
                    for kc in range(16):
                        s_, k_ = (sA, kc) if kc < 8 else (sB, kc - 8)
                        mm(ps[b][0:rows, 0:512], R1[:, kc, r0:r0 + rows], pc(s_, k_, 512, 0, 512), start=(kc == 0), stop=(kc == 15), r=['w%d' % s_, 'r%d' % kc], w=['ps%d' % b])
                    f = fscr()
                    S.op('dve', lambda: nc.vector.memset(sm[:, 0:1], 0.0), w=['sm'])
                    act(fsc[0:rows, f, 0:512], ps[b][0:rows, 0:512], AF.Square, r=['ps%d' % b], w=['f%d' % f, 'sm'], accum=sm[0:rows, 0:1])
                    act(sm[0:rows, 1:2], sm[0:rows, 0:1], AF.Ln, r=['sm'], w=['sm'], scale=1.0 / 512, bias=epsc[0:rows, 0:1])
                    act(sm[0:rows, 2:3], sm[0:rows, 1:2], AF.Exp, r=['sm'], w=['sm'], scale=-0.5)
                    stt(cst[0:rows, :], ps[b][0:rows, 0:512], sm[0:rows, 2:3], kvt[0:rows, :], ALU.mult, ALU.mult, r=['ps%d' % b, 'sm', 'kvt'], w=['cst'])
                    if samp:
                        S.dma('sp', o_ckvs[l, r0:r0 + rows, :], cst[0:rows, :], sem="cst_o", r=['cst'])
                    else:
                        S.dma('sp', o_ckvp[l, tok0 + r0:tok0 + r0 + rows, :], cst[0:rows, :], sem="cst_o", r=['cst'])
                    b2 = gbank()
                    for c in range(4):
                        tr(ps[b2][:, c * 128:c * 128 + rows], cst[0:rows, c * 128:(c + 1) * 128], rows, r=['cst'], w=['ps%d' % b2], inc=(c == 3))
                    for c in range(4):
                        cp(alt(), ownckv[:, c, r0:r0 + rows], ps[b2][:, c * 128:c * 128 + rows], r=['ps%d' % b2], w=['ock'])
                chk('kv')
                s = new_piece()
                S.dma('pool', slot(s)[:, 0:1024].rearrange("p (k n) -> p k n", k=16), Wl[:, 1024:1088].rearrange("(k p) n -> p k n", p=128), sem="w%d" % s, w=['w%d' % s])
                for (r0, rows) in tiles:
                    b = gbank()
                    S.dma('sp', ttab[0:rows, :], tabT[tok0 + r0:tok0 + r0 + rows, :], sem="ttab", w=['ttab'])
                    for kc in range(16):
                        mm(ps[b][0:rows, 0:64], R1[:, kc, r0:r0 + rows], pc(s, kc, 64, 0, 64), start=(kc == 0), stop=(kc == 15), r=['w%d' % s, 'r%d' % kc], w=['ps%d' % b])
                    tt(krt[0:rows, 0, 0:64], ps[b][0:rows, 0:64], ttab[0:rows, 0:64], ALU.mult, r=['ps%d' % b, 'ttab'], w=['krtA'])
                    tt(krt[0:rows, 1, 0:32], ps[b][0:rows, 32:64], ttab[0:rows, 64:96], ALU.mult, r=['ps%d' % b, 'ttab'], w=['krtB'])
                    tt(krt[0:rows, 1, 32:64], ps[b][0:rows, 0:32], ttab[0:rows, 96:128], ALU.mult, r=['ps%d' % b, 'ttab', 'krtB'], w=['krtB'])
                    tt(krt[0:rows, 2, 0:64], krt[0:rows, 0, 0:64], krt[0:rows, 1, 0:64], ALU.add, r=['krtA', 'krtB'], w=['krtC'])
                    tt(krt[0:rows, 2, 64:128], krt[0:rows, 0, 0:64], krt[0:rows, 1, 0:64], ALU.add, r=['krtA', 'krtB', 'krtC'], w=['krtC'])
                    if samp:
                        S.dma('sp', o_krs[l, r0:r0 + rows, :], krt[0:rows, 2, 0:64], sem="krt_o", r=['krtC'])
                    else:
                        S.dma('sp', o_krp[l, tok0 + r0:tok0 + r0 + rows, :], krt[0:rows, 2, 0:64], sem="krt_o", r=['krtC'])
                    b2 = gbank()
                    tr(ps[b2][:, 0:rows], krt[0:rows, 2, :], rows, r=['krtC'], w=['ps%d' % b2])
                    cp(alt(), ownkr[:, r0:r0 + rows], ps[b2][:, 0:rows], r=['ps%d' % b2], w=['okr'])
                chk('kpe')
                gcs = R46[:, 0:1024].bitcast(BF16)
                CUW = 516
                cu = R46[:, 1024:1024 + 4 * CUW]
                cukeys = ['g%d' % i for i in range(4, 13)]
                for half in range(2):
                    sA = load_std(Wl, 0, 8, 2112 + half * 512, 512)
                    sB = load_std(Wl, 8, 8, 2112 + half * 512, 512)
                    for j in range(4):
                        b = gbank()
                        for kc in range(16):
                            s_, k_ = (sA, kc) if kc < 8 else (sB, kc - 8)
                            mm(ps[b][:, 0:T], pc(s_, k_, 512, j * 128), R1[:, kc, 0:T], start=(kc == 0), stop=(kc == 15), r=['w%d' % s_, 'r%d' % kc], w=['ps%d' % b])
                        cp('act', gcs[:, j * 512:j * 512 + T], ps[b][:, 0:T], r=['ps%d' % b], w=['g%d' % j])
                    sA = load_std(Wl, 0, 8, 3136 + half * 512, 512)
                    sB = load_std(Wl, 8, 8, 3136 + half * 512, 512)
                    for j in range(4):
                        c = half * 4 + j
                        b = gbank()
                        for kc in range(16):
                            s_, k_ = (sA, kc) if kc < 8 else (sB, kc - 8)
                            mm(ps[b][:, 0:T], pc(s_, k_, 512, j * 128), R1[:, kc, 0:T], start=(kc == 0), stop=(kc == 15), r=['w%d' % s_, 'r%d' % kc], w=['ps%d' % b])
                        for si, sg in enumerate(segs):
                            so = j * CUW + si * (2 + sg.n)
                            if samp:
                                cp('dve', cu[:, so:so + 2], blk['halo'][si][:, :, c], r=['halo%d' % si], w=cukeys)
                            else:
                                cp('dve', cu[:, so:so + 2], hal[:, l, c, :], r=['hal'], w=cukeys)
                            tt(cu[:, so + 2:so + 2 + sg.n], ps[b][:, sg.c0:sg.c0 + sg.n], gcs[:, j * 512 + sg.c0:j * 512 + sg.c0 + sg.n], ALU.mult,
                               r=['ps%d' % b, 'g%d' % j], w=cukeys)
                            if samp or b_idx == 3:
                                cp('dve', cs2[:, :, c] if not samp else blk['cs2'][si][:, :, c], cu[:, so + sg.n:so + sg.n + 2], r=cukeys, w=['cs2_%d' % si])
                            if not samp:
                                cp('dve', hal[:, l, c, :], cu[:, so + sg.n:so + sg.n + 2], r=cukeys, w=['hal'])
                    sA = load_std(Wl, 0, 8, 1088 + half * 512, 512)
                    sB = load_std(Wl, 8, 8, 1088 + half * 512, 512)
                    for j in range(4):
                        c = half * 4 + j
                        b = gbank()
                        for kc in range(16):
                            s_, k_ = (sA, kc) if kc < 8 else (sB, kc - 8)
                            mm(ps[b][:, 0:T], pc(s_, k_, 512, j * 128), R1[:, kc, 0:T], start=(kc == 0), stop=(kc == 15), r=['w%d' % s_, 'r%d' % kc], w=['ps%d' % b])
                        f = fscr()
                        for si, sg in enumerate(segs):
                            so = j * CUW + si * (2 + sg.n)
                            wv = lambda k: vT[:, 464 + l * 24 + k * 8 + c:464 + l * 24 + k * 8 + c + 1]
                            y = fsc[:, f, sg.c0:sg.c0 + sg.n]
                            ts(y, cu[:, so:so + sg.n], wv(0), None, ALU.mult, None, r=cukeys, w=['f%d' % f])
                            stt(y, cu[:, so + 1:so + 1 + sg.n], wv(1), y, ALU.mult, ALU.add, r=cukeys + ['f%d' % f], w=['f%d' % f])
                            stt(y, cu[:, so + 2:so + 2 + sg.n], wv(2), y, ALU.mult, ALU.add, r=cukeys + ['f%d' % f], w=['f%d' % f])
                            tt(convn[:, c, sg.c0:sg.c0 + sg.n], ps[b][:, sg.c0:sg.c0 + sg.n], y, ALU.mult, r=['ps%d' % b, 'f%d' % f], w=['cn%d' % c])
                        ssq_accum(7, convn[:, c, 0:T], ['cn%d' % c], T, c == 0, c == 7)
                if samp or b_idx == 3:
                    for si, sg in enumerate(segs):
                        src = blk['cs2'][si] if samp else cs2
                        for t in range(2):
                            dst = (o_convs[l, si, t] if samp else o_convp[l, t]).rearrange("(c p) -> p c", p=128)
                            S.dma('sp', dst, src[:, t, :], sem="cs2_o%d" % si, r=['cs2_%d' % si], nonc=True)
                rstd_from(7, 1024, 1, T)
                for c in range(8):
                    stt(convn[:, c, 0:T], convn[:, c, 0:T], vT[:, 432 + l * 8 + c:432 + l * 8 + c + 1], rst[:, 1, 0:T], ALU.mult, ALU.mult,
                        r=['cn%d' % c, 'rst1'], w=['cn%d' % c])
                chk('conv')
                rstd_from(6, 512, 0, T)
                for m in range(4):
                    stt(ql[:, m, 0:T], ql[:, m, 0:T], vT[:, 384 + l * 4 + m:384 + l * 4 + m + 1], rst[:, 0, 0:T], ALU.mult, ALU.mult,
                        r=['ql%d' % m, 'rst0'], w=['ql%d' % m])
                for g in range(2):
                    s = new_piece()
                    dst = slot(s).rearrange("p (k h c) -> p k h c", k=4, h=4)
                    srcv = w_uq[l].rearrange("(k p) (h c) -> p k h c", p=128, c=192)
                    for kc in range(4):
                        S.dma('pool', dst[:, kc, :, 0:192], srcv[:, kc, 4 * g:4 * g + 4, :], sem="w%d" % s, w=['w%d' % s])
                        S.dma('pool', dst[:, kc, :, 192:224], srcv[:, kc, 4 * g:4 * g + 4, 160:192], sem="w%d" % s, w=['w%d' % s])
                        S.dma('pool', dst[:, kc, :, 224:256], srcv[:, kc, 4 * g:4 * g + 4, 128:160], sem="w%d" % s, w=['w%d' % s])
                    for hh in range(4):
                        h = 4 * g + hh
                        for part in range(2):
                            b = gbank()
                            for kc in range(4):
                                mm(ps[b][:, 0:T], slot(s)[:, kc * 1024 + hh * 256 + part * 128:kc * 1024 + hh * 256 + part * 128 + 128], ql[:, kc, 0:T],
                                   start=(kc == 0), stop=(kc == 3), r=['w%d' % s, 'ql%d' % kc], w=['ps%d' % b])
                            if part == 0:
                                cp('act', R1[:, 2 * h, 0:T], ps[b][:, 0:T], r=['ps%d' % b], w=['r%d' % (2 * h)])
                            else:
                                tt(R1[:, 2 * h + 1, 0:T], ps[b][:, 0:T], csT[:, 0:T], ALU.mult, r=['ps%d' % b, 'csT'], w=['r%d' % (2 * h + 1)])
                chk('q')
                if not samp:
                    wk = [load_std(w_ukv[l], 0, 4, g * 1024, 1024) for g in range(2)]

                    def ckv_acc(kc, t0, n):
                        if t0 >= b_idx * 512:
                            return ownckv[:, kc, t0 - b_idx * 512:t0 - b_idx * 512 + n], ['ock']
                        o = (l * 4 + kc) * 1536 + t0
                        return cache[:, o:o + n], ['cache']

                    def kr_acc(t0, n):
                        if t0 >= b_idx * 512:
                            return ownkr[:, t0 - b_idx * 512:t0 - b_idx * 512 + n], ['okr']
                        return krc[:, l * 1536 + t0:l * 1536 + t0 + n], ['krc']
                    attention(l, blk, dict(q0=0, nq=512, nkb=b_idx + 1, ckv=ckv_acc, kr=kr_acc, own_n=0, diag=True), wk)
                    if b_idx < 3:
                        for kc in range(4):
                            o = (l * 4 + kc) * 1536 + b_idx * 512
                            cp(alt(), cache[:, o:o + 512], ownckv[:, kc, :], r=['ock'], w=['cache'])
                        cp(alt(), krc[:, l * 1536 + b_idx * 512:l * 1536 + b_idx * 512 + 512], ownkr[:, :], r=['okr'], w=['krc'])
                else:
                    for si, sg in enumerate(segs):
                        for q4 in range(4):
                            stg = R46[:, (q4 % 2) * 2048:(q4 % 2) * 2048 + 2048]
                            sk = ['g%d' % ((q4 % 2) * 8 + i) for i in range(8)]
                            S.dma('sp', stg.rearrange("p (t n) -> p t n", t=4), pkv[l, si, q4 * 512:(q4 + 1) * 512, :].rearrange("(t p) n -> p t n", p=128),
                                  sem="pst%d" % (q4 % 2), w=sk)
                            for t4 in range(4):
                                b = gbank()
                                for c in range(4):
                                    tr(ps[b][:, c * 128:(c + 1) * 128], stg[:, t4 * 512 + c * 128:t4 * 512 + (c + 1) * 128], 128, r=sk, w=['ps%d' % b], inc=(c == 3))
                                for c in range(4):
                                    o = c * 2048 + q4 * 512 + t4 * 128
                                    cp(alt(), cache[:, o:o + 128], ps[b][:, c * 128:(c + 1) * 128], r=['ps%d' % b], w=['cache'])
                        for q4 in range(4):
                            stg = R46[:, (q4 % 2) * 2048:(q4 % 2) * 2048 + 512]
                            sk = ['g%d' % ((q4 % 2) * 8 + i) for i in range(2)]
                            for dup in range(2):
                                S.dma('sp', stg.rearrange("p (t n) -> p t n", t=4)[:, :, dup * 64:(dup + 1) * 64],
                                      pkr[l, si, q4 * 512:(q4 + 1) * 512, :].rearrange("(t p) n -> p t n", p=128), sem="pst%d" % (q4 % 2), w=sk)
                            b = gbank()
                            for t4 in range(4):
                                tr(ps[b][:, t4 * 128:(t4 + 1) * 128], stg[:, t4 * 128:(t4 + 1) * 128], 128, r=sk, w=['ps%d' % b], inc=(t4 == 3))
                            cp(alt(), krc[:, q4 * 512:(q4 + 1) * 512], ps[b][:, 0:512], r=['ps%d' % b], w=['krc'])
                        wk = [load_std(w_ukv[l], 0, 4, g * 1024, 1024) for g in range(2)]

                        def ckv_acc(kc, t0, n):
                            return cache[:, kc * 2048 + t0:kc * 2048 + t0 + n], ['cache']

                        def kr_acc(t0, n):
                            return krc[:, t0:t0 + n], ['krc']
                        attention(l, blk, dict(q0=sg.c0, nq=sg.n, nkb=4, ckv=ckv_acc, kr=kr_acc, own_n=sg.n, diag=False), wk)
                chk('attn')
                b = gbank()
                for h in range(NH):
                    ssq_accum(b, R1[:, 2 * h, 0:T], ['r%d' % (2 * h)], T, h == 0, h == NH - 1)
                rstd_from(b, 1024, 0, T)
                for h in range(NH):
                    stt(R1[:, 2 * h, 0:T], R1[:, 2 * h, 0:T], vT[:, 400 + l * 8 + h:400 + l * 8 + h + 1], rst[:, 0, 0:T], ALU.mult, ALU.mult,
                        r=['r%d' % (2 * h), 'rst0'], w=['r%d' % (2 * h)])
                for cg in range(4):
                    sA = load_std(w_out[l], 0, 8, cg * 512, 512)
                    sB = load_std(w_out[l], 8, 8, cg * 512, 512)
                    for j in range(4):
                        m = cg * 4 + j
                        b = gbank()
                        for kc in range(16):
                            if kc < 8:
                                mm(ps[b][:, 0:T], pc(sA, kc, 512, j * 128), R1[:, 2 * kc, 0:T], start=(kc == 0), stop=False, r=['w%d' % sA, 'r%d' % (2 * kc)], w=['ps%d' % b])
                            else:
                                mm(ps[b][:, 0:T], pc(sB, kc - 8, 512, j * 128), convn[:, kc - 8, 0:T], start=False, stop=(kc == 15), r=['w%d' % sB, 'cn%d' % (kc - 8)], w=['ps%d' % b])
                        for sg in segs:
                            stt(xT[:, m, sg.c0:sg.c0 + sg.n], ps[b][:, sg.c0:sg.c0 + sg.n], modv(l, sg.e, MG_A, m), xT[:, m, sg.c0:sg.c0 + sg.n], ALU.mult, ALU.add,
                                r=['ps%d' % b, 'x%d' % m], w=['x%d' % m])
                chk('wout')
                norm_mod(l, blk, MSH_M, MSC_M)
                for g in range(8):
                    ub = (g % 2) * 2048
                    up = R46[:, ub:ub + 2048].bitcast(BF16)
                    uk = ['g%d' % ((g % 2) * 8 + i) for i in range(8)]
                    for half in range(2):
                        sA = load_std(w_up[l], 0, 8, g * 1024 + half * 512, 512)
                        sB = load_std(w_up[l], 8, 8, g * 1024 + half * 512, 512)
                        klist = [((sA, kc) if kc < 8 else (sB, kc - 8)) + (R1[:, kc, 0:T], 'r%d' % kc) for kc in range(16)]

                        def consume_up(j, ps_ap, ps_key, half=half, up=up, uk=uk):
                            act(rtmp[:, 0:T], ps_ap, AF.Relu, r=[ps_key], w=['rtmp'])
                            jj = half * 4 + j
                            tt(up[:, jj * 512:jj * 512 + T], rtmp[:, 0:T], rtmp[:, 0:T], ALU.mult, r=['rtmp'], w=uk)
                        proj_group(blk, klist, consume_up)
                    for cg in range(4):
                        s = load_std(w_down[l], g * 8, 8, cg * 512, 512)
                        klist = [(s, kc, up[:, kc * 512:kc * 512 + T], uk) for kc in range(8)]

                        def consume_dn(j, ps_ap, ps_key, cg=cg):
                            m = cg * 4 + j
                            for sg in segs:
                                stt(xT[:, m, sg.c0:sg.c0 + sg.n], ps_ap[:, sg.c0:sg.c0 + sg.n], modv(l, sg.e, MG_M, m), xT[:, m, sg.c0:sg.c0 + sg.n], ALU.mult, ALU.add,
                                    r=[ps_key, 'x%d' % m], w=['x%d' % m])
                        proj_group(blk, klist, consume_dn)

            def run_block(blk, xsrc, ydst):
                T = blk['T']
                tiles = blk['tiles']
                gk = ['g%d' % i for i in range(16)]
                for ti, (r0, rows) in enumerate(tiles):
                    stg = R46[:, (ti % 2) * 2048:(ti % 2) * 2048 + 2048]
                    sk = ['g%d' % ((ti % 2) * 8 + i) for i in range(8)]
                    S.dma('sp', stg[0:rows, :], xsrc[r0:r0 + rows, :], sem="xst%d" % (ti % 2), w=sk)
                    chk('xl1')
                    for c4 in range(4):
                        b = gbank()
                        if _CACHE.get('xv', 0) == 3:
                            b = 1
                        for c in range(4):
                            cc = c4 * 4 + c
                            tr(ps[b][:, c * 128:c * 128 + rows], stg[0:rows, cc * 128:(cc + 1) * 128], rows, r=sk, w=['ps%d' % b], inc=(c == 3))
                        chk('xl2')
                        chk('xt%d_%d' % (ti, c4))
                        xv = _CACHE.get('xv', 7)
                        for c in range(4):
                            cc = c4 * 4 + c
                            if xv == 1 and c4 == 1 and c > 0:
                                continue
                            if xv == 2:
                                cc = c
                            if xv == 5 and c4 == 1 and c != 1:
                                continue
                            en = alt()
                            if (xv == 4 and c4 == 1) or xv == 7:
                                en = 'act'
                            if xv == 5 and c4 == 1:
                                en = 'dve'
                            if xv == 6:
                                cp(en, fsc[:, 0, c * 128:c * 128 + rows], ps[b][:, c * 128:c * 128 + rows], r=['ps%d' % b], w=['x%d' % cc])
                            else:
                                cp(en, xT[:, cc, r0:r0 + rows], ps[b][:, c * 128:c * 128 + rows], r=['ps%d' % b], w=['x%d' % cc])
                        if _CACHE.get('serialtr'):
                            S._wait('pe', ('act', S.cnt['act']))
                            S._wait('pe', ('dve', S.cnt['dve']))
                        chk('xl3')
                        chk('xg%d_%d' % (ti, c4))
                chk('xload')
                for l in range(NL):
                    layer_block(l, blk)
                chk('layers')
                b = gbank()
                for kc in range(16):
                    ssq_accum(b, xT[:, kc, 0:T], ['x%d' % kc], T, kc == 0, kc == 15)
                rstd_from(b, D, 0, T)
                for kc in range(16):
                    stt(xT[:, kc, 0:T], xT[:, kc, 0:T], vT[:, 560 + kc:561 + kc], rst[:, 0, 0:T], ALU.mult, ALU.mult, r=['x%d' % kc, 'rst0'], w=['x%d' % kc])
                for ti, (r0, rows) in enumerate(tiles):
                    stg = R46[:, (ti % 2) * 2048:(ti % 2) * 2048 + 2048]
                    sk = ['g%d' % ((ti % 2) * 8 + i) for i in range(8)]
                    for c4 in range(4):
                        b = gbank()
                        for c in range(4):
                            cc = c4 * 4 + c
                            tr(ps[b][0:rows, c * 128:(c + 1) * 128], xT[:, cc, r0:r0 + rows], 128, r=['x%d' % cc], w=['ps%d' % b], inc=(c == 3))
                        cp(alt(), stg[0:rows, c4 * 512:(c4 + 1) * 512], ps[b][0:rows, 0:512], r=['ps%d' % b], w=sk)
                    S.dma('sp', ydst[r0:r0 + rows, :], stg[0:rows, :], sem="yst%d" % (ti % 2), r=sk)

            for bi in range(NBLK):
                blk = dict(T=512, segs=[Seg(0, 512, 0)], tiles=[(i * 128, 128) for i in range(4)], tok0=bi * 512, samp=False, b=bi)
                run_block(blk, xp[bi * 512:(bi + 1) * 512, :], o_yp[bi * 512:(bi + 1) * 512, :])
            halo_t = [sbt("halo%d" % i, [128, 2, 8], F32) for i in range(2)]
            cs2_t = [sbt("cs2s%d" % i, [128, 2, 8], F32) for i in range(2)]
            sblk = dict(T=32, segs=[Seg(0, 16, 1), Seg(16, 16, 2)], tiles=[(0, 32)], tok0=2048, samp=True, b=4, halo=halo_t, cs2=cs2_t)
            orig_layer_block = layer_block

            def layer_block_s(l, blk):
                for si in range(2):
                    for t in range(2):
                        S.dma('sp', halo_t[si][:, t, :], sconv[l, si, t].rearrange("(c p) -> p c", p=128), sem="halo%d" % si, w=['halo%d' % si], nonc=True)
                orig_layer_block(l, blk)
            layer_block = layer_block_s
            if SAMPLE:
                if _CACHE.get('nslots_s', 8) > 4:
                    for en_ in ('pe', 'act', 'dve'):
                        if S.cnt[en_] > 0:
                            S._wait('pool', (en_, S.cnt[en_]))
                    st['nslots'] = _CACHE.get('nslots_s', 8)
                run_block(sblk, xs, o_ys)
        except _Stop:
            pass
        S.finish('sp')
        _CACHE['counts'] = dict(S.cnt)
        _CACHE['nwaits'] = S.nwaits
    return nc


_CACHE = {}


def _tables():
    half = 32
    inv = (10000.0 ** (-np.arange(half, dtype=np.float32) / half)).astype(np.float32)
    pos = np.concatenate([np.arange(2048), 2048 + np.arange(16), 2048 + np.arange(16)]).astype(np.float32)
    ang = pos[:, None] * inv[None, :]
    cos, sin = np.cos(ang).astype(np.float32), np.sin(ang).astype(np.float32)
    tabT = np.concatenate([cos, cos, -sin, sin], axis=1).astype(np.float32)
    tabF = np.ascontiguousarray(tabT.T)
    return tabF, np.ascontiguousarray(tabT)


def kernel(x_prompt, x_sample, c_prompt, c_sample, cache_kv_latent, cache_k_rope, state_conv,
           w_ada, b_ada, w_in, q_norm, w_uq, kv_norm, w_ukv, conv_w,
           attn_out_norm, conv_out_norm, w_out, w_up, w_down, final_norm):
    f = lambda a: np.ascontiguousarray(np.asarray(a, dtype=np.float32))
    x_prompt, x_sample, c_prompt, c_sample = f(x_prompt), f(x_sample), f(c_prompt), f(c_sample)
    cache_kv_latent, cache_k_rope, state_conv = f(cache_kv_latent), f(cache_k_rope), f(state_conv)
    w_ada, b_ada, w_in, q_norm, w_uq, kv_norm, w_ukv = f(w_ada), f(b_ada), f(w_in), f(q_norm), f(w_uq), f(kv_norm), f(w_ukv)
    conv_w, attn_out_norm, conv_out_norm, w_out, w_up, w_down, final_norm = f(conv_w), f(attn_out_norm), f(conv_out_norm), f(w_out), f(w_up), f(w_down), f(final_norm)
    if 'nc' not in _CACHE:
        _CACHE['nc'] = build_program()
    nc = _CACHE['nc']
    tabF, tabT = _tables()
    common = np.concatenate([b_ada.reshape(384, 128), q_norm.reshape(16, 128), attn_out_norm.reshape(32, 128),
                             conv_out_norm.reshape(32, 128), conv_w.reshape(96, 128), final_norm.reshape(16, 128)], axis=0)
    in_maps = []
    for c in range(8):
        vecs = np.zeros((640, 128), np.float32)
        vecs[0:576] = common
        vecs[576:592] = c_prompt[c].reshape(16, 128)
        vecs[592:608] = c_sample[2 * c].reshape(16, 128)
        vecs[608:624] = c_sample[2 * c + 1].reshape(16, 128)
        in_maps.append(dict(
            xp=x_prompt[c], xs=np.ascontiguousarray(x_sample[2 * c:2 * c + 2].reshape(32, D)), vecs=vecs, kvn=kv_norm,
            pkv=np.ascontiguousarray(cache_kv_latent[:, 2 * c:2 * c + 2]), pkr=np.ascontiguousarray(cache_k_rope[:, 2 * c:2 * c + 2]),
            sconv=np.ascontiguousarray(state_conv[:, 2 * c:2 * c + 2]), tabF=tabF, tabT=tabT,
            w_ada=w_ada, w_in=w_in, w_uq=w_uq, w_ukv=w_ukv, w_out=w_out, w_up=w_up, w_down=w_down))
    res = run_bass_kernel_spmd(nc, in_maps, core_ids=list(range(8)))
    R = res.results
    y_p = np.stack([R[c]["o_yp"] for c in range(8)], axis=0)
    y_s = np.concatenate([R[c]["o_ys"].reshape(2, 16, D) for c in range(8)], axis=0)
    ckv_p = np.stack([R[c]["o_ckvp"] for c in range(8)], axis=1)
    kr_p = np.stack([R[c]["o_krp"] for c in range(8)], axis=1)
    conv_p = np.stack([R[c]["o_convp"] for c in range(8)], axis=1)
    ckv_s = np.concatenate([R[c]["o_ckvs"].reshape(L, 2, 16, 512) for c in range(8)], axis=1)
    kr_s = np.concatenate([R[c]["o_krs"].reshape(L, 2, 16, 64) for c in range(8)], axis=1)
    conv_s = np.concatenate([R[c]["o_convs"] for c in range(8)], axis=1)
    return (y_p.astype(np.float32), y_s.astype(np.float32), ckv_p.astype(np.float32), kr_p.astype(np.float32),
            conv_p.astype(np.float32), ckv_s.astype(np.float32), kr_s.astype(np.float32), conv_s.astype(np.float32))
```

```python
import contextlib
import numpy as np
import concourse.bass as bass
import concourse.mybir as mybir
from concourse.bass_utils import run_bass_kernel_spmd

F32, BF16 = mybir.dt.float32, mybir.dt.bfloat16
AF = mybir.ActivationFunctionType
ALU = mybir.AluOpType

L = 4
D = 2048
SEQ = 2048
NH = 8
TB = 512
EPS = 1e-6
ATTN_SCALE = 192.0 ** -0.5
INC = 4160
DFF = 8192
NTAB = 2048 + 32


class Sched:
    def __init__(self, nc, es):
        self.nc = nc
        self.es = es
        self.eng = {'pe': nc.tensor, 'act': nc.scalar, 'dve': nc.vector, 'pool': nc.gpsimd, 'sp': nc.sync}
        self.h = {}
        self.cnt = {}
        for e in self.eng:
            self.h[e] = es.enter_context(nc.semaphore("s_" + e))
            self.cnt[e] = 0
        self.seen = {e: {} for e in self.eng}
        self.snap = {}
        self.bufs = {}
        self.nwaits = 0

    def _wait(self, e, tok):
        name, val = tok
        if name == 'pe' and e == 'pe':
            return
        if self.seen[e].get(name, 0) >= val:
            return
        self.eng[e].wait_ge(self.h[name], val)
        self.nwaits += 1
        self.seen[e][name] = val
        sn = self.snap.get(tok)
        if sn:
            se = self.seen[e]
            for k, v in sn.items():
                if se.get(k, 0) < v:
                    se[k] = v

    def _deps(self, e, r, w):
        deps = set()
        for k in r:
            b = self.bufs.get(k)
            if b and b[0]:
                deps.add(b[0])
        for k in w:
            b = self.bufs.get(k)
            if b:
                if b[0]:
                    deps.add(b[0])
                for n, v in b[1].items():
                    deps.add((n, v))
        for t in sorted(deps):
            self._wait(e, t)

    def _record(self, tok, r, w):
        for k in r:
            b = self.bufs.setdefault(k, [None, {}])
            if b[1].get(tok[0], 0) < tok[1]:
                b[1][tok[0]] = tok[1]
        for k in w:
            self.bufs[k] = [tok, {}]

    def op(self, e, fn, r=(), w=(), inc=True):
        self._deps(e, r, w)
        if e in ('dve', 'act') and self.cnt[e] > 0 and _CACHE.get('selfser', 1):
            self._wait(e, (e, self.cnt[e]))
        ins = fn()
        if inc:
            self.cnt[e] += 1
            ins.then_inc(self.h[e], 1)
            tok = (e, self.cnt[e])
            self.snap[tok] = dict(self.seen[e])
        else:
            tok = (e, self.cnt[e] + 1)
        self._record(tok, r, w)
        return ins

    def dma(self, q, out, in_, sem, r=(), w=(), nonc=False):
        for k in w:
            b = self.bufs.get(k)
            if b and b[0] and b[0][0] == sem and not b[1]:
                b[0] = None
        self._deps(q, r, w)
        if sem not in self.h:
            self.h[sem] = self.es.enter_context(self.nc.semaphore("d_" + sem))
            self.cnt[sem] = 0
        if nonc:
            with self.nc.allow_non_contiguous_dma(reason="small strided vector"):
                ins = self.eng[q].dma_start(out=out, in_=in_)
        else:
            ins = self.eng[q].dma_start(out=out, in_=in_)
        self.cnt[sem] += 16
        ins.then_inc(self.h[sem], 16)
        tok = (sem, self.cnt[sem])
        self.snap[tok] = dict(self.seen[q])
        self._record(tok, r, w)

    def finish(self, e='sp'):
        for name, h in self.h.items():
            if name in self.eng:
                continue
            self._wait(e, (name, self.cnt[name]))
        for name in ('act', 'dve', 'pe'):
            if self.cnt[name] > 0:
                self._wait(e, (name, self.cnt[name]))


class _Stop(Exception):
    pass


class Seg:
    def __init__(self, c0, n, e):
        self.c0, self.n, self.e = c0, n, e


def build_program(NL=L, NBLK=4, SAMPLE=True):
    nc = bass.Bass("TRN2", target_bir_lowering=False)
    dt_in = lambda n, s: nc.dram_tensor(n, s, F32, kind="ExternalInput").ap()
    dt_out = lambda n, s: nc.dram_tensor(n, s, F32, kind="ExternalOutput").ap()
    xp = dt_in("xp", [SEQ, D])
    xs = dt_in("xs", [32, D])
    vecs = dt_in("vecs", [640, 128])
    kvn = dt_in("kvn", [NL, 512])
    pkv = dt_in("pkv", [NL, 2, 2048, 512])
    pkr = dt_in("pkr", [NL, 2, 2048, 64])
    sconv = dt_in("sconv", [NL, 2, 2, 1024])
    tabF = dt_in("tabF", [128, NTAB])
    tabT = dt_in("tabT", [NTAB, 128])
    w_ada = dt_in("w_ada", [NL, D, 6 * D])
    w_in = dt_in("w_in", [NL, D, INC])
    w_uq = dt_in("w_uq", [NL, 512, 1536])
    w_ukv = dt_in("w_ukv", [NL, 512, 2048])
    w_out = dt_in("w_out", [NL, D, D])
    w_up = dt_in("w_up", [NL, D, DFF])
    w_down = dt_in("w_down", [NL, DFF, D])
    wscr = [nc.dram_tensor("wscr%d" % i, [100, 128, 4096], BF16, kind="Internal").ap() for i in range(NL)]
    o_yp = dt_out("o_yp", [SEQ, D])
    o_ys = dt_out("o_ys", [32, D])
    o_ckvp = dt_out("o_ckvp", [L, SEQ, 512])
    o_krp = dt_out("o_krp", [L, SEQ, 64])
    o_convp = dt_out("o_convp", [L, 2, 1024])
    o_ckvs = dt_out("o_ckvs", [L, 32, 512])
    o_krs = dt_out("o_krs", [L, 32, 64])
    o_convs = dt_out("o_convs", [L, 2, 2, 1024])

    with contextlib.ExitStack() as es:
        sbt = lambda n, s, d: es.enter_context(nc.sbuf_tensor(n, s, d))
        S = Sched(nc, es)
        cache = sbt("cache", [128, L * 4 * 1536], BF16)
        ownckv = sbt("ownckv", [128, 4, 512], BF16)
        krc = sbt("krc", [128, L * 1536], BF16)
        ownkr = sbt("ownkr", [128, 512], BF16)
        xT = sbt("xT", [128, 16, 512], F32)
        R1 = sbt("R1", [128, 16, 512], BF16)
        convn = sbt("convn", [128, 8, 512], BF16)
        R46 = sbt("R46", [128, 4096], F32)
        ring = sbt("ring", [128, 4, 4096], BF16)
        sq = sbt("sq", [128, 2, 512], BF16)
        rst = sbt("rst", [128, 2, 512], F32)
        fsc = sbt("fsc", [128, 3, 512], F32)
        ql = sbt("ql", [128, 4, 512], BF16)
        pT = sbt("pT", [128, 2, 512], BF16)
        cst = sbt("cst", [128, 512], F32)
        krt = sbt("krt", [128, 3, 128], F32)
        ttab = sbt("ttab", [128, 128], F32)
        csT = sbt("csT", [128, 512], F32)
        kvt = sbt("kvt", [128, 512], F32)
        ident = sbt("ident", [128, 128], F32)
        ones = sbt("ones", [128, 128], BF16)
        onesr = sbt("onesr", [1, 4], BF16)
        epsc = sbt("epsc", [128, 1], F32)
        vT = sbt("vT", [128, 640], F32)
        sT = sbt("sT", [128, 16, 3], BF16)
        mod = sbt("mod", [128, L, 3, 96], F32)
        hal = sbt("hal", [128, L, 8, 2], F32)
        cs2 = sbt("cs2", [128, 2, 8], F32)
        sm = sbt("sm", [128, 8], F32)
        kown = sbt("kown", [128, 32], BF16)
        vown = sbt("vown", [16, 128], BF16)
        rtmp = sbt("rtmp", [128, 512], BF16)
        ps = [es.enter_context(nc.psum_tensor("ps%d" % i, [128, 512], F32)) for i in range(8)]

        st = {'gb': 0, 'piece': 0, 'alt': 0, 'sq': 0, 'f': 0, 'pt': 0, 'nrec': {}}

        def gbank():
            st['gb'] = (st['gb'] + 1) % 4
            return st['gb']

        def alt():
            st['alt'] ^= 1
            return 'act' if st['alt'] else 'dve'

        def mm(out, lhsT, rhs, start, stop, r, w, inc=None):
            if inc is None:
                inc = stop
            S.op('pe', lambda: nc.tensor.matmul(out, lhsT=lhsT, rhs=rhs, start=start, stop=stop), r=r, w=w, inc=inc)

        def tr(out, in_, rows, r, w, inc=True):
            if _CACHE.get('mmtr', 1):
                S.op('pe', lambda: nc.tensor.matmul(out, lhsT=in_, rhs=ident[0:rows, 0:rows], start=True, stop=True), r=r, w=w, inc=inc)
            else:
                S.op('pe', lambda: nc.tensor.transpose(out=out, in_=in_, identity=ident[0:rows, 0:rows]), r=r, w=w, inc=inc)

        def act(out, in_, func, r, w, bias=None, scale=None, accum=None):
            kw = {}
            if bias is not None:
                kw['bias'] = bias
            if scale is not None:
                kw['scale'] = scale
            if accum is not None:
                kw['accum_out'] = accum
            S.op('act', lambda: nc.scalar.activation(out=out, in_=in_, func=func, **kw), r=r, w=w)

        def cp(e, out, in_, r, w):
            if e == 'act':
                S.op('act', lambda: nc.scalar.activation(out=out, in_=in_, func=AF.Copy), r=r, w=w)
            else:
                S.op('dve', lambda: nc.vector.tensor_scalar(out=out, in0=in_, scalar1=1.0, scalar2=None, op0=ALU.mult), r=r, w=w)

        def tt(out, a, b, op, r, w):
            S.op('dve', lambda: nc.vector.tensor_tensor(out=out, in0=a, in1=b, op=op), r=r, w=w)

        def stt(out, in0, scalar, in1, op0, op1, r, w):
            S.op('dve', lambda: nc.vector.scalar_tensor_tensor(out=out, in0=in0, scalar=scalar, in1=in1, op0=op0, op1=op1), r=r, w=w)

        def ts(out, in0, s1, s2, op0, op1, r, w):
            if s2 is None:
                S.op('dve', lambda: nc.vector.tensor_scalar(out=out, in0=in0, scalar1=s1, scalar2=None, op0=op0), r=r, w=w)
            else:
                S.op('dve', lambda: nc.vector.tensor_scalar(out=out, in0=in0, scalar1=s1, scalar2=s2, op0=op0, op1=op1), r=r, w=w)

        def fscr():
            st['f'] = (st['f'] + 1) % 3
            return st['f']

        def new_piece():
            s = st['piece'] % st.get('nslots', 4)
            st['piece'] += 1
            return s

        def slot(s):
            if s < 4:
                return ring[:, s, :]
            return cache[:, 8192 + (s - 4) * 4096:8192 + (s - 3) * 4096]

        recs = {}

        def cached_piece(key, fill):
            s = new_piece()
            if key is None or not _CACHE.get('wcache', 1):
                fill(s)
                return s
            if key in recs:
                l_, i_ = recs[key]
                S.dma('pool', slot(s), wscr[l_][i_], sem="w%d" % s, w=["w%d" % s])
            else:
                assert st.get('pass', 0) == 0, key
                l_ = st['cur_l']
                i_ = st['nrec'].get(l_, 0)
                st['nrec'][l_] = i_ + 1
                assert i_ < 100
                recs[key] = (l_, i_)
                fill(s)
                S.dma('sp', wscr[l_][i_], slot(s), sem="wst%d" % s, r=["w%d" % s])
            return s

        def load_std(W, kc0, nk, c0, ncols):
            def fill(s):
                dst = slot(s)[:, 0:nk * ncols].rearrange("p (k n) -> p k n", k=nk)
                src = W[kc0 * 128:(kc0 + nk) * 128, c0:c0 + ncols].rearrange("(k p) n -> p k n", p=128)
                S.dma('pool', dst, src, sem="w%d" % s, w=["w%d" % s])
            key = None if st.get('cur_l') is None else (W.name, st['cur_l'], kc0, nk, c0, ncols)
            return cached_piece(key, fill)

        def pc(s, k, ncols, m0, mw=128):
            return slot(s)[:, k * ncols + m0:k * ncols + m0 + mw]

        def chk(name):
            if _CACHE.get('stop') == name:
                raise _Stop()

        try:
            S.op('pool', lambda: nc.gpsimd.memset(ident[:], 0.0), w=['ident'])
            S.op('pool', lambda: nc.gpsimd.affine_select(out=ident[:], in_=ident[:], pattern=[[-1, 128]], compare_op=ALU.not_equal, fill=1.0, base=0, channel_multiplier=1), r=['ident'], w=['ident'])
            S.op('pool', lambda: nc.gpsimd.memset(ones[:], 1.0), w=['ones'])
            S.op('pool', lambda: nc.gpsimd.memset(onesr[:], 1.0), w=['onesr'])
            S.op('pool', lambda: nc.gpsimd.memset(epsc[:], EPS), w=['epsc'])
            S.op('pool', lambda: nc.gpsimd.memset(hal[:], 0.0), w=['hal'])
            for _i in range(_CACHE.get('padcopy', 0)):
                S.op('dve', lambda: nc.vector.tensor_copy(out=sm[:, 4:5], in_=sm[:, 5:6]), w=[])
            for _i in range(_CACHE.get('pad', 0)):
                S.op('dve', lambda: nc.vector.memset(sm[:, 4:5], 0.0), w=[])
            ctok = ('pool', S.cnt['pool'])
            for e in ('pe', 'act', 'dve', 'sp'):
                S._wait(e, ctok)
            xst = R46[:, 0:2048]
            for i in range(5):
                S.dma('sp', R46[:, i * 128:(i + 1) * 128], vecs[i * 128:(i + 1) * 128, :], sem="vecs%d" % i, w=['g%d' % i])
            for i in range(5):
                tr(ps[0][:, 0:128], R46[:, i * 128:(i + 1) * 128], 128, r=['g%d' % i], w=['ps0'])
                cp('dve', vT[:, i * 128:(i + 1) * 128], ps[0][:, 0:128], r=['ps0'], w=['vT'])
            for e in range(3):
                act(sT[:, :, e], vT[:, 576 + e * 16:576 + e * 16 + 16], AF.Silu, r=['vT'], w=['sT'])
            S._wait('pe', ('act', S.cnt['act']))
            S._wait('dve', ('act', S.cnt['act']))
            S._wait('act', ('dve', S.cnt['dve']))
            S._wait('pe', ('dve', S.cnt['dve']))

            chk('vt')
            for l in range(NL):
                for cg in range(24):
                    sA = load_std(w_ada[l], 0, 8, cg * 512, 512)
                    sB = load_std(w_ada[l], 8, 8, cg * 512, 512)
                    b = gbank()
                    for j in range(4):
                        m = cg * 4 + j
                        for kc in range(16):
                            s_, k_ = (sA, kc) if kc < 8 else (sB, kc - 8)
                            mm(ps[b][:, 3 * j:3 * j + 3], pc(s_, k_, 512, j * 128), sT[:, kc, :], start=(kc == 0), stop=(kc == 15),
                               r=['w%d' % s_], w=['ps%d' % b], inc=(kc == 15 and j == 3))
                    for j in range(4):
                        m = cg * 4 + j
                        kind = m // 16
                        ts(mod[:, l, :, m], ps[b][:, 3 * j:3 * j + 3], vT[:, l * 96 + m:l * 96 + m + 1],
                           1.0 if kind in (1, 4) else 0.0, ALU.add, ALU.add, r=['ps%d' % b], w=['mod'])
            S._wait('act', ('dve', S.cnt['dve']))

            chk('mod')
            MSH_A, MSC_A, MG_A, MSH_M, MSC_M, MG_M = 0, 1, 2, 3, 4, 5

            def modv(l, e, kind, c):
                return mod[:, l, e, kind * 16 + c:kind * 16 + c + 1]

            def ssq_accum(bank, src_ap, src_keys, T, first, last):
                i = st['sq'] = (st['sq'] + 1) % 2
                act(sq[:, i, 0:T], src_ap, AF.Square, r=src_keys, w=['sq%d' % i])
                mm(ps[bank][:, 0:T], ones[:], sq[:, i, 0:T], start=first, stop=last, r=['sq%d' % i], w=['ps%d' % bank], inc=True)

            def rstd_from(bank, n, slot, T):
                f = fscr()
                act(fsc[:, f, 0:T], ps[bank][:, 0:T], AF.Ln, r=['ps%d' % bank], w=['f%d' % f], scale=1.0 / n, bias=epsc[:, 0:1])
                act(rst[:, slot, 0:T], fsc[:, f, 0:T], AF.Exp, r=['f%d' % f], w=['rst%d' % slot], scale=-0.5)

            def norm_mod(l, blk, kind_sh, kind_sc):
                T = blk['T']
                b = gbank()
                for kc in range(16):
                    ssq_accum(b, xT[:, kc, 0:T], ['x%d' % kc], T, kc == 0, kc == 15)
                rstd_from(b, D, 0, T)
                for kc in range(16):
                    f = fscr()
                    tt(fsc[:, f, 0:T], xT[:, kc, 0:T], rst[:, 0, 0:T], ALU.mult, r=['x%d' % kc, 'rst0'], w=['f%d' % f])
                    for sg in blk['segs']:
                        act(R1[:, kc, sg.c0:sg.c0 + sg.n], fsc[:, f, sg.c0:sg.c0 + sg.n], AF.Identity, r=['f%d' % f], w=['r%d' % kc],
                            scale=modv(l, sg.e, kind_sc, kc), bias=modv(l, sg.e, kind_sh, kc))

            def attention(l, blk, grp, wk):
                q0, nq = grp['q0'], grp['nq']
                nkb = grp['nkb']
                for h in range(NH):
                    g, hh = h // 4, h % 4
                    wkeys = ['w%d' % wk[g]]
                    hb = h % 2
                    kexp = R46[:, hb * 2048:hb * 2048 + 1024].bitcast(BF16)
                    vexp = R46[:, hb * 2048 + 1024:hb * 2048 + 2048].bitcast(BF16)
                    kk = ['g%d' % (hb * 8 + i) for i in range(4)]
                    vk = ['g%d' % (hb * 8 + 4 + i) for i in range(4)]
                    for kb in range(nkb):
                        b = gbank()
                        for kc in range(4):
                            ap_, keys_ = grp['ckv'](kc, kb * 512, 512)
                            mm(ps[b][:, 0:512], pc(wk[g], kc, 1024, hh * 256), ap_, start=(kc == 0), stop=(kc == 3), r=wkeys + keys_, w=['ps%d' % b])
                        cp(alt(), kexp[:, kb * 512:(kb + 1) * 512], ps[b][:, 0:512], r=['ps%d' % b], w=kk)
                        b = gbank()
                        for t4 in range(4):
                            for kc in range(4):
                                ap_, keys_ = grp['ckv'](kc, kb * 512 + t4 * 128, 128)
                                mm(ps[b][:, t4 * 128:(t4 + 1) * 128], ap_, pc(wk[g], kc, 1024, hh * 256 + 128), start=(kc == 0), stop=(kc == 3),
                                   r=wkeys + keys_, w=['ps%d' % b], inc=(kc == 3 and t4 == 3))
                        cp(alt(), vexp[:, kb * 512:(kb + 1) * 512], ps[b][:, 0:512], r=['ps%d' % b], w=vk)
                    own_n = grp['own_n']
                    if own_n:
                        b = gbank()
                        for kc in range(4):
                            mm(ps[b][:, 0:own_n], pc(wk[g], kc, 1024, hh * 256), ownckv[:, kc, q0:q0 + own_n], start=(kc == 0), stop=(kc == 3), r=wkeys + ['ock'], w=['ps%d' % b])
                        cp(alt(), kown[:, 0:own_n], ps[b][:, 0:own_n], r=['ps%d' % b], w=['kown'])
                        b = gbank()
                        for kc in range(4):
                            mm(ps[b][0:own_n, 0:128], ownckv[:, kc, q0:q0 + own_n], pc(wk[g], kc, 1024, hh * 256 + 128), start=(kc == 0), stop=(kc == 3), r=wkeys + ['ock'], w=['ps%d' % b])
                        cp(alt(), vown[0:own_n, :], ps[b][0:own_n, 0:128], r=['ps%d' % b], w=['vown'])
                    tiles = []
                    for kt in range(nkb * 4):
                        if grp['diag'] and kt >= (nkb - 1) * 4:
                            j = kt - (nkb - 1) * 4
                            tiles.append((kt, 128, q0 + j * 128, nq - j * 128, True))
                        else:
                            tiles.append((kt, 128, q0, nq, False))
                    if own_n:
                        tiles.append((-1, own_n, q0, nq, False))
                    qn_ap = lambda c0, n: R1[:, 2 * h, c0:c0 + n]
                    qp_ap = lambda c0, n: R1[:, 2 * h + 1, c0:c0 + n]
                    qkeys = ['r%d' % (2 * h), 'r%d' % (2 * h + 1)]
                    for ti, (kt, nk, c0, n, dg) in enumerate(tiles):
                        sb_ = 4 + (ti % 2)
                        if kt >= 0:
                            kl = kexp[:, kt * 128:(kt + 1) * 128]
                            krl, krkeys = grp['kr'](kt * 128, 128)
                            vl = vexp[:, kt * 128:(kt + 1) * 128]
                            rk, rv = kk, vk
                        else:
                            kl = kown[:, 0:nk]
                            krl, krkeys = ownkr[:, q0:q0 + nk], ['okr']
                            vl = vown[0:nk, :]
                            rk, rv = ['kown'], ['vown']
                        mm(ps[sb_][0:nk, 0:n], kl, qn_ap(c0, n), start=True, stop=False, r=rk + qkeys, w=['ps%d' % sb_], inc=False)
                        mm(ps[sb_][0:nk, 0:n], krl, qp_ap(c0, n), start=False, stop=True, r=krkeys + qkeys, w=['ps%d' % sb_], inc=True)
                        pi = st['pt'] = (st['pt'] + 1) % 2
                        act(pT[0:nk, pi, 0:n], ps[sb_][0:nk, 0:n], AF.Exp, r=['ps%d' % sb_], w=['pT%d' % pi], scale=ATTN_SCALE)
                        if dg:
                            S.op('dve', lambda: nc.vector.memset(pT[64:128, pi, 0:64], 0.0), r=[], w=['pT%d' % pi])
                        first = (ti == 0)
                        last = (ti == len(tiles) - 1)
                        mm(ps[6][:, c0 - q0:c0 - q0 + n], vl, pT[0:nk, pi, 0:n], start=first, stop=last, r=rv + ['pT%d' % pi], w=['ps6'], inc=False)
                        mm(ps[7][:, c0 - q0:c0 - q0 + n], ones[0:nk, :], pT[0:nk, pi, 0:n], start=first, stop=last, r=['pT%d' % pi], w=['ps7'], inc=True)
                    f = fscr()
                    S.op('dve', lambda: nc.vector.reciprocal(out=fsc[:, f, 0:nq], in_=ps[7][:, 0:nq]), r=['ps7'], w=['f%d' % f])
                    tt(R1[:, 2 * h, q0:q0 + nq], ps[6][:, 0:nq], fsc[:, f, 0:nq], ALU.mult, r=['ps6', 'f%d' % f], w=['r%d' % (2 * h)])

            def layer_block(l, blk):
                T = blk['T']
                segs = blk['segs']
                tiles = blk['tiles']
                tok0 = blk['tok0']
                samp = blk['samp']
                b_idx = blk['b']
                Wl = w_in[l]
                st['cur_l'] = l
                S.dma('sp', kvt[:], kvn[l:l + 1, :].partition_broadcast(128), sem="kvt", w=['kvt'])
                if l == 0:
                    S.dma('sp', csT[:, 0:T], tabF[:, tok0:tok0 + T], sem="csT", w=['csT'])
                norm_mod(l, blk, MSH_A, MSC_A)
                chk('norm1')
                sA = load_std(Wl, 0, 8, 0, 512)
                sB = load_std(Wl, 8, 8, 0, 512)
                for m in range(4):
                    b = gbank()
                    for kc in range(16):
                        s_, k_ = (sA, kc) if kc < 8 else (sB, kc - 8)
                        mm(ps[b][:, 0:T], pc(s_, k_, 512, m * 128), R1[:, kc, 0:T], start=(kc == 0), stop=(kc == 15), r=['w%d' % s_, 'r%d' % kc], w=['ps%d' % b])
                    cp('dve', ql[:, m, 0:T], ps[b][:, 0:T], r=['ps%d' % b], w=['ql%d' % m])
                    ssq_accum(6, ql[:, m, 0:T], ['ql%d' % m], T, m == 0, m == 3)
                chk('qlat')
                sA = load_std(Wl, 0, 8, 512, 512)
                sB = load_std(Wl, 8, 8, 512, 512)
                for (r0, rows) in tiles:
                    b = gbank()
                    for kc in range(16):
                        s_, k_ = (sA, kc) if kc < 8 else (sB, kc - 8)
                        mm(ps[b][0:rows, 0:512], R1[:, kc, r0:r0 + rows], pc(s_, k_, 512, 0, 512), start=(kc == 0), stop=(kc == 15), r=['w%d' % s_, 'r%d' % kc], w=['ps%d' % b])
                    f = fscr()
                    S.op('dve', lambda: nc.vector.memset(sm[:, 0:1], 0.0), w=['sm'])
                    act(fsc[0:rows, f, 0:512], ps[b][0:rows, 0:512], AF.Square, r=['ps%d' % b], w=['f%d' % f, 'sm'], accum=sm[0:rows, 0:1])
                    act(sm[0:rows, 1:2], sm[0:rows, 0:1], AF.Ln, r=['sm'], w=['sm'], scale=1.0 / 512, bias=epsc[0:rows, 0:1])
                    act(sm[0:rows, 2:3], sm[0:rows, 1:2], AF.Exp, r=['sm'], w=['sm'], scale=-0.5)
                    stt(cst[0:rows, :], ps[b][0:rows, 0:512], sm[0:rows, 2:3], kvt[0:rows, :], ALU.mult, ALU.mult, r=['ps%d' % b, 'sm', 'kvt'], w=['cst'])
                    if samp:
                        S.dma('sp', o_ckvs[l, r0:r0 + rows, :], cst[0:rows, :], sem="cst_o", r=['cst'])
                    else:
                        S.dma('sp', o_ckvp[l, tok0 + r0:tok0 + r0 + rows, :], cst[0:rows, :], sem="cst_o", r=['cst'])
                    b2 = gbank()
                    for c in range(4):
                        tr(ps[b2][:, c * 128:c * 128 + rows], cst[0:rows, c * 128:(c + 1) * 128], rows, r=['cst'], w=['ps%d' % b2], inc=(c == 3))
                    for c in range(4):
                        cp(alt(), ownckv[:, c, r0:r0 + rows], ps[b2][:, c * 128:c * 128 + rows], r=['ps%d' % b2], w=['ock'])
                chk('kv')
                def fill_kpe(s):
                    S.dma('pool', slot(s)[:, 0:1024].rearrange("p (k n) -> p k n", k=16), Wl[:, 1024:1088].rearrange("(k p) n -> p k n", p=128), sem="w%d" % s, w=['w%d' % s])
                s = cached_piece(('kpe', l), fill_kpe)
                for (r0, rows) in tiles:
                    b = gbank()
                    S.dma('sp', ttab[0:rows, :], tabT[tok0 + r0:tok0 + r0 + rows, :], sem="ttab", w=['ttab'])
                    for kc in range(16):
                        mm(ps[b][0:rows, 0:64], R1[:, kc, r0:r0 + rows], pc(s, kc, 64, 0, 64), start=(kc == 0), stop=(kc == 15), r=['w%d' % s, 'r%d' % kc], w=['ps%d' % b])
                    tt(krt[0:rows, 0, 0:64], ps[b][0:rows, 0:64], ttab[0:rows, 0:64], ALU.mult, r=['ps%d' % b, 'ttab'], w=['krtA'])
                    tt(krt[0:rows, 1, 0:32], ps[b][0:rows, 32:64], ttab[0:rows, 64:96], ALU.mult, r=['ps%d' % b, 'ttab'], w=['krtB'])
                    tt(krt[0:rows, 1, 32:64], ps[b][0:rows, 0:32], ttab[0:rows, 96:128], ALU.mult, r=['ps%d' % b, 'ttab', 'krtB'], w=['krtB'])
                    tt(krt[0:rows, 2, 0:64], krt[0:rows, 0, 0:64], krt[0:rows, 1, 0:64], ALU.add, r=['krtA', 'krtB'], w=['krtC'])
                    tt(krt[0:rows, 2, 64:128], krt[0:rows, 0, 0:64], krt[0:rows, 1, 0:64], ALU.add, r=['krtA', 'krtB', 'krtC'], w=['krtC'])
                    if samp:
                        S.dma('sp', o_krs[l, r0:r0 + rows, :], krt[0:rows, 2, 0:64], sem="krt_o", r=['krtC'])
                    else:
                        S.dma('sp', o_krp[l, tok0 + r0:tok0 + r0 + rows, :], krt[0:rows, 2, 0:64], sem="krt_o", r=['krtC'])
                    b2 = gbank()
                    tr(ps[b2][:, 0:rows], krt[0:rows, 2, :], rows, r=['krtC'], w=['ps%d' % b2])
                    cp(alt(), ownkr[:, r0:r0 + rows], ps[b2][:, 0:rows], r=['ps%d' % b2], w=['okr'])
                chk('kpe')
                gcs = R46[:, 0:1024].bitcast(BF16)
                CUW = 516
                cu = R46[:, 1024:1024 + 4 * CUW]
                cukeys = ['g%d' % i for i in range(4, 13)]
                for half in range(2):
                    sA = load_std(Wl, 0, 8, 2112 + half * 512, 512)
                    sB = load_std(Wl, 8, 8, 2112 + half * 512, 512)
                    for j in range(4):
                        b = gbank()
                        for kc in range(16):
                            s_, k_ = (sA, kc) if kc < 8 else (sB, kc - 8)
                            mm(ps[b][:, 0:T], pc(s_, k_, 512, j * 128), R1[:, kc, 0:T], start=(kc == 0), stop=(kc == 15), r=['w%d' % s_, 'r%d' % kc], w=['ps%d' % b])
                        cp('act', gcs[:, j * 512:j * 512 + T], ps[b][:, 0:T], r=['ps%d' % b], w=['g%d' % j])
                    sA = load_std(Wl, 0, 8, 3136 + half * 512, 512)
                    sB = load_std(Wl, 8, 8, 3136 + half * 512, 512)
                    for j in range(4):
                        c = half * 4 + j
                        b = gbank()
                        for kc in range(16):
                            s_, k_ = (sA, kc) if kc < 8 else (sB, kc - 8)
                            mm(ps[b][:, 0:T], pc(s_, k_, 512, j * 128), R1[:, kc, 0:T], start=(kc == 0), stop=(kc == 15), r=['w%d' % s_, 'r%d' % kc], w=['ps%d' % b])
                        for si, sg in enumerate(segs):
                            so = j * CUW + si * (2 + sg.n)
                            if samp:
                                cp('dve', cu[:, so:so + 2], blk['halo'][si][:, :, c], r=['halo%d' % si], w=cukeys)
                            else:
                                cp('dve', cu[:, so:so + 2], hal[:, l, c, :], r=['hal'], w=cukeys)
                            tt(cu[:, so + 2:so + 2 + sg.n], ps[b][:, sg.c0:sg.c0 + sg.n], gcs[:, j * 512 + sg.c0:j * 512 + sg.c0 + sg.n], ALU.mult,
                               r=['ps%d' % b, 'g%d' % j], w=cukeys)
                            if samp or b_idx == 3:
                                cp('dve', cs2[:, :, c] if not samp else blk['cs2'][si][:, :, c], cu[:, so + sg.n:so + sg.n + 2], r=cukeys, w=['cs2_%d' % si])
                            if not samp:
                                cp('dve', hal[:, l, c, :], cu[:, so + sg.n:so + sg.n + 2], r=cukeys, w=['hal'])
                    sA = load_std(Wl, 0, 8, 1088 + half * 512, 512)
                    sB = load_std(Wl, 8, 8, 1088 + half * 512, 512)
                    for j in range(4):
                        c = half * 4 + j
                        b = gbank()
                        for kc in range(16):
                            s_, k_ = (sA, kc) if kc < 8 else (sB, kc - 8)
                            mm(ps[b][:, 0:T], pc(s_, k_, 512, j * 128), R1[:, kc, 0:T], start=(kc == 0), stop=(kc == 15), r=['w%d' % s_, 'r%d' % kc], w=['ps%d' % b])
                        f = fscr()
                        for si, sg in enumerate(segs):
                            so = j * CUW + si * (2 + sg.n)
                            wv = lambda k: vT[:, 464 + l * 24 + k * 8 + c:464 + l * 24 + k * 8 + c + 1]
                            y = fsc[:, f, sg.c0:sg.c0 + sg.n]
                            ts(y, cu[:, so:so + sg.n], wv(0), None, ALU.mult, None, r=cukeys, w=['f%d' % f])
                            stt(y, cu[:, so + 1:so + 1 + sg.n], wv(1), y, ALU.mult, ALU.add, r=cukeys + ['f%d' % f], w=['f%d' % f])
                            stt(y, cu[:, so + 2:so + 2 + sg.n], wv(2), y, ALU.mult, ALU.add, r=cukeys + ['f%d' % f], w=['f%d' % f])
                            tt(convn[:, c, sg.c0:sg.c0 + sg.n], ps[b][:, sg.c0:sg.c0 + sg.n], y, ALU.mult, r=['ps%d' % b, 'f%d' % f], w=['cn%d' % c])
                        ssq_accum(7, convn[:, c, 0:T], ['cn%d' % c], T, c == 0, c == 7)
                if samp or b_idx == 3:
                    for si, sg in enumerate(segs):
                        src = blk['cs2'][si] if samp else cs2
                        for t in range(2):
                            dst = (o_convs[l, si, t] if samp else o_convp[l, t]).rearrange("(c p) -> p c", p=128)
                            S.dma('sp', dst, src[:, t, :], sem="cs2_o%d" % si, r=['cs2_%d' % si], nonc=True)
                rstd_from(7, 1024, 1, T)
                for c in range(8):
                    stt(convn[:, c, 0:T], convn[:, c, 0:T], vT[:, 432 + l * 8 + c:432 + l * 8 + c + 1], rst[:, 1, 0:T], ALU.mult, ALU.mult,
                        r=['cn%d' % c, 'rst1'], w=['cn%d' % c])
                chk('conv')
                rstd_from(6, 512, 0, T)
                for m in range(4):
                    stt(ql[:, m, 0:T], ql[:, m, 0:T], vT[:, 384 + l * 4 + m:384 + l * 4 + m + 1], rst[:, 0, 0:T], ALU.mult, ALU.mult,
                        r=['ql%d' % m, 'rst0'], w=['ql%d' % m])
                for g in range(2):
                    def fill_q2(s, g=g):
                        dst = slot(s).rearrange("p (k h c) -> p k h c", k=4, h=4)
                        srcv = w_uq[l].rearrange("(k p) (h c) -> p k h c", p=128, c=192)
                        for kc in range(4):
                            S.dma('pool', dst[:, kc, :, 0:192], srcv[:, kc, 4 * g:4 * g + 4, :], sem="w%d" % s, w=['w%d' % s])
                            S.dma('pool', dst[:, kc, :, 192:224], srcv[:, kc, 4 * g:4 * g + 4, 160:192], sem="w%d" % s, w=['w%d' % s])
                            S.dma('pool', dst[:, kc, :, 224:256], srcv[:, kc, 4 * g:4 * g + 4, 128:160], sem="w%d" % s, w=['w%d' % s])
                    s = cached_piece(('q2', l, g), fill_q2)
                    for hh in range(4):
                        h = 4 * g + hh
                        for part in range(2):
                            b = gbank()
                            for kc in range(4):
                                mm(ps[b][:, 0:T], slot(s)[:, kc * 1024 + hh * 256 + part * 128:kc * 1024 + hh * 256 + part * 128 + 128], ql[:, kc, 0:T],
                                   start=(kc == 0), stop=(kc == 3), r=['w%d' % s, 'ql%d' % kc], w=['ps%d' % b])
                            if part == 0:
                                cp('act', R1[:, 2 * h, 0:T], ps[b][:, 0:T], r=['ps%d' % b], w=['r%d' % (2 * h)])
                            else:
                                tt(R1[:, 2 * h + 1, 0:T], ps[b][:, 0:T], csT[:, 0:T], ALU.mult, r=['ps%d' % b, 'csT'], w=['r%d' % (2 * h + 1)])
                chk('q')
                if not samp:
                    wk = [load_std(w_ukv[l], 0, 4, g * 1024, 1024) for g in range(2)]

                    def ckv_acc(kc, t0, n):
                        if t0 >= b_idx * 512:
                            return ownckv[:, kc, t0 - b_idx * 512:t0 - b_idx * 512 + n], ['ock']
                        o = (l * 4 + kc) * 1536 + t0
                        return cache[:, o:o + n], ['cache']

                    def kr_acc(t0, n):
                        if t0 >= b_idx * 512:
                            return ownkr[:, t0 - b_idx * 512:t0 - b_idx * 512 + n], ['okr']
                        return krc[:, l * 1536 + t0:l * 1536 + t0 + n], ['krc']
                    attention(l, blk, dict(q0=0, nq=512, nkb=b_idx + 1, ckv=ckv_acc, kr=kr_acc, own_n=0, diag=True), wk)
                    if b_idx < 3:
                        for kc in range(4):
                            o = (l * 4 + kc) * 1536 + b_idx * 512
                            cp(alt(), cache[:, o:o + 512], ownckv[:, kc, :], r=['ock'], w=['cache'])
                        cp(alt(), krc[:, l * 1536 + b_idx * 512:l * 1536 + b_idx * 512 + 512], ownkr[:, :], r=['okr'], w=['krc'])
                else:
                    for si, sg in enumerate(segs):
                        for q4 in range(4):
                            stg = R46[:, (q4 % 2) * 2048:(q4 % 2) * 2048 + 2048]
                            sk = ['g%d' % ((q4 % 2) * 8 + i) for i in range(8)]
                            S.dma('sp', stg.rearrange("p (t n) -> p t n", t=4), pkv[l, si, q4 * 512:(q4 + 1) * 512, :].rearrange("(t p) n -> p t n", p=128),
                                  sem="pst%d" % (q4 % 2), w=sk)
                            for t4 in range(4):
                                b = gbank()
                                for c in range(4):
                                    tr(ps[b][:, c * 128:(c + 1) * 128], stg[:, t4 * 512 + c * 128:t4 * 512 + (c + 1) * 128], 128, r=sk, w=['ps%d' % b], inc=(c == 3))
                                for c in range(4):
                                    o = c * 2048 + q4 * 512 + t4 * 128
                                    cp(alt(), cache[:, o:o + 128], ps[b][:, c * 128:(c + 1) * 128], r=['ps%d' % b], w=['cache'])
                        for q4 in range(4):
                            stg = R46[:, (q4 % 2) * 2048:(q4 % 2) * 2048 + 512]
                            sk = ['g%d' % ((q4 % 2) * 8 + i) for i in range(2)]
                            for dup in range(2):
                                S.dma('sp', stg.rearrange("p (t n) -> p t n", t=4)[:, :, dup * 64:(dup + 1) * 64],
                                      pkr[l, si, q4 * 512:(q4 + 1) * 512, :].rearrange("(t p) n -> p t n", p=128), sem="pst%d" % (q4 % 2), w=sk)
                            b = gbank()
                            for t4 in range(4):
                                tr(ps[b][:, t4 * 128:(t4 + 1) * 128], stg[:, t4 * 128:(t4 + 1) * 128], 128, r=sk, w=['ps%d' % b], inc=(t4 == 3))
                            cp(alt(), krc[:, q4 * 512:(q4 + 1) * 512], ps[b][:, 0:512], r=['ps%d' % b], w=['krc'])
                        wk = [load_std(w_ukv[l], 0, 4, g * 1024, 1024) for g in range(2)]

                        def ckv_acc(kc, t0, n):
                            return cache[:, kc * 2048 + t0:kc * 2048 + t0 + n], ['cache']

                        def kr_acc(t0, n):
                            return krc[:, t0:t0 + n], ['krc']
                        attention(l, blk, dict(q0=sg.c0, nq=sg.n, nkb=4, ckv=ckv_acc, kr=kr_acc, own_n=sg.n, diag=False), wk)
                chk('attn')
                b = gbank()
                for h in range(NH):
                    ssq_accum(b, R1[:, 2 * h, 0:T], ['r%d' % (2 * h)], T, h == 0, h == NH - 1)
                rstd_from(b, 1024, 0, T)
                for h in range(NH):
                    stt(R1[:, 2 * h, 0:T], R1[:, 2 * h, 0:T], vT[:, 400 + l * 8 + h:400 + l * 8 + h + 1], rst[:, 0, 0:T], ALU.mult, ALU.mult,
                        r=['r%d' % (2 * h), 'rst0'], w=['r%d' % (2 * h)])
                for cg in range(4):
                    sA = load_std(w_out[l], 0, 8, cg * 512, 512)
                    sB = load_std(w_out[l], 8, 8, cg * 512, 512)
                    for j in range(4):
                        m = cg * 4 + j
                        b = gbank()
                        for kc in range(16):
                            if kc < 8:
                                mm(ps[b][:, 0:T], pc(sA, kc, 512, j * 128), R1[:, 2 * kc, 0:T], start=(kc == 0), stop=False, r=['w%d' % sA, 'r%d' % (2 * kc)], w=['ps%d' % b])
                            else:
                                mm(ps[b][:, 0:T], pc(sB, kc - 8, 512, j * 128), convn[:, kc - 8, 0:T], start=False, stop=(kc == 15), r=['w%d' % sB, 'cn%d' % (kc - 8)], w=['ps%d' % b])
                        for sg in segs:
                            stt(xT[:, m, sg.c0:sg.c0 + sg.n], ps[b][:, sg.c0:sg.c0 + sg.n], modv(l, sg.e, MG_A, m), xT[:, m, sg.c0:sg.c0 + sg.n], ALU.mult, ALU.add,
                                r=['ps%d' % b, 'x%d' % m], w=['x%d' % m])
                chk('wout')
                norm_mod(l, blk, MSH_M, MSC_M)
                for g in range(8):
                    ub = (g % 2) * 2048
                    up = R46[:, ub:ub + 2048].bitcast(BF16)
                    uk = ['g%d' % ((g % 2) * 8 + i) for i in range(8)]
                    for half in range(2):
                        sA = load_std(w_up[l], 0, 8, g * 1024 + half * 512, 512)
                        sB = load_std(w_up[l], 8, 8, g * 1024 + half * 512, 512)
                        for j in range(4):
                            b = gbank()
                            for kc in range(16):
                                s_, k_ = (sA, kc) if kc < 8 else (sB, kc - 8)
                                mm(ps[b][:, 0:T], pc(s_, k_, 512, j * 128), R1[:, kc, 0:T], start=(kc == 0), stop=(kc == 15), r=['w%d' % s_, 'r%d' % kc], w=['ps%d' % b])
                            act(rtmp[:, 0:T], ps[b][:, 0:T], AF.Relu, r=['ps%d' % b], w=['rtmp'])
                            jj = half * 4 + j
                            tt(up[:, jj * 512:jj * 512 + T], rtmp[:, 0:T], rtmp[:, 0:T], ALU.mult, r=['rtmp'], w=uk)
                    for cg in range(4):
                        s = load_std(w_down[l], g * 8, 8, cg * 512, 512)
                        for j in range(4):
                            m = cg * 4 + j
                            b = gbank()
                            for kc in range(8):
                                mm(ps[b][:, 0:T], pc(s, kc, 512, j * 128), up[:, kc * 512:kc * 512 + T], start=(kc == 0), stop=(kc == 7), r=['w%d' % s] + uk, w=['ps%d' % b])
                            for sg in segs:
                                stt(xT[:, m, sg.c0:sg.c0 + sg.n], ps[b][:, sg.c0:sg.c0 + sg.n], modv(l, sg.e, MG_M, m), xT[:, m, sg.c0:sg.c0 + sg.n], ALU.mult, ALU.add,
                                    r=['ps%d' % b, 'x%d' % m], w=['x%d' % m])

            def run_block(blk, xsrc, ydst):
                T = blk['T']
                tiles = blk['tiles']
                gk = ['g%d' % i for i in range(16)]
                for ti, (r0, rows) in enumerate(tiles):
                    stg = R46[:, (ti % 2) * 2048:(ti % 2) * 2048 + 2048]
                    sk = ['g%d' % ((ti % 2) * 8 + i) for i in range(8)]
                    S.dma('sp', stg[0:rows, :], xsrc[r0:r0 + rows, :], sem="xst%d" % (ti % 2), w=sk)
                    chk('xl1')
                    for c4 in range(4):
                        b = gbank()
                        if _CACHE.get('xv', 0) == 3:
                            b = 1
                        for c in range(4):
                            cc = c4 * 4 + c
                            tr(ps[b][:, c * 128:c * 128 + rows], stg[0:rows, cc * 128:(cc + 1) * 128], rows, r=sk, w=['ps%d' % b], inc=(c == 3))
                        chk('xl2')
                        chk('xt%d_%d' % (ti, c4))
                        xv = _CACHE.get('xv', 7)
                        for c in range(4):
                            cc = c4 * 4 + c
                            if xv == 1 and c4 == 1 and c > 0:
                                continue
                            if xv == 2:
                                cc = c
                            if xv == 5 and c4 == 1 and c != 1:
                                continue
                            en = alt()
                            if (xv == 4 and c4 == 1) or xv == 7:
                                en = 'act'
                            if xv == 5 and c4 == 1:
                                en = 'dve'
                            if xv == 6:
                                cp(en, fsc[:, 0, c * 128:c * 128 + rows], ps[b][:, c * 128:c * 128 + rows], r=['ps%d' % b], w=['x%d' % cc])
                            else:
                                cp(en, xT[:, cc, r0:r0 + rows], ps[b][:, c * 128:c * 128 + rows], r=['ps%d' % b], w=['x%d' % cc])
                        if _CACHE.get('serialtr'):
                            S._wait('pe', ('act', S.cnt['act']))
                            S._wait('pe', ('dve', S.cnt['dve']))
                        chk('xl3')
                        chk('xg%d_%d' % (ti, c4))
                chk('xload')
                for l in range(NL):
                    layer_block(l, blk)
                chk('layers')
                b = gbank()
                for kc in range(16):
                    ssq_accum(b, xT[:, kc, 0:T], ['x%d' % kc], T, kc == 0, kc == 15)
                rstd_from(b, D, 0, T)
                for kc in range(16):
                    stt(xT[:, kc, 0:T], xT[:, kc, 0:T], vT[:, 560 + kc:561 + kc], rst[:, 0, 0:T], ALU.mult, ALU.mult, r=['x%d' % kc, 'rst0'], w=['x%d' % kc])
                for ti, (r0, rows) in enumerate(tiles):
                    stg = R46[:, (ti % 2) * 2048:(ti % 2) * 2048 + 2048]
                    sk = ['g%d' % ((ti % 2) * 8 + i) for i in range(8)]
                    for c4 in range(4):
                        b = gbank()
                        for c in range(4):
                            cc = c4 * 4 + c
                            tr(ps[b][0:rows, c * 128:(c + 1) * 128], xT[:, cc, r0:r0 + rows], 128, r=['x%d' % cc], w=['ps%d' % b], inc=(c == 3))
                        cp(alt(), stg[0:rows, c4 * 512:(c4 + 1) * 512], ps[b][0:rows, 0:512], r=['ps%d' % b], w=sk)
                    S.dma('sp', ydst[r0:r0 + rows, :], stg[0:rows, :], sem="yst%d" % (ti % 2), r=sk)

            for bi in range(NBLK):
                blk = dict(T=512, segs=[Seg(0, 512, 0)], tiles=[(i * 128, 128) for i in range(4)], tok0=bi * 512, samp=False, b=bi)
                st['pass'] = bi
                run_block(blk, xp[bi * 512:(bi + 1) * 512, :], o_yp[bi * 512:(bi + 1) * 512, :])
                if bi == 0 and _CACHE.get('wcache', 1):
                    for s_ in range(4):
                        if ("wst%d" % s_) in S.cnt:
                            S._wait('pool', ("wst%d" % s_, S.cnt["wst%d" % s_]))
            halo_t = [sbt("halo%d" % i, [128, 2, 8], F32) for i in range(2)]
            cs2_t = [sbt("cs2s%d" % i, [128, 2, 8], F32) for i in range(2)]
            sblk = dict(T=32, segs=[Seg(0, 16, 1), Seg(16, 16, 2)], tiles=[(0, 32)], tok0=2048, samp=True, b=4, halo=halo_t, cs2=cs2_t)
            orig_layer_block = layer_block

            def layer_block_s(l, blk):
                for si in range(2):
                    for t in range(2):
                        S.dma('sp', halo_t[si][:, t, :], sconv[l, si, t].rearrange("(c p) -> p c", p=128), sem="halo%d" % si, w=['halo%d' % si], nonc=True)
                orig_layer_block(l, blk)
            layer_block = layer_block_s
            if SAMPLE:
                if _CACHE.get('nslots_s', 8) > 4:
                    for en_ in ('pe', 'act', 'dve'):
                        if S.cnt[en_] > 0:
                            S._wait('pool', (en_, S.cnt[en_]))
                    st['nslots'] = _CACHE.get('nslots_s', 8)
                st['pass'] = 4
                run_block(sblk, xs, o_ys)
        except _Stop:
            pass
        S.finish('sp')
        _CACHE['counts'] = dict(S.cnt)
        _CACHE['nwaits'] = S.nwaits
    return nc


_CACHE = {}


def _tables():
    half = 32
    inv = (10000.0 ** (-np.arange(half, dtype=np.float32) / half)).astype(np.float32)
    pos = np.concatenate([np.arange(2048), 2048 + np.arange(16), 2048 + np.arange(16)]).astype(np.float32)
    ang = pos[:, None] * inv[None, :]
    cos, sin = np.cos(ang).astype(np.float32), np.sin(ang).astype(np.float32)
    tabT = np.concatenate([cos, cos, -sin, sin], axis=1).astype(np.float32)
    tabF = np.ascontiguousarray(tabT.T)
    return tabF, np.ascontiguousarray(tabT)


def kernel(x_prompt, x_sample, c_prompt, c_sample, cache_kv_latent, cache_k_rope, state_conv,
           w_ada, b_ada, w_in, q_norm, w_uq, kv_norm, w_ukv, conv_w,
           attn_out_norm, conv_out_norm, w_out, w_up, w_down, final_norm):
    f = lambda a: np.ascontiguousarray(np.asarray(a, dtype=np.float32))
    x_prompt, x_sample, c_prompt, c_sample = f(x_prompt), f(x_sample), f(c_prompt), f(c_sample)
    cache_kv_latent, cache_k_rope, state_conv = f(cache_kv_latent), f(cache_k_rope), f(state_conv)
    w_ada, b_ada, w_in, q_norm, w_uq, kv_norm, w_ukv = f(w_ada), f(b_ada), f(w_in), f(q_norm), f(w_uq), f(kv_norm), f(w_ukv)
    conv_w, attn_out_norm, conv_out_norm, w_out, w_up, w_down, final_norm = f(conv_w), f(attn_out_norm), f(conv_out_norm), f(w_out), f(w_up), f(w_down), f(final_norm)
    if 'nc' not in _CACHE:
        _CACHE['nc'] = build_program()
    nc = _CACHE['nc']
    tabF, tabT = _tables()
    common = np.concatenate([b_ada.reshape(384, 128), q_norm.reshape(16, 128), attn_out_norm.reshape(32, 128),
                             conv_out_norm.reshape(32, 128), conv_w.reshape(96, 128), final_norm.reshape(16, 128)], axis=0)
    in_maps = []
    for c in range(8):
        vecs = np.zeros((640, 128), np.float32)
        vecs[0:576] = common
        vecs[576:592] = c_prompt[c].reshape(16, 128)
        vecs[592:608] = c_sample[2 * c].reshape(16, 128)
        vecs[608:624] = c_sample[2 * c + 1].reshape(16, 128)
        in_maps.append(dict(
            xp=x_prompt[c], xs=np.ascontiguousarray(x_sample[2 * c:2 * c + 2].reshape(32, D)), vecs=vecs, kvn=kv_norm,
            pkv=np.ascontiguousarray(cache_kv_latent[:, 2 * c:2 * c + 2]), pkr=np.ascontiguousarray(cache_k_rope[:, 2 * c:2 * c + 2]),
            sconv=np.ascontiguousarray(state_conv[:, 2 * c:2 * c + 2]), tabF=tabF, tabT=tabT,
            w_ada=w_ada, w_in=w_in, w_uq=w_uq, w_ukv=w_ukv, w_out=w_out, w_up=w_up, w_down=w_down))
    res = run_bass_kernel_spmd(nc, in_maps, core_ids=list(range(8)))
    R = res.results
    y_p = np.stack([R[c]["o_yp"] for c in range(8)], axis=0)
    y_s = np.concatenate([R[c]["o_ys"].reshape(2, 16, D) for c in range(8)], axis=0)
    ckv_p = np.stack([R[c]["o_ckvp"] for c in range(8)], axis=1)
    kr_p = np.stack([R[c]["o_krp"] for c in range(8)], axis=1)
    conv_p = np.stack([R[c]["o_convp"] for c in range(8)], axis=1)
    ckv_s = np.concatenate([R[c]["o_ckvs"].reshape(L, 2, 16, 512) for c in range(8)], axis=1)
    kr_s = np.concatenate([R[c]["o_krs"].reshape(L, 2, 16, 64) for c in range(8)], axis=1)
    conv_s = np.concatenate([R[c]["o_convs"] for c in range(8)], axis=1)
    return (y_p.astype(np.float32), y_s.astype(np.float32), ckv_p.astype(np.float32), kr_p.astype(np.float32),
            conv_p.astype(np.float32), ckv_s.astype(np.float32), kr_s.astype(np.float32), conv_s.astype(np.float32))
```

```python
import contextlib
import numpy as np
import concourse.bass as bass
import concourse.mybir as mybir
from concourse.bass_utils import run_bass_kernel_spmd

F32, BF16 = mybir.dt.float32, mybir.dt.bfloat16
AF = mybir.ActivationFunctionType
ALU = mybir.AluOpType

L = 4
D = 2048
SEQ = 2048
NH = 8
TB = 512
EPS = 1e-6
ATTN_SCALE = 192.0 ** -0.5
INC = 4160
DFF = 8192
NTAB = 2048 + 32


class Sched:
    def __init__(self, nc, es):
        self.nc = nc
        self.es = es
        self.eng = {'pe': nc.tensor, 'act': nc.scalar, 'dve': nc.vector, 'pool': nc.gpsimd, 'sp': nc.sync}
        self.h = {}
        self.cnt = {}
        for e in self.eng:
            self.h[e] = es.enter_context(nc.semaphore("s_" + e))
            self.cnt[e] = 0
        self.seen = {e: {} for e in self.eng}
        self.snap = {}
        self.bufs = {}
        self.nwaits = 0

    def _wait(self, e, tok):
        name, val = tok
        if name == 'pe' and e == 'pe':
            return
        if self.seen[e].get(name, 0) >= val:
            return
        self.eng[e].wait_ge(self.h[name], val)
        self.nwaits += 1
        self.seen[e][name] = val
        sn = self.snap.get(tok)
        if sn:
            se = self.seen[e]
            for k, v in sn.items():
                if se.get(k, 0) < v:
                    se[k] = v

    def _deps(self, e, r, w):
        deps = set()
        for k in r:
            b = self.bufs.get(k)
            if b and b[0]:
                deps.add(b[0])
        for k in w:
            b = self.bufs.get(k)
            if b:
                if b[0]:
                    deps.add(b[0])
                for n, v in b[1].items():
                    deps.add((n, v))
        for t in sorted(deps):
            self._wait(e, t)

    def _record(self, tok, r, w):
        for k in r:
            b = self.bufs.setdefault(k, [None, {}])
            if b[1].get(tok[0], 0) < tok[1]:
                b[1][tok[0]] = tok[1]
        for k in w:
            self.bufs[k] = [tok, {}]

    def op(self, e, fn, r=(), w=(), inc=True):
        self._deps(e, r, w)
        if e in ('dve', 'act') and self.cnt[e] > 0 and _CACHE.get('selfser', 1):
            self._wait(e, (e, self.cnt[e]))
        ins = fn()
        if inc:
            self.cnt[e] += 1
            ins.then_inc(self.h[e], 1)
            tok = (e, self.cnt[e])
            self.snap[tok] = dict(self.seen[e])
        else:
            tok = (e, self.cnt[e] + 1)
        self._record(tok, r, w)
        return ins

    def dma(self, q, out, in_, sem, r=(), w=(), nonc=False):
        for k in w:
            b = self.bufs.get(k)
            if b and b[0] and b[0][0] == sem and not b[1]:
                b[0] = None
        self._deps(q, r, w)
        if sem not in self.h:
            self.h[sem] = self.es.enter_context(self.nc.semaphore("d_" + sem))
            self.cnt[sem] = 0
        if nonc:
            with self.nc.allow_non_contiguous_dma(reason="small strided vector"):
                ins = self.eng[q].dma_start(out=out, in_=in_)
        else:
            ins = self.eng[q].dma_start(out=out, in_=in_)
        self.cnt[sem] += 16
        ins.then_inc(self.h[sem], 16)
        tok = (sem, self.cnt[sem])
        self.snap[tok] = dict(self.seen[q])
        self._record(tok, r, w)

    def finish(self, e='sp'):
        for name, h in self.h.items():
            if name in self.eng:
                continue
            self._wait(e, (name, self.cnt[name]))
        for name in ('act', 'dve', 'pe'):
            if self.cnt[name] > 0:
                self._wait(e, (name, self.cnt[name]))


class _Stop(Exception):
    pass


class Seg:
    def __init__(self, c0, n, e):
        self.c0, self.n, self.e = c0, n, e


def build_program(NL=L, NBLK=4, SAMPLE=True):
    nc = bass.Bass("TRN2", target_bir_lowering=False)
    dt_in = lambda n, s: nc.dram_tensor(n, s, F32, kind="ExternalInput").ap()
    dt_out = lambda n, s: nc.dram_tensor(n, s, F32, kind="ExternalOutput").ap()
    xp = dt_in("xp", [SEQ, D])
    xs = dt_in("xs", [32, D])
    vecs = dt_in("vecs", [640, 128])
    kvn = dt_in("kvn", [NL, 512])
    pkv = dt_in("pkv", [NL, 2, 2048, 512])
    pkr = dt_in("pkr", [NL, 2, 2048, 64])
    sconv = dt_in("sconv", [NL, 2, 2, 1024])
    tabF = dt_in("tabF", [128, NTAB])
    tabT = dt_in("tabT", [NTAB, 128])
    w_ada = dt_in("w_ada", [NL, D, 6 * D])
    w_in = dt_in("w_in", [NL, D, INC])
    w_uq = dt_in("w_uq", [NL, 512, 1536])
    w_ukv = dt_in("w_ukv", [NL, 512, 2048])
    w_out = dt_in("w_out", [NL, D, D])
    w_up = dt_in("w_up", [NL, D, DFF])
    w_down = dt_in("w_down", [NL, DFF, D])
    wscr = [nc.dram_tensor("wscr%d" % i, [100, 128, 4096], BF16, kind="Internal").ap() for i in range(NL)]
    o_yp = dt_out("o_yp", [SEQ, D])
    o_ys = dt_out("o_ys", [32, D])
    o_ckvp = dt_out("o_ckvp", [L, SEQ, 512])
    o_krp = dt_out("o_krp", [L, SEQ, 64])
    o_convp = dt_out("o_convp", [L, 2, 1024])
    o_ckvs = dt_out("o_ckvs", [L, 32, 512])
    o_krs = dt_out("o_krs", [L, 32, 64])
    o_convs = dt_out("o_convs", [L, 2, 2, 1024])

    with contextlib.ExitStack() as es:
        sbt = lambda n, s, d: es.enter_context(nc.sbuf_tensor(n, s, d))
        S = Sched(nc, es)
        cache = sbt("cache", [128, L * 4 * 1536], BF16)
        ownckv = sbt("ownckv", [128, 4, 512], BF16)
        krc = sbt("krc", [128, L * 1536], BF16)
        ownkr = sbt("ownkr", [128, 512], BF16)
        xT = sbt("xT", [128, 16, 512], F32)
        R1 = sbt("R1", [128, 16, 512], BF16)
        convn = sbt("convn", [128, 8, 512], BF16)
        R46 = sbt("R46", [128, 4096], F32)
        ring = sbt("ring", [128, 4, 4096], BF16)
        sq = sbt("sq", [128, 2, 512], BF16)
        rst = sbt("rst", [128, 2, 512], F32)
        fsc = sbt("fsc", [128, 3, 512], F32)
        ql = sbt("ql", [128, 4, 512], BF16)
        pT = sbt("pT", [128, 2, 512], BF16)
        cst = sbt("cst", [128, 512], F32)
        krt = sbt("krt", [128, 3, 128], F32)
        ttab = sbt("ttab", [128, 128], F32)
        csT = sbt("csT", [128, 512], F32)
        kvt = sbt("kvt", [128, 512], F32)
        ident = sbt("ident", [128, 128], F32)
        ones = sbt("ones", [128, 128], BF16)
        onesr = sbt("onesr", [1, 4], BF16)
        epsc = sbt("epsc", [128, 1], F32)
        vT = sbt("vT", [128, 640], F32)
        sT = sbt("sT", [128, 16, 3], BF16)
        mod = sbt("mod", [128, L, 3, 96], F32)
        hal = sbt("hal", [128, L, 8, 2], F32)
        cs2 = sbt("cs2", [128, 2, 8], F32)
        sm = sbt("sm", [128, 8], F32)
        kown = sbt("kown", [128, 32], BF16)
        vown = sbt("vown", [16, 128], BF16)
        rtmp = sbt("rtmp", [128, 512], BF16)
        ps = [es.enter_context(nc.psum_tensor("ps%d" % i, [128, 512], F32)) for i in range(8)]

        st = {'gb': 0, 'piece': 0, 'alt': 0, 'sq': 0, 'f': 0, 'pt': 0, 'nrec': {}}

        def gbank():
            st['gb'] = (st['gb'] + 1) % 4
            return st['gb']

        def alt():
            st['alt'] ^= 1
            return 'act' if st['alt'] else 'dve'

        def mm(out, lhsT, rhs, start, stop, r, w, inc=None):
            if inc is None:
                inc = stop
            S.op('pe', lambda: nc.tensor.matmul(out, lhsT=lhsT, rhs=rhs, start=start, stop=stop), r=r, w=w, inc=inc)

        def tr(out, in_, rows, r, w, inc=True):
            if _CACHE.get('mmtr', 1):
                S.op('pe', lambda: nc.tensor.matmul(out, lhsT=in_, rhs=ident[0:rows, 0:rows], start=True, stop=True), r=r, w=w, inc=inc)
            else:
                S.op('pe', lambda: nc.tensor.transpose(out=out, in_=in_, identity=ident[0:rows, 0:rows]), r=r, w=w, inc=inc)

        def act(out, in_, func, r, w, bias=None, scale=None, accum=None):
            kw = {}
            if bias is not None:
                kw['bias'] = bias
            if scale is not None:
                kw['scale'] = scale
            if accum is not None:
                kw['accum_out'] = accum
            S.op('act', lambda: nc.scalar.activation(out=out, in_=in_, func=func, **kw), r=r, w=w)

        def cp(e, out, in_, r, w):
            if e == 'act':
                S.op('act', lambda: nc.scalar.activation(out=out, in_=in_, func=AF.Copy), r=r, w=w)
            else:
                S.op('dve', lambda: nc.vector.tensor_scalar(out=out, in0=in_, scalar1=1.0, scalar2=None, op0=ALU.mult), r=r, w=w)

        def tt(out, a, b, op, r, w):
            S.op('dve', lambda: nc.vector.tensor_tensor(out=out, in0=a, in1=b, op=op), r=r, w=w)

        def stt(out, in0, scalar, in1, op0, op1, r, w):
            S.op('dve', lambda: nc.vector.scalar_tensor_tensor(out=out, in0=in0, scalar=scalar, in1=in1, op0=op0, op1=op1), r=r, w=w)

        def ts(out, in0, s1, s2, op0, op1, r, w):
            if s2 is None:
                S.op('dve', lambda: nc.vector.tensor_scalar(out=out, in0=in0, scalar1=s1, scalar2=None, op0=op0), r=r, w=w)
            else:
                S.op('dve', lambda: nc.vector.tensor_scalar(out=out, in0=in0, scalar1=s1, scalar2=s2, op0=op0, op1=op1), r=r, w=w)

        def fscr():
            st['f'] = (st['f'] + 1) % 3
            return st['f']

        def new_piece():
            s = st['piece'] % st.get('nslots', 4)
            st['piece'] += 1
            return s

        def slot(s):
            if s < 4:
                return ring[:, s, :]
            return cache[:, 8192 + (s - 4) * 4096:8192 + (s - 3) * 4096]

        recs = {}

        def cached_piece(key, fill):
            s = new_piece()
            if key is None or not _CACHE.get('wcache', 1):
                fill(s)
                return s
            if key in recs:
                l_, i_ = recs[key]
                S.dma('pool', slot(s), wscr[l_][i_], sem="w%d" % s, w=["w%d" % s])
            else:
                assert st.get('pass', 0) == 0, key
                l_ = st['cur_l']
                i_ = st['nrec'].get(l_, 0)
                st['nrec'][l_] = i_ + 1
                assert i_ < 100
                recs[key] = (l_, i_)
                fill(s)
                S.dma('sp', wscr[l_][i_], slot(s), sem="wst%d" % s, r=["w%d" % s])
            return s

        def load_std(W, kc0, nk, c0, ncols):
            def fill(s):
                dst = slot(s)[:, 0:nk * ncols].rearrange("p (k n) -> p k n", k=nk)
                src = W[kc0 * 128:(kc0 + nk) * 128, c0:c0 + ncols].rearrange("(k p) n -> p k n", p=128)
                S.dma('pool', dst, src, sem="w%d" % s, w=["w%d" % s])
            key = None if st.get('cur_l') is None else (W.name, st['cur_l'], kc0, nk, c0, ncols)
            return cached_piece(key, fill)

        def pc(s, k, ncols, m0, mw=128):
            return slot(s)[:, k * ncols + m0:k * ncols + m0 + mw]

        def chk(name):
            if _CACHE.get('stop') == name:
                raise _Stop()

        try:
            S.op('pool', lambda: nc.gpsimd.memset(ident[:], 0.0), w=['ident'])
            S.op('pool', lambda: nc.gpsimd.affine_select(out=ident[:], in_=ident[:], pattern=[[-1, 128]], compare_op=ALU.not_equal, fill=1.0, base=0, channel_multiplier=1), r=['ident'], w=['ident'])
            S.op('pool', lambda: nc.gpsimd.memset(ones[:], 1.0), w=['ones'])
            S.op('pool', lambda: nc.gpsimd.memset(onesr[:], 1.0), w=['onesr'])
            S.op('pool', lambda: nc.gpsimd.memset(epsc[:], EPS), w=['epsc'])
            S.op('pool', lambda: nc.gpsimd.memset(hal[:], 0.0), w=['hal'])
            for _i in range(_CACHE.get('padcopy', 0)):
                S.op('dve', lambda: nc.vector.tensor_copy(out=sm[:, 4:5], in_=sm[:, 5:6]), w=[])
            for _i in range(_CACHE.get('pad', 0)):
                S.op('dve', lambda: nc.vector.memset(sm[:, 4:5], 0.0), w=[])
            ctok = ('pool', S.cnt['pool'])
            for e in ('pe', 'act', 'dve', 'sp'):
                S._wait(e, ctok)
            xst = R46[:, 0:2048]
            for i in range(5):
                S.dma('sp', R46[:, i * 128:(i + 1) * 128], vecs[i * 128:(i + 1) * 128, :], sem="vecs%d" % i, w=['g%d' % i])
            for i in range(5):
                tr(ps[0][:, 0:128], R46[:, i * 128:(i + 1) * 128], 128, r=['g%d' % i], w=['ps0'])
                cp('dve', vT[:, i * 128:(i + 1) * 128], ps[0][:, 0:128], r=['ps0'], w=['vT'])
            for e in range(3):
                act(sT[:, :, e], vT[:, 576 + e * 16:576 + e * 16 + 16], AF.Silu, r=['vT'], w=['sT'])
            S._wait('pe', ('act', S.cnt['act']))
            S._wait('dve', ('act', S.cnt['act']))
            S._wait('act', ('dve', S.cnt['dve']))
            S._wait('pe', ('dve', S.cnt['dve']))

            chk('vt')
            for l in range(NL):
                for cg in range(24):
                    sA = load_std(w_ada[l], 0, 8, cg * 512, 512)
                    sB = load_std(w_ada[l], 8, 8, cg * 512, 512)
                    b = gbank()
                    for j in range(4):
                        m = cg * 4 + j
                        for kc in range(16):
                            s_, k_ = (sA, kc) if kc < 8 else (sB, kc - 8)
                            mm(ps[b][:, 3 * j:3 * j + 3], pc(s_, k_, 512, j * 128), sT[:, kc, :], start=(kc == 0), stop=(kc == 15),
                               r=['w%d' % s_], w=['ps%d' % b], inc=(kc == 15 and j == 3))
                    for j in range(4):
                        m = cg * 4 + j
                        kind = m // 16
                        ts(mod[:, l, :, m], ps[b][:, 3 * j:3 * j + 3], vT[:, l * 96 + m:l * 96 + m + 1],
                           1.0 if kind in (1, 4) else 0.0, ALU.add, ALU.add, r=['ps%d' % b], w=['mod'])
            S._wait('act', ('dve', S.cnt['dve']))

            chk('mod')
            MSH_A, MSC_A, MG_A, MSH_M, MSC_M, MG_M = 0, 1, 2, 3, 4, 5

            def modv(l, e, kind, c):
                return mod[:, l, e, kind * 16 + c:kind * 16 + c + 1]

            def ssq_accum(bank, src_ap, src_keys, T, first, last):
                i = st['sq'] = (st['sq'] + 1) % 2
                act(sq[:, i, 0:T], src_ap, AF.Square, r=src_keys, w=['sq%d' % i])
                mm(ps[bank][:, 0:T], ones[:], sq[:, i, 0:T], start=first, stop=last, r=['sq%d' % i], w=['ps%d' % bank], inc=True)

            def rstd_from(bank, n, slot, T):
                f = fscr()
                act(fsc[:, f, 0:T], ps[bank][:, 0:T], AF.Ln, r=['ps%d' % bank], w=['f%d' % f], scale=1.0 / n, bias=epsc[:, 0:1])
                act(rst[:, slot, 0:T], fsc[:, f, 0:T], AF.Exp, r=['f%d' % f], w=['rst%d' % slot], scale=-0.5)

            def norm_mod(l, blk, kind_sh, kind_sc):
                T = blk['T']
                b = gbank()
                for kc in range(16):
                    ssq_accum(b, xT[:, kc, 0:T], ['x%d' % kc], T, kc == 0, kc == 15)
                rstd_from(b, D, 0, T)
                for kc in range(16):
                    f = fscr()
                    tt(fsc[:, f, 0:T], xT[:, kc, 0:T], rst[:, 0, 0:T], ALU.mult, r=['x%d' % kc, 'rst0'], w=['f%d' % f])
                    for sg in blk['segs']:
                        act(R1[:, kc, sg.c0:sg.c0 + sg.n], fsc[:, f, sg.c0:sg.c0 + sg.n], AF.Identity, r=['f%d' % f], w=['r%d' % kc],
                            scale=modv(l, sg.e, kind_sc, kc), bias=modv(l, sg.e, kind_sh, kc))

            def attention(l, blk, grp, wk):
                q0, nq = grp['q0'], grp['nq']
                nkb = grp['nkb']
                for h in range(NH):
                    g, hh = h // 4, h % 4
                    wkeys = ['w%d' % wk[g]]
                    hb = h % 2
                    kexp = R46[:, hb * 2048:hb * 2048 + 1024].bitcast(BF16)
                    vexp = R46[:, hb * 2048 + 1024:hb * 2048 + 2048].bitcast(BF16)
                    kk = ['g%d' % (hb * 8 + i) for i in range(4)]
                    vk = ['g%d' % (hb * 8 + 4 + i) for i in range(4)]
                    for kb in range(nkb):
                        b = gbank()
                        for kc in range(4):
                            ap_, keys_ = grp['ckv'](kc, kb * 512, 512)
                            mm(ps[b][:, 0:512], pc(wk[g], kc, 1024, hh * 256), ap_, start=(kc == 0), stop=(kc == 3), r=wkeys + keys_, w=['ps%d' % b])
                        cp(alt(), kexp[:, kb * 512:(kb + 1) * 512], ps[b][:, 0:512], r=['ps%d' % b], w=kk)
                        b = gbank()
                        for t4 in range(4):
                            for kc in range(4):
                                ap_, keys_ = grp['ckv'](kc, kb * 512 + t4 * 128, 128)
                                mm(ps[b][:, t4 * 128:(t4 + 1) * 128], ap_, pc(wk[g], kc, 1024, hh * 256 + 128), start=(kc == 0), stop=(kc == 3),
                                   r=wkeys + keys_, w=['ps%d' % b], inc=(kc == 3 and t4 == 3))
                        cp(alt(), vexp[:, kb * 512:(kb + 1) * 512], ps[b][:, 0:512], r=['ps%d' % b], w=vk)
                    own_n = grp['own_n']
                    if own_n:
                        b = gbank()
                        for kc in range(4):
                            mm(ps[b][:, 0:own_n], pc(wk[g], kc, 1024, hh * 256), ownckv[:, kc, q0:q0 + own_n], start=(kc == 0), stop=(kc == 3), r=wkeys + ['ock'], w=['ps%d' % b])
                        cp(alt(), kown[:, 0:own_n], ps[b][:, 0:own_n], r=['ps%d' % b], w=['kown'])
                        b = gbank()
                        for kc in range(4):
                            mm(ps[b][0:own_n, 0:128], ownckv[:, kc, q0:q0 + own_n], pc(wk[g], kc, 1024, hh * 256 + 128), start=(kc == 0), stop=(kc == 3), r=wkeys + ['ock'], w=['ps%d' % b])
                        cp(alt(), vown[0:own_n, :], ps[b][0:own_n, 0:128], r=['ps%d' % b], w=['vown'])
                    tiles = []
                    for kt in range(nkb * 4):
                        if grp['diag'] and kt >= (nkb - 1) * 4:
                            j = kt - (nkb - 1) * 4
                            tiles.append((kt, 128, q0 + j * 128, nq - j * 128, True))
                        else:
                            tiles.append((kt, 128, q0, nq, False))
                    if own_n:
                        tiles.append((-1, own_n, q0, nq, False))
                    qn_ap = lambda c0, n: R1[:, 2 * h, c0:c0 + n]
                    qp_ap = lambda c0, n: R1[:, 2 * h + 1, c0:c0 + n]
                    qkeys = ['r%d' % (2 * h), 'r%d' % (2 * h + 1)]
                    def tile_ops(ti):
                        kt, nk, c0, n, dg = tiles[ti]
                        if kt >= 0:
                            kl = kexp[:, kt * 128:(kt + 1) * 128]
                            krl, krkeys = grp['kr'](kt * 128, 128)
                            vl = vexp[:, kt * 128:(kt + 1) * 128]
                            rk, rv = kk, vk
                        else:
                            kl = kown[:, 0:nk]
                            krl, krkeys = ownkr[:, q0:q0 + nk], ['okr']
                            vl = vown[0:nk, :]
                            rk, rv = ['kown'], ['vown']
                        return kt, nk, c0, n, dg, kl, krl, krkeys, vl, rk, rv

                    def emit_S(ti):
                        kt, nk, c0, n, dg, kl, krl, krkeys, vl, rk, rv = tile_ops(ti)
                        sb_ = 4 + (ti % 2)
                        mm(ps[sb_][0:nk, 0:n], kl, qn_ap(c0, n), start=True, stop=False, r=rk + qkeys, w=['ps%d' % sb_], inc=False)
                        mm(ps[sb_][0:nk, 0:n], krl, qp_ap(c0, n), start=False, stop=True, r=krkeys + qkeys, w=['ps%d' % sb_], inc=True)
                        pi = st['pt'] = (st['pt'] + 1) % 2
                        act(pT[0:nk, pi, 0:n], ps[sb_][0:nk, 0:n], AF.Exp, r=['ps%d' % sb_], w=['pT%d' % pi], scale=ATTN_SCALE)
                        if dg:
                            S.op('dve', lambda: nc.vector.memset(pT[64:128, pi, 0:64], 0.0), r=[], w=['pT%d' % pi])
                        return pi

                    def emit_PV(ti, pi):
                        kt, nk, c0, n, dg, kl, krl, krkeys, vl, rk, rv = tile_ops(ti)
                        first = (ti == 0)
                        last = (ti == len(tiles) - 1)
                        mm(ps[6][:, c0 - q0:c0 - q0 + n], vl, pT[0:nk, pi, 0:n], start=first, stop=last, r=rv + ['pT%d' % pi], w=['ps6'], inc=False)
                        mm(ps[7][:, c0 - q0:c0 - q0 + n], ones[0:nk, :], pT[0:nk, pi, 0:n], start=first, stop=last, r=['pT%d' % pi], w=['ps7'], inc=True)

                    pis = {0: emit_S(0)}
                    for ti in range(len(tiles)):
                        if ti + 1 < len(tiles):
                            pis[ti + 1] = emit_S(ti + 1)
                        emit_PV(ti, pis[ti])
                    f = fscr()
                    S.op('dve', lambda: nc.vector.reciprocal(out=fsc[:, f, 0:nq], in_=ps[7][:, 0:nq]), r=['ps7'], w=['f%d' % f])
                    tt(R1[:, 2 * h, q0:q0 + nq], ps[6][:, 0:nq], fsc[:, f, 0:nq], ALU.mult, r=['ps6', 'f%d' % f], w=['r%d' % (2 * h)])

            def layer_block(l, blk):
                T = blk['T']
                segs = blk['segs']
                tiles = blk['tiles']
                tok0 = blk['tok0']
                samp = blk['samp']
                b_idx = blk['b']
                Wl = w_in[l]
                st['cur_l'] = l
                S.dma('sp', kvt[:], kvn[l:l + 1, :].partition_broadcast(128), sem="kvt", w=['kvt'])
                if l == 0:
                    S.dma('sp', csT[:, 0:T], tabF[:, tok0:tok0 + T], sem="csT", w=['csT'])
                norm_mod(l, blk, MSH_A, MSC_A)
                chk('norm1')
                sA = load_std(Wl, 0, 8, 0, 512)
                sB = load_std(Wl, 8, 8, 0, 512)
                for m in range(4):
                    b = gbank()
                    for kc in range(16):
                        s_, k_ = (sA, kc) if kc < 8 else (sB, kc - 8)
                        mm(ps[b][:, 0:T], pc(s_, k_, 512, m * 128), R1[:, kc, 0:T], start=(kc == 0), stop=(kc == 15), r=['w%d' % s_, 'r%d' % kc], w=['ps%d' % b])
                    cp('dve', ql[:, m, 0:T], ps[b][:, 0:T], r=['ps%d' % b], w=['ql%d' % m])
                    ssq_accum(6, ql[:, m, 0:T], ['ql%d' % m], T, m == 0, m == 3)
                chk('qlat')
                sA = load_std(Wl, 0, 8, 512, 512)
                sB = load_std(Wl, 8, 8, 512, 512)
                for (r0, rows) in tiles:
                    b = gbank()
                    for kc in range(16):
                        s_, k_ = (sA, kc) if kc < 8 else (sB, kc - 8)
                        mm(ps[b][0:rows, 0:512], R1[:, kc, r0:r0 + rows], pc(s_, k_, 512, 0, 512), start=(kc == 0), stop=(kc == 15), r=['w%d' % s_, 'r%d' % kc], w=['ps%d' % b])
                    f = fscr()
                    S.op('dve', lambda: nc.vector.memset(sm[:, 0:1], 0.0), w=['sm'])
                    act(fsc[0:rows, f, 0:512], ps[b][0:rows, 0:512], AF.Square, r=['ps%d' % b], w=['f%d' % f, 'sm'], accum=sm[0:rows, 0:1])
                    act(sm[0:rows, 1:2], sm[0:rows, 0:1], AF.Ln, r=['sm'], w=['sm'], scale=1.0 / 512, bias=epsc[0:rows, 0:1])
                    act(sm[0:rows, 2:3], sm[0:rows, 1:2], AF.Exp, r=['sm'], w=['sm'], scale=-0.5)
                    stt(cst[0:rows, :], ps[b][0:rows, 0:512], sm[0:rows, 2:3], kvt[0:rows, :], ALU.mult, ALU.mult, r=['ps%d' % b, 'sm', 'kvt'], w=['cst'])
                    if samp:
                        S.dma('sp', o_ckvs[l, r0:r0 + rows, :], cst[0:rows, :], sem="cst_o", r=['cst'])
                    else:
                        S.dma('sp', o_ckvp[l, tok0 + r0:tok0 + r0 + rows, :], cst[0:rows, :], sem="cst_o", r=['cst'])
                    b2 = gbank()
                    for c in range(4):
                        tr(ps[b2][:, c * 128:c * 128 + rows], cst[0:rows, c * 128:(c + 1) * 128], rows, r=['cst'], w=['ps%d' % b2], inc=(c == 3))
                    for c in range(4):
                        cp(alt(), ownckv[:, c, r0:r0 + rows], ps[b2][:, c * 128:c * 128 + rows], r=['ps%d' % b2], w=['ock'])
                chk('kv')
                def fill_kpe(s):
                    S.dma('pool', slot(s)[:, 0:1024].rearrange("p (k n) -> p k n", k=16), Wl[:, 1024:1088].rearrange("(k p) n -> p k n", p=128), sem="w%d" % s, w=['w%d' % s])
                s = cached_piece(('kpe', l), fill_kpe)
                for (r0, rows) in tiles:
                    b = gbank()
                    S.dma('sp', ttab[0:rows, :], tabT[tok0 + r0:tok0 + r0 + rows, :], sem="ttab", w=['ttab'])
                    for kc in range(16):
                        mm(ps[b][0:rows, 0:64], R1[:, kc, r0:r0 + rows], pc(s, kc, 64, 0, 64), start=(kc == 0), stop=(kc == 15), r=['w%d' % s, 'r%d' % kc], w=['ps%d' % b])
                    tt(krt[0:rows, 0, 0:64], ps[b][0:rows, 0:64], ttab[0:rows, 0:64], ALU.mult, r=['ps%d' % b, 'ttab'], w=['krtA'])
                    tt(krt[0:rows, 1, 0:32], ps[b][0:rows, 32:64], ttab[0:rows, 64:96], ALU.mult, r=['ps%d' % b, 'ttab'], w=['krtB'])
                    tt(krt[0:rows, 1, 32:64], ps[b][0:rows, 0:32], ttab[0:rows, 96:128], ALU.mult, r=['ps%d' % b, 'ttab', 'krtB'], w=['krtB'])
                    tt(krt[0:rows, 2, 0:64], krt[0:rows, 0, 0:64], krt[0:rows, 1, 0:64], ALU.add, r=['krtA', 'krtB'], w=['krtC'])
                    tt(krt[0:rows, 2, 64:128], krt[0:rows, 0, 0:64], krt[0:rows, 1, 0:64], ALU.add, r=['krtA', 'krtB', 'krtC'], w=['krtC'])
                    if samp:
                        S.dma('sp', o_krs[l, r0:r0 + rows, :], krt[0:rows, 2, 0:64], sem="krt_o", r=['krtC'])
                    else:
                        S.dma('sp', o_krp[l, tok0 + r0:tok0 + r0 + rows, :], krt[0:rows, 2, 0:64], sem="krt_o", r=['krtC'])
                    b2 = gbank()
                    tr(ps[b2][:, 0:rows], krt[0:rows, 2, :], rows, r=['krtC'], w=['ps%d' % b2])
                    cp(alt(), ownkr[:, r0:r0 + rows], ps[b2][:, 0:rows], r=['ps%d' % b2], w=['okr'])
                chk('kpe')
                gcs = R46[:, 0:1024].bitcast(BF16)
                CUW = 516
                cu = R46[:, 1024:1024 + 4 * CUW]
                cukeys = ['g%d' % i for i in range(4, 13)]
                for half in range(2):
                    sA = load_std(Wl, 0, 8, 2112 + half * 512, 512)
                    sB = load_std(Wl, 8, 8, 2112 + half * 512, 512)
                    for j in range(4):
                        b = gbank()
                        for kc in range(16):
                            s_, k_ = (sA, kc) if kc < 8 else (sB, kc - 8)
                            mm(ps[b][:, 0:T], pc(s_, k_, 512, j * 128), R1[:, kc, 0:T], start=(kc == 0), stop=(kc == 15), r=['w%d' % s_, 'r%d' % kc], w=['ps%d' % b])
                        cp('act', gcs[:, j * 512:j * 512 + T], ps[b][:, 0:T], r=['ps%d' % b], w=['g%d' % j])
                    sA = load_std(Wl, 0, 8, 3136 + half * 512, 512)
                    sB = load_std(Wl, 8, 8, 3136 + half * 512, 512)
                    for j in range(4):
                        c = half * 4 + j
                        b = gbank()
                        for kc in range(16):
                            s_, k_ = (sA, kc) if kc < 8 else (sB, kc - 8)
                            mm(ps[b][:, 0:T], pc(s_, k_, 512, j * 128), R1[:, kc, 0:T], start=(kc == 0), stop=(kc == 15), r=['w%d' % s_, 'r%d' % kc], w=['ps%d' % b])
                        for si, sg in enumerate(segs):
                            so = j * CUW + si * (2 + sg.n)
                            if samp:
                                cp('dve', cu[:, so:so + 2], blk['halo'][si][:, :, c], r=['halo%d' % si], w=cukeys)
                            else:
                                cp('dve', cu[:, so:so + 2], hal[:, l, c, :], r=['hal'], w=cukeys)
                            tt(cu[:, so + 2:so + 2 + sg.n], ps[b][:, sg.c0:sg.c0 + sg.n], gcs[:, j * 512 + sg.c0:j * 512 + sg.c0 + sg.n], ALU.mult,
                               r=['ps%d' % b, 'g%d' % j], w=cukeys)
                            if samp or b_idx == 3:
                                cp('dve', cs2[:, :, c] if not samp else blk['cs2'][si][:, :, c], cu[:, so + sg.n:so + sg.n + 2], r=cukeys, w=['cs2_%d' % si])
                            if not samp:
                                cp('dve', hal[:, l, c, :], cu[:, so + sg.n:so + sg.n + 2], r=cukeys, w=['hal'])
                    sA = load_std(Wl, 0, 8, 1088 + half * 512, 512)
                    sB = load_std(Wl, 8, 8, 1088 + half * 512, 512)
                    for j in range(4):
                        c = half * 4 + j
                        b = gbank()
                        for kc in range(16):
                            s_, k_ = (sA, kc) if kc < 8 else (sB, kc - 8)
                            mm(ps[b][:, 0:T], pc(s_, k_, 512, j * 128), R1[:, kc, 0:T], start=(kc == 0), stop=(kc == 15), r=['w%d' % s_, 'r%d' % kc], w=['ps%d' % b])
                        f = fscr()
                        for si, sg in enumerate(segs):
                            so = j * CUW + si * (2 + sg.n)
                            wv = lambda k: vT[:, 464 + l * 24 + k * 8 + c:464 + l * 24 + k * 8 + c + 1]
                            y = fsc[:, f, sg.c0:sg.c0 + sg.n]
                            ts(y, cu[:, so:so + sg.n], wv(0), None, ALU.mult, None, r=cukeys, w=['f%d' % f])
                            stt(y, cu[:, so + 1:so + 1 + sg.n], wv(1), y, ALU.mult, ALU.add, r=cukeys + ['f%d' % f], w=['f%d' % f])
                            stt(y, cu[:, so + 2:so + 2 + sg.n], wv(2), y, ALU.mult, ALU.add, r=cukeys + ['f%d' % f], w=['f%d' % f])
                            tt(convn[:, c, sg.c0:sg.c0 + sg.n], ps[b][:, sg.c0:sg.c0 + sg.n], y, ALU.mult, r=['ps%d' % b, 'f%d' % f], w=['cn%d' % c])
                        ssq_accum(7, convn[:, c, 0:T], ['cn%d' % c], T, c == 0, c == 7)
                if samp or b_idx == 3:
                    for si, sg in enumerate(segs):
                        src = blk['cs2'][si] if samp else cs2
                        for t in range(2):
                            dst = (o_convs[l, si, t] if samp else o_convp[l, t]).rearrange("(c p) -> p c", p=128)
                            S.dma('sp', dst, src[:, t, :], sem="cs2_o%d" % si, r=['cs2_%d' % si], nonc=True)
                rstd_from(7, 1024, 1, T)
                for c in range(8):
                    stt(convn[:, c, 0:T], convn[:, c, 0:T], vT[:, 432 + l * 8 + c:432 + l * 8 + c + 1], rst[:, 1, 0:T], ALU.mult, ALU.mult,
                        r=['cn%d' % c, 'rst1'], w=['cn%d' % c])
                chk('conv')
                rstd_from(6, 512, 0, T)
                for m in range(4):
                    stt(ql[:, m, 0:T], ql[:, m, 0:T], vT[:, 384 + l * 4 + m:384 + l * 4 + m + 1], rst[:, 0, 0:T], ALU.mult, ALU.mult,
                        r=['ql%d' % m, 'rst0'], w=['ql%d' % m])
                for g in range(2):
                    def fill_q2(s, g=g):
                        dst = slot(s).rearrange("p (k h c) -> p k h c", k=4, h=4)
                        srcv = w_uq[l].rearrange("(k p) (h c) -> p k h c", p=128, c=192)
                        for kc in range(4):
                            S.dma('pool', dst[:, kc, :, 0:192], srcv[:, kc, 4 * g:4 * g + 4, :], sem="w%d" % s, w=['w%d' % s])
                            S.dma('pool', dst[:, kc, :, 192:224], srcv[:, kc, 4 * g:4 * g + 4, 160:192], sem="w%d" % s, w=['w%d' % s])
                            S.dma('pool', dst[:, kc, :, 224:256], srcv[:, kc, 4 * g:4 * g + 4, 128:160], sem="w%d" % s, w=['w%d' % s])
                    s = cached_piece(('q2', l, g), fill_q2)
                    for hh in range(4):
                        h = 4 * g + hh
                        for part in range(2):
                            b = gbank()
                            for kc in range(4):
                                mm(ps[b][:, 0:T], slot(s)[:, kc * 1024 + hh * 256 + part * 128:kc * 1024 + hh * 256 + part * 128 + 128], ql[:, kc, 0:T],
                                   start=(kc == 0), stop=(kc == 3), r=['w%d' % s, 'ql%d' % kc], w=['ps%d' % b])
                            if part == 0:
                                cp('act', R1[:, 2 * h, 0:T], ps[b][:, 0:T], r=['ps%d' % b], w=['r%d' % (2 * h)])
                            else:
                                tt(R1[:, 2 * h + 1, 0:T], ps[b][:, 0:T], csT[:, 0:T], ALU.mult, r=['ps%d' % b, 'csT'], w=['r%d' % (2 * h + 1)])
                chk('q')
                if not samp:
                    wk = [load_std(w_ukv[l], 0, 4, g * 1024, 1024) for g in range(2)]

                    def ckv_acc(kc, t0, n):
                        if t0 >= b_idx * 512:
                            return ownckv[:, kc, t0 - b_idx * 512:t0 - b_idx * 512 + n], ['ock']
                        o = (l * 4 + kc) * 1536 + t0
                        return cache[:, o:o + n], ['cache']

                    def kr_acc(t0, n):
                        if t0 >= b_idx * 512:
                            return ownkr[:, t0 - b_idx * 512:t0 - b_idx * 512 + n], ['okr']
                        return krc[:, l * 1536 + t0:l * 1536 + t0 + n], ['krc']
                    attention(l, blk, dict(q0=0, nq=512, nkb=b_idx + 1, ckv=ckv_acc, kr=kr_acc, own_n=0, diag=True), wk)
                    if b_idx < 3:
                        for kc in range(4):
                            o = (l * 4 + kc) * 1536 + b_idx * 512
                            cp(alt(), cache[:, o:o + 512], ownckv[:, kc, :], r=['ock'], w=['cache'])
                        cp(alt(), krc[:, l * 1536 + b_idx * 512:l * 1536 + b_idx * 512 + 512], ownkr[:, :], r=['okr'], w=['krc'])
                else:
                    for si, sg in enumerate(segs):
                        for q4 in range(4):
                            stg = R46[:, (q4 % 2) * 2048:(q4 % 2) * 2048 + 2048]
                            sk = ['g%d' % ((q4 % 2) * 8 + i) for i in range(8)]
                            S.dma('sp', stg.rearrange("p (t n) -> p t n", t=4), pkv[l, si, q4 * 512:(q4 + 1) * 512, :].rearrange("(t p) n -> p t n", p=128),
                                  sem="pst%d" % (q4 % 2), w=sk)
                            for t4 in range(4):
                                b = gbank()
                                for c in range(4):
                                    tr(ps[b][:, c * 128:(c + 1) * 128], stg[:, t4 * 512 + c * 128:t4 * 512 + (c + 1) * 128], 128, r=sk, w=['ps%d' % b], inc=(c == 3))
                                for c in range(4):
                                    o = c * 2048 + q4 * 512 + t4 * 128
                                    cp(alt(), cache[:, o:o + 128], ps[b][:, c * 128:(c + 1) * 128], r=['ps%d' % b], w=['cache'])
                        for q4 in range(4):
                            stg = R46[:, (q4 % 2) * 2048:(q4 % 2) * 2048 + 512]
                            sk = ['g%d' % ((q4 % 2) * 8 + i) for i in range(2)]
                            for dup in range(2):
                                S.dma('sp', stg.rearrange("p (t n) -> p t n", t=4)[:, :, dup * 64:(dup + 1) * 64],
                                      pkr[l, si, q4 * 512:(q4 + 1) * 512, :].rearrange("(t p) n -> p t n", p=128), sem="pst%d" % (q4 % 2), w=sk)
                            b = gbank()
                            for t4 in range(4):
                                tr(ps[b][:, t4 * 128:(t4 + 1) * 128], stg[:, t4 * 128:(t4 + 1) * 128], 128, r=sk, w=['ps%d' % b], inc=(t4 == 3))
                            cp(alt(), krc[:, q4 * 512:(q4 + 1) * 512], ps[b][:, 0:512], r=['ps%d' % b], w=['krc'])
                        wk = [load_std(w_ukv[l], 0, 4, g * 1024, 1024) for g in range(2)]

                        def ckv_acc(kc, t0, n):
                            return cache[:, kc * 2048 + t0:kc * 2048 + t0 + n], ['cache']

                        def kr_acc(t0, n):
                            return krc[:, t0:t0 + n], ['krc']
                        attention(l, blk, dict(q0=sg.c0, nq=sg.n, nkb=4, ckv=ckv_acc, kr=kr_acc, own_n=sg.n, diag=False), wk)
                chk('attn')
                b = gbank()
                for h in range(NH):
                    ssq_accum(b, R1[:, 2 * h, 0:T], ['r%d' % (2 * h)], T, h == 0, h == NH - 1)
                rstd_from(b, 1024, 0, T)
                for h in range(NH):
                    stt(R1[:, 2 * h, 0:T], R1[:, 2 * h, 0:T], vT[:, 400 + l * 8 + h:400 + l * 8 + h + 1], rst[:, 0, 0:T], ALU.mult, ALU.mult,
                        r=['r%d' % (2 * h), 'rst0'], w=['r%d' % (2 * h)])
                for cg in range(4):
                    sA = load_std(w_out[l], 0, 8, cg * 512, 512)
                    sB = load_std(w_out[l], 8, 8, cg * 512, 512)
                    for j in range(4):
                        m = cg * 4 + j
                        b = gbank()
                        for kc in range(16):
                            if kc < 8:
                                mm(ps[b][:, 0:T], pc(sA, kc, 512, j * 128), R1[:, 2 * kc, 0:T], start=(kc == 0), stop=False, r=['w%d' % sA, 'r%d' % (2 * kc)], w=['ps%d' % b])
                            else:
                                mm(ps[b][:, 0:T], pc(sB, kc - 8, 512, j * 128), convn[:, kc - 8, 0:T], start=False, stop=(kc == 15), r=['w%d' % sB, 'cn%d' % (kc - 8)], w=['ps%d' % b])
                        for sg in segs:
                            stt(xT[:, m, sg.c0:sg.c0 + sg.n], ps[b][:, sg.c0:sg.c0 + sg.n], modv(l, sg.e, MG_A, m), xT[:, m, sg.c0:sg.c0 + sg.n], ALU.mult, ALU.add,
                                r=['ps%d' % b, 'x%d' % m], w=['x%d' % m])
                chk('wout')
                norm_mod(l, blk, MSH_M, MSC_M)
                for g in range(8):
                    ub = (g % 2) * 2048
                    up = R46[:, ub:ub + 2048].bitcast(BF16)
                    uk = ['g%d' % ((g % 2) * 8 + i) for i in range(8)]
                    for half in range(2):
                        sA = load_std(w_up[l], 0, 8, g * 1024 + half * 512, 512)
                        sB = load_std(w_up[l], 8, 8, g * 1024 + half * 512, 512)
                        for j in range(4):
                            b = gbank()
                            for kc in range(16):
                                s_, k_ = (sA, kc) if kc < 8 else (sB, kc - 8)
                                mm(ps[b][:, 0:T], pc(s_, k_, 512, j * 128), R1[:, kc, 0:T], start=(kc == 0), stop=(kc == 15), r=['w%d' % s_, 'r%d' % kc], w=['ps%d' % b])
                            act(rtmp[:, 0:T], ps[b][:, 0:T], AF.Relu, r=['ps%d' % b], w=['rtmp'])
                            jj = half * 4 + j
                            tt(up[:, jj * 512:jj * 512 + T], rtmp[:, 0:T], rtmp[:, 0:T], ALU.mult, r=['rtmp'], w=uk)
                    for cg in range(4):
                        s = load_std(w_down[l], g * 8, 8, cg * 512, 512)
                        for j in range(4):
                            m = cg * 4 + j
                            b = gbank()
                            for kc in range(8):
                                mm(ps[b][:, 0:T], pc(s, kc, 512, j * 128), up[:, kc * 512:kc * 512 + T], start=(kc == 0), stop=(kc == 7), r=['w%d' % s] + uk, w=['ps%d' % b])
                            for sg in segs:
                                stt(xT[:, m, sg.c0:sg.c0 + sg.n], ps[b][:, sg.c0:sg.c0 + sg.n], modv(l, sg.e, MG_M, m), xT[:, m, sg.c0:sg.c0 + sg.n], ALU.mult, ALU.add,
                                    r=['ps%d' % b, 'x%d' % m], w=['x%d' % m])

            def run_block(blk, xsrc, ydst):
                T = blk['T']
                tiles = blk['tiles']
                gk = ['g%d' % i for i in range(16)]
                for ti, (r0, rows) in enumerate(tiles):
                    stg = R46[:, (ti % 2) * 2048:(ti % 2) * 2048 + 2048]
                    sk = ['g%d' % ((ti % 2) * 8 + i) for i in range(8)]
                    S.dma('sp', stg[0:rows, :], xsrc[r0:r0 + rows, :], sem="xst%d" % (ti % 2), w=sk)
                    chk('xl1')
                    for c4 in range(4):
                        b = gbank()
                        if _CACHE.get('xv', 0) == 3:
                            b = 1
                        for c in range(4):
                            cc = c4 * 4 + c
                            tr(ps[b][:, c * 128:c * 128 + rows], stg[0:rows, cc * 128:(cc + 1) * 128], rows, r=sk, w=['ps%d' % b], inc=(c == 3))
                        chk('xl2')
                        chk('xt%d_%d' % (ti, c4))
                        xv = _CACHE.get('xv', 7)
                        for c in range(4):
                            cc = c4 * 4 + c
                            if xv == 1 and c4 == 1 and c > 0:
                                continue
                            if xv == 2:
                                cc = c
                            if xv == 5 and c4 == 1 and c != 1:
                                continue
                            en = alt()
                            if (xv == 4 and c4 == 1) or xv == 7:
                                en = 'act'
                            if xv == 5 and c4 == 1:
                                en = 'dve'
                            if xv == 6:
                                cp(en, fsc[:, 0, c * 128:c * 128 + rows], ps[b][:, c * 128:c * 128 + rows], r=['ps%d' % b], w=['x%d' % cc])
                            else:
                                cp(en, xT[:, cc, r0:r0 + rows], ps[b][:, c * 128:c * 128 + rows], r=['ps%d' % b], w=['x%d' % cc])
                        if _CACHE.get('serialtr'):
                            S._wait('pe', ('act', S.cnt['act']))
                            S._wait('pe', ('dve', S.cnt['dve']))
                        chk('xl3')
                        chk('xg%d_%d' % (ti, c4))
                chk('xload')
                for l in range(NL):
                    layer_block(l, blk)
                chk('layers')
                b = gbank()
                for kc in range(16):
                    ssq_accum(b, xT[:, kc, 0:T], ['x%d' % kc], T, kc == 0, kc == 15)
                rstd_from(b, D, 0, T)
                for kc in range(16):
                    stt(xT[:, kc, 0:T], xT[:, kc, 0:T], vT[:, 560 + kc:561 + kc], rst[:, 0, 0:T], ALU.mult, ALU.mult, r=['x%d' % kc, 'rst0'], w=['x%d' % kc])
                for ti, (r0, rows) in enumerate(tiles):
                    stg = R46[:, (ti % 2) * 2048:(ti % 2) * 2048 + 2048]
                    sk = ['g%d' % ((ti % 2) * 8 + i) for i in range(8)]
                    for c4 in range(4):
                        b = gbank()
                        for c in range(4):
                            cc = c4 * 4 + c
                            tr(ps[b][0:rows, c * 128:(c + 1) * 128], xT[:, cc, r0:r0 + rows], 128, r=['x%d' % cc], w=['ps%d' % b], inc=(c == 3))
                        cp(alt(), stg[0:rows, c4 * 512:(c4 + 1) * 512], ps[b][0:rows, 0:512], r=['ps%d' % b], w=sk)
                    S.dma('sp', ydst[r0:r0 + rows, :], stg[0:rows, :], sem="yst%d" % (ti % 2), r=sk)

            for bi in range(NBLK):
                blk = dict(T=512, segs=[Seg(0, 512, 0)], tiles=[(i * 128, 128) for i in range(4)], tok0=bi * 512, samp=False, b=bi)
                st['pass'] = bi
                run_block(blk, xp[bi * 512:(bi + 1) * 512, :], o_yp[bi * 512:(bi + 1) * 512, :])
                if bi == 0 and _CACHE.get('wcache', 1):
                    for s_ in range(4):
                        if ("wst%d" % s_) in S.cnt:
                            S._wait('pool', ("wst%d" % s_, S.cnt["wst%d" % s_]))
            halo_t = [sbt("halo%d" % i, [128, 2, 8], F32) for i in range(2)]
            cs2_t = [sbt("cs2s%d" % i, [128, 2, 8], F32) for i in range(2)]
            sblk = dict(T=32, segs=[Seg(0, 16, 1), Seg(16, 16, 2)], tiles=[(0, 32)], tok0=2048, samp=True, b=4, halo=halo_t, cs2=cs2_t)
            orig_layer_block = layer_block

            def layer_block_s(l, blk):
                for si in range(2):
                    for t in range(2):
                        S.dma('sp', halo_t[si][:, t, :], sconv[l, si, t].rearrange("(c p) -> p c", p=128), sem="halo%d" % si, w=['halo%d' % si], nonc=True)
                orig_layer_block(l, blk)
            layer_block = layer_block_s
            if SAMPLE:
                if _CACHE.get('nslots_s', 8) > 4:
                    for en_ in ('pe', 'act', 'dve'):
                        if S.cnt[en_] > 0:
                            S._wait('pool', (en_, S.cnt[en_]))
                    st['nslots'] = _CACHE.get('nslots_s', 8)
                st['pass'] = 4
                run_block(sblk, xs, o_ys)
        except _Stop:
            pass
        S.finish('sp')
        _CACHE['counts'] = dict(S.cnt)
        _CACHE['nwaits'] = S.nwaits
    return nc


_CACHE = {}


def _tables():
    half = 32
    inv = (10000.0 ** (-np.arange(half, dtype=np.float32) / half)).astype(np.float32)
    pos = np.concatenate([np.arange(2048), 2048 + np.arange(16), 2048 + np.arange(16)]).astype(np.float32)
    ang = pos[:, None] * inv[None, :]
    cos, sin = np.cos(ang).astype(np.float32), np.sin(ang).astype(np.float32)
    tabT = np.concatenate([cos, cos, -sin, sin], axis=1).astype(np.float32)
    tabF = np.ascontiguousarray(tabT.T)
    return tabF, np.ascontiguousarray(tabT)


def kernel(x_prompt, x_sample, c_prompt, c_sample, cache_kv_latent, cache_k_rope, state_conv,
           w_ada, b_ada, w_in, q_norm, w_uq, kv_norm, w_ukv, conv_w,
           attn_out_norm, conv_out_norm, w_out, w_up, w_down, final_norm):
    f = lambda a: np.ascontiguousarray(np.asarray(a, dtype=np.float32))
    x_prompt, x_sample, c_prompt, c_sample = f(x_prompt), f(x_sample), f(c_prompt), f(c_sample)
    cache_kv_latent, cache_k_rope, state_conv = f(cache_kv_latent), f(cache_k_rope), f(state_conv)
    w_ada, b_ada, w_in, q_norm, w_uq, kv_norm, w_ukv = f(w_ada), f(b_ada), f(w_in), f(q_norm), f(w_uq), f(kv_norm), f(w_ukv)
    conv_w, attn_out_norm, conv_out_norm, w_out, w_up, w_down, final_norm = f(conv_w), f(attn_out_norm), f(conv_out_norm), f(w_out), f(w_up), f(w_down), f(final_norm)
    if 'nc' not in _CACHE:
        _CACHE['nc'] = build_program()
    nc = _CACHE['nc']
    tabF, tabT = _tables()
    common = np.concatenate([b_ada.reshape(384, 128), q_norm.reshape(16, 128), attn_out_norm.reshape(32, 128),
                             conv_out_norm.reshape(32, 128), conv_w.reshape(96, 128), final_norm.reshape(16, 128)], axis=0)
    in_maps = []
    for c in range(8):
        vecs = np.zeros((640, 128), np.float32)
        vecs[0:576] = common
        vecs[576:592] = c_prompt[c].reshape(16, 128)
        vecs[592:608] = c_sample[2 * c].reshape(16, 128)
        vecs[608:624] = c_sample[2 * c + 1].reshape(16, 128)
        in_maps.append(dict(
            xp=x_prompt[c], xs=np.ascontiguousarray(x_sample[2 * c:2 * c + 2].reshape(32, D)), vecs=vecs, kvn=kv_norm,
            pkv=np.ascontiguousarray(cache_kv_latent[:, 2 * c:2 * c + 2]), pkr=np.ascontiguousarray(cache_k_rope[:, 2 * c:2 * c + 2]),
            sconv=np.ascontiguousarray(state_conv[:, 2 * c:2 * c + 2]), tabF=tabF, tabT=tabT,
            w_ada=w_ada, w_in=w_in, w_uq=w_uq, w_ukv=w_ukv, w_out=w_out, w_up=w_up, w_down=w_down))
    res = run_bass_kernel_spmd(nc, in_maps, core_ids=list(range(8)))
    R = res.results
    y_p = np.stack([R[c]["o_yp"] for c in range(8)], axis=0)
    y_s = np.concatenate([R[c]["o_ys"].reshape(2, 16, D) for c in range(8)], axis=0)
    ckv_p = np.stack([R[c]["o_ckvp"] for c in range(8)], axis=1)
    kr_p = np.stack([R[c]["o_krp"] for c in range(8)], axis=1)
    conv_p = np.stack([R[c]["o_convp"] for c in range(8)], axis=1)
    ckv_s = np.concatenate([R[c]["o_ckvs"].reshape(L, 2, 16, 512) for c in range(8)], axis=1)
    kr_s = np.concatenate([R[c]["o_krs"].reshape(L, 2, 16, 64) for c in range(8)], axis=1)
    conv_s = np.concatenate([R[c]["o_convs"] for c in range(8)], axis=1)
    return (y_p.astype(np.float32), y_s.astype(np.float32), ckv_p.astype(np.float32), kr_p.astype(np.float32),
            conv_p.astype(np.float32), ckv_s.astype(np.float32), kr_s.astype(np.float32), conv_s.astype(np.float32))
```

```python
import contextlib
import numpy as np
import concourse.bass as bass
import concourse.mybir as mybir
from concourse.bass_utils import run_bass_kernel_spmd

F32, BF16 = mybir.dt.float32, mybir.dt.bfloat16
AF = mybir.ActivationFunctionType
ALU = mybir.AluOpType

L = 4
D = 2048
SEQ = 2048
NH = 8
TB = 512
EPS = 1e-6
ATTN_SCALE = 192.0 ** -0.5
INC = 4160
DFF = 8192
NTAB = 2048 + 32


class Sched:
    def __init__(self, nc, es):
        self.nc = nc
        self.es = es
        self.eng = {'pe': nc.tensor, 'act': nc.scalar, 'dve': nc.vector, 'pool': nc.gpsimd, 'sp': nc.sync}
        self.h = {}
        self.cnt = {}
        for e in self.eng:
            self.h[e] = es.enter_context(nc.semaphore("s_" + e))
            self.cnt[e] = 0
        self.seen = {e: {} for e in self.eng}
        self.snap = {}
        self.bufs = {}
        self.nwaits = 0

    def _wait(self, e, tok):
        name, val = tok
        if name == 'pe' and e == 'pe':
            return
        if self.seen[e].get(name, 0) >= val:
            return
        self.eng[e].wait_ge(self.h[name], val)
        self.nwaits += 1
        self.seen[e][name] = val
        sn = self.snap.get(tok)
        if sn:
            se = self.seen[e]
            for k, v in sn.items():
                if se.get(k, 0) < v:
                    se[k] = v

    def _deps(self, e, r, w):
        deps = set()
        for k in r:
            b = self.bufs.get(k)
            if b and b[0]:
                deps.add(b[0])
        for k in w:
            b = self.bufs.get(k)
            if b:
                if b[0]:
                    deps.add(b[0])
                for n, v in b[1].items():
                    deps.add((n, v))
        for t in sorted(deps):
            self._wait(e, t)

    def _record(self, tok, r, w):
        for k in r:
            b = self.bufs.setdefault(k, [None, {}])
            if b[1].get(tok[0], 0) < tok[1]:
                b[1][tok[0]] = tok[1]
        for k in w:
            self.bufs[k] = [tok, {}]

    def op(self, e, fn, r=(), w=(), inc=True):
        self._deps(e, r, w)
        if e in ('dve', 'act') and self.cnt[e] > 0 and _CACHE.get('selfser', 1):
            self._wait(e, (e, self.cnt[e]))
        ins = fn()
        if inc:
            self.cnt[e] += 1
            ins.then_inc(self.h[e], 1)
            tok = (e, self.cnt[e])
            self.snap[tok] = dict(self.seen[e])
        else:
            tok = (e, self.cnt[e] + 1)
        self._record(tok, r, w)
        return ins

    def dma(self, q, out, in_, sem, r=(), w=(), nonc=False):
        for k in w:
            b = self.bufs.get(k)
            if b and b[0] and b[0][0] == sem and not b[1]:
                b[0] = None
        self._deps(q, r, w)
        if sem not in self.h:
            self.h[sem] = self.es.enter_context(self.nc.semaphore("d_" + sem))
            self.cnt[sem] = 0
        if nonc:
            with self.nc.allow_non_contiguous_dma(reason="small strided vector"):
                ins = self.eng[q].dma_start(out=out, in_=in_)
        else:
            ins = self.eng[q].dma_start(out=out, in_=in_)
        self.cnt[sem] += 16
        ins.then_inc(self.h[sem], 16)
        tok = (sem, self.cnt[sem])
        self.snap[tok] = dict(self.seen[q])
        self._record(tok, r, w)

    def finish(self, e='sp'):
        for name, h in self.h.items():
            if name in self.eng:
                continue
            self._wait(e, (name, self.cnt[name]))
        for name in ('act', 'dve', 'pe'):
            if self.cnt[name] > 0:
                self._wait(e, (name, self.cnt[name]))


class _Stop(Exception):
    pass


class Seg:
    def __init__(self, c0, n, e):
        self.c0, self.n, self.e = c0, n, e


def build_program(NL=L, NBLK=4, SAMPLE=True):
    nc = bass.Bass("TRN2", target_bir_lowering=False)
    dt_in = lambda n, s: nc.dram_tensor(n, s, F32, kind="ExternalInput").ap()
    dt_out = lambda n, s: nc.dram_tensor(n, s, F32, kind="ExternalOutput").ap()
    xp = dt_in("xp", [SEQ, D])
    xs = dt_in("xs", [32, D])
    vecs = dt_in("vecs", [640, 128])
    kvn = dt_in("kvn", [NL, 512])
    pkv = dt_in("pkv", [NL, 2, 2048, 512])
    pkr = dt_in("pkr", [NL, 2, 2048, 64])
    sconv = dt_in("sconv", [NL, 2, 2, 1024])
    tabF = dt_in("tabF", [128, NTAB])
    tabT = dt_in("tabT", [NTAB, 128])
    w_ada = dt_in("w_ada", [NL, D, 6 * D])
    w_in = dt_in("w_in", [NL, D, INC])
    w_uq = dt_in("w_uq", [NL, 512, 1536])
    w_ukv = dt_in("w_ukv", [NL, 512, 2048])
    w_out = dt_in("w_out", [NL, D, D])
    w_up = dt_in("w_up", [NL, D, DFF])
    w_down = dt_in("w_down", [NL, DFF, D])
    wscr = [nc.dram_tensor("wscr%d" % i, [100, 128, 4096], BF16, kind="Internal").ap() for i in range(NL)]
    o_yp = dt_out("o_yp", [SEQ, D])
    o_ys = dt_out("o_ys", [32, D])
    o_ckvp = dt_out("o_ckvp", [L, SEQ, 512])
    o_krp = dt_out("o_krp", [L, SEQ, 64])
    o_convp = dt_out("o_convp", [L, 2, 1024])
    o_ckvs = dt_out("o_ckvs", [L, 32, 512])
    o_krs = dt_out("o_krs", [L, 32, 64])
    o_convs = dt_out("o_convs", [L, 2, 2, 1024])

    with contextlib.ExitStack() as es:
        sbt = lambda n, s, d: es.enter_context(nc.sbuf_tensor(n, s, d))
        S = Sched(nc, es)
        cache = sbt("cache", [128, L * 4 * 1536], BF16)
        ownckv = sbt("ownckv", [128, 4, 512], BF16)
        krc = sbt("krc", [128, L * 1536], BF16)
        ownkr = sbt("ownkr", [128, 512], BF16)
        xT = sbt("xT", [128, 16, 512], F32)
        R1 = sbt("R1", [128, 16, 512], BF16)
        convn = sbt("convn", [128, 8, 512], BF16)
        R46 = sbt("R46", [128, 4096], F32)
        ring = sbt("ring", [128, 4, 4096], BF16)
        sq = sbt("sq", [128, 2, 512], BF16)
        rst = sbt("rst", [128, 2, 512], F32)
        fsc = sbt("fsc", [128, 3, 512], F32)
        ql = sbt("ql", [128, 4, 512], BF16)
        pT = sbt("pT", [128, 2, 512], BF16)
        cst = sbt("cst", [128, 512], F32)
        krt = sbt("krt", [128, 3, 128], F32)
        ttab = sbt("ttab", [128, 128], F32)
        csT = sbt("csT", [128, 512], F32)
        kvt = sbt("kvt", [128, 512], F32)
        ident = sbt("ident", [128, 128], F32)
        ones = sbt("ones", [128, 128], BF16)
        onesr = sbt("onesr", [1, 4], BF16)
        epsc = sbt("epsc", [128, 1], F32)
        vT = sbt("vT", [128, 640], F32)
        sT = sbt("sT", [128, 16, 3], BF16)
        mod = sbt("mod", [128, L, 3, 96], F32)
        hal = sbt("hal", [128, L, 8, 2], F32)
        cs2 = sbt("cs2", [128, 2, 8], F32)
        sm = sbt("sm", [128, 8], F32)
        kown = sbt("kown", [128, 32], BF16)
        vown = sbt("vown", [16, 128], BF16)
        rtmp = sbt("rtmp", [128, 512], BF16)
        ps = [es.enter_context(nc.psum_tensor("ps%d" % i, [128, 512], F32)) for i in range(8)]

        st = {'gb': 0, 'piece': 0, 'alt': 0, 'sq': 0, 'f': 0, 'pt': 0, 'nrec': {}}

        def gbank():
            st['gb'] = (st['gb'] + 1) % 4
            return st['gb']

        def alt():
            st['alt'] ^= 1
            return 'act' if st['alt'] else 'dve'

        def mm(out, lhsT, rhs, start, stop, r, w, inc=None):
            if inc is None:
                inc = stop
            S.op('pe', lambda: nc.tensor.matmul(out, lhsT=lhsT, rhs=rhs, start=start, stop=stop), r=r, w=w, inc=inc)

        def tr(out, in_, rows, r, w, inc=True):
            if _CACHE.get('mmtr', 1):
                S.op('pe', lambda: nc.tensor.matmul(out, lhsT=in_, rhs=ident[0:rows, 0:rows], start=True, stop=True), r=r, w=w, inc=inc)
            else:
                S.op('pe', lambda: nc.tensor.transpose(out=out, in_=in_, identity=ident[0:rows, 0:rows]), r=r, w=w, inc=inc)

        def act(out, in_, func, r, w, bias=None, scale=None, accum=None):
            kw = {}
            if bias is not None:
                kw['bias'] = bias
            if scale is not None:
                kw['scale'] = scale
            if accum is not None:
                kw['accum_out'] = accum
            S.op('act', lambda: nc.scalar.activation(out=out, in_=in_, func=func, **kw), r=r, w=w)

        def cp(e, out, in_, r, w):
            if e == 'act':
                S.op('act', lambda: nc.scalar.activation(out=out, in_=in_, func=AF.Copy), r=r, w=w)
            else:
                S.op('dve', lambda: nc.vector.tensor_scalar(out=out, in0=in_, scalar1=1.0, scalar2=None, op0=ALU.mult), r=r, w=w)

        def tt(out, a, b, op, r, w):
            S.op('dve', lambda: nc.vector.tensor_tensor(out=out, in0=a, in1=b, op=op), r=r, w=w)

        def stt(out, in0, scalar, in1, op0, op1, r, w):
            S.op('dve', lambda: nc.vector.scalar_tensor_tensor(out=out, in0=in0, scalar=scalar, in1=in1, op0=op0, op1=op1), r=r, w=w)

        def ts(out, in0, s1, s2, op0, op1, r, w):
            if s2 is None:
                S.op('dve', lambda: nc.vector.tensor_scalar(out=out, in0=in0, scalar1=s1, scalar2=None, op0=op0), r=r, w=w)
            else:
                S.op('dve', lambda: nc.vector.tensor_scalar(out=out, in0=in0, scalar1=s1, scalar2=s2, op0=op0, op1=op1), r=r, w=w)

        def fscr():
            st['f'] = (st['f'] + 1) % 3
            return st['f']

        def new_piece():
            s = st['piece'] % st.get('nslots', 4)
            st['piece'] += 1
            return s

        def slot(s):
            if s < 4:
                return ring[:, s, :]
            return cache[:, 8192 + (s - 4) * 4096:8192 + (s - 3) * 4096]

        recs = {}

        def cached_piece(key, fill):
            s = new_piece()
            if key is None or not _CACHE.get('wcache', 1):
                fill(s)
                return s
            if key in recs:
                l_, i_ = recs[key]
                S.dma('pool', slot(s), wscr[l_][i_], sem="w%d" % s, w=["w%d" % s])
            else:
                assert st.get('pass', 0) == 0, key
                l_ = st['cur_l']
                i_ = st['nrec'].get(l_, 0)
                st['nrec'][l_] = i_ + 1
                assert i_ < 100
                recs[key] = (l_, i_)
                fill(s)
                S.dma('sp', wscr[l_][i_], slot(s), sem="wst%d" % s, r=["w%d" % s])
            return s

        def load_std(W, kc0, nk, c0, ncols):
            def fill(s):
                dst = slot(s)[:, 0:nk * ncols].rearrange("p (k n) -> p k n", k=nk)
                src = W[kc0 * 128:(kc0 + nk) * 128, c0:c0 + ncols].rearrange("(k p) n -> p k n", p=128)
                S.dma('pool', dst, src, sem="w%d" % s, w=["w%d" % s])
            key = None if st.get('cur_l') is None else (W.name, st['cur_l'], kc0, nk, c0, ncols)
            return cached_piece(key, fill)

        def pc(s, k, ncols, m0, mw=128):
            return slot(s)[:, k * ncols + m0:k * ncols + m0 + mw]

        def chk(name):
            if _CACHE.get('stop') == name:
                raise _Stop()

        try:
            S.op('pool', lambda: nc.gpsimd.memset(ident[:], 0.0), w=['ident'])
            S.op('pool', lambda: nc.gpsimd.affine_select(out=ident[:], in_=ident[:], pattern=[[-1, 128]], compare_op=ALU.not_equal, fill=1.0, base=0, channel_multiplier=1), r=['ident'], w=['ident'])
            S.op('pool', lambda: nc.gpsimd.memset(ones[:], 1.0), w=['ones'])
            S.op('pool', lambda: nc.gpsimd.memset(onesr[:], 1.0), w=['onesr'])
            S.op('pool', lambda: nc.gpsimd.memset(epsc[:], EPS), w=['epsc'])
            S.op('pool', lambda: nc.gpsimd.memset(hal[:], 0.0), w=['hal'])
            for _i in range(_CACHE.get('padcopy', 0)):
                S.op('dve', lambda: nc.vector.tensor_copy(out=sm[:, 4:5], in_=sm[:, 5:6]), w=[])
            for _i in range(_CACHE.get('pad', 0)):
                S.op('dve', lambda: nc.vector.memset(sm[:, 4:5], 0.0), w=[])
            ctok = ('pool', S.cnt['pool'])
            for e in ('pe', 'act', 'dve', 'sp'):
                S._wait(e, ctok)
            xst = R46[:, 0:2048]
            for i in range(5):
                S.dma('sp', R46[:, i * 128:(i + 1) * 128], vecs[i * 128:(i + 1) * 128, :], sem="vecs%d" % i, w=['g%d' % i])
            for i in range(5):
                tr(ps[0][:, 0:128], R46[:, i * 128:(i + 1) * 128], 128, r=['g%d' % i], w=['ps0'])
                cp('dve', vT[:, i * 128:(i + 1) * 128], ps[0][:, 0:128], r=['ps0'], w=['vT'])
            for e in range(3):
                act(sT[:, :, e], vT[:, 576 + e * 16:576 + e * 16 + 16], AF.Silu, r=['vT'], w=['sT'])
            S._wait('pe', ('act', S.cnt['act']))
            S._wait('dve', ('act', S.cnt['act']))
            S._wait('act', ('dve', S.cnt['dve']))
            S._wait('pe', ('dve', S.cnt['dve']))

            chk('vt')
            for l in range(NL):
                for cg in range(24):
                    sA = load_std(w_ada[l], 0, 8, cg * 512, 512)
                    sB = load_std(w_ada[l], 8, 8, cg * 512, 512)
                    b = gbank()
                    for j in range(4):
                        m = cg * 4 + j
                        for kc in range(16):
                            s_, k_ = (sA, kc) if kc < 8 else (sB, kc - 8)
                            mm(ps[b][:, 3 * j:3 * j + 3], pc(s_, k_, 512, j * 128), sT[:, kc, :], start=(kc == 0), stop=(kc == 15),
                               r=['w%d' % s_], w=['ps%d' % b], inc=(kc == 15 and j == 3))
                    for j in range(4):
                        m = cg * 4 + j
                        kind = m // 16
                        ts(mod[:, l, :, m], ps[b][:, 3 * j:3 * j + 3], vT[:, l * 96 + m:l * 96 + m + 1],
                           1.0 if kind in (1, 4) else 0.0, ALU.add, ALU.add, r=['ps%d' % b], w=['mod'])
            S._wait('act', ('dve', S.cnt['dve']))

            chk('mod')
            MSH_A, MSC_A, MG_A, MSH_M, MSC_M, MG_M = 0, 1, 2, 3, 4, 5

            def modv(l, e, kind, c):
                return mod[:, l, e, kind * 16 + c:kind * 16 + c + 1]

            def ssq_accum(bank, src_ap, src_keys, T, first, last):
                i = st['sq'] = (st['sq'] + 1) % 2
                act(sq[:, i, 0:T], src_ap, AF.Square, r=src_keys, w=['sq%d' % i])
                mm(ps[bank][:, 0:T], ones[:], sq[:, i, 0:T], start=first, stop=last, r=['sq%d' % i], w=['ps%d' % bank], inc=True)

            def rstd_from(bank, n, slot, T):
                f = fscr()
                act(fsc[:, f, 0:T], ps[bank][:, 0:T], AF.Ln, r=['ps%d' % bank], w=['f%d' % f], scale=1.0 / n, bias=epsc[:, 0:1])
                act(rst[:, slot, 0:T], fsc[:, f, 0:T], AF.Exp, r=['f%d' % f], w=['rst%d' % slot], scale=-0.5)

            def norm_mod(l, blk, kind_sh, kind_sc):
                T = blk['T']
                b = gbank()
                for kc in range(16):
                    ssq_accum(b, xT[:, kc, 0:T], ['x%d' % kc], T, kc == 0, kc == 15)
                rstd_from(b, D, 0, T)
                for kc in range(16):
                    f = fscr()
                    tt(fsc[:, f, 0:T], xT[:, kc, 0:T], rst[:, 0, 0:T], ALU.mult, r=['x%d' % kc, 'rst0'], w=['f%d' % f])
                    for sg in blk['segs']:
                        act(R1[:, kc, sg.c0:sg.c0 + sg.n], fsc[:, f, sg.c0:sg.c0 + sg.n], AF.Identity, r=['f%d' % f], w=['r%d' % kc],
                            scale=modv(l, sg.e, kind_sc, kc), bias=modv(l, sg.e, kind_sh, kc))

            def attention(l, blk, grp, wk):
                q0, nq = grp['q0'], grp['nq']
                nkb = grp['nkb']
                for h in range(NH):
                    g, hh = h // 4, h % 4
                    wkeys = ['w%d' % wk[g]]
                    hb = h % 2
                    kexp = R46[:, hb * 2048:hb * 2048 + 1024].bitcast(BF16)
                    vexp = R46[:, hb * 2048 + 1024:hb * 2048 + 2048].bitcast(BF16)
                    kk = ['g%d' % (hb * 8 + i) for i in range(4)]
                    vk = ['g%d' % (hb * 8 + 4 + i) for i in range(4)]
                    for kb in range(nkb):
                        b = gbank()
                        for kc in range(4):
                            ap_, keys_ = grp['ckv'](kc, kb * 512, 512)
                            mm(ps[b][:, 0:512], pc(wk[g], kc, 1024, hh * 256), ap_, start=(kc == 0), stop=(kc == 3), r=wkeys + keys_, w=['ps%d' % b])
                        cp(alt(), kexp[:, kb * 512:(kb + 1) * 512], ps[b][:, 0:512], r=['ps%d' % b], w=kk)
                        b = gbank()
                        for t4 in range(4):
                            for kc in range(4):
                                ap_, keys_ = grp['ckv'](kc, kb * 512 + t4 * 128, 128)
                                mm(ps[b][:, t4 * 128:(t4 + 1) * 128], ap_, pc(wk[g], kc, 1024, hh * 256 + 128), start=(kc == 0), stop=(kc == 3),
                                   r=wkeys + keys_, w=['ps%d' % b], inc=(kc == 3 and t4 == 3))
                        cp(alt(), vexp[:, kb * 512:(kb + 1) * 512], ps[b][:, 0:512], r=['ps%d' % b], w=vk)
                    own_n = grp['own_n']
                    if own_n:
                        b = gbank()
                        for kc in range(4):
                            mm(ps[b][:, 0:own_n], pc(wk[g], kc, 1024, hh * 256), ownckv[:, kc, q0:q0 + own_n], start=(kc == 0), stop=(kc == 3), r=wkeys + ['ock'], w=['ps%d' % b])
                        cp(alt(), kown[:, 0:own_n], ps[b][:, 0:own_n], r=['ps%d' % b], w=['kown'])
                        b = gbank()
                        for kc in range(4):
                            mm(ps[b][0:own_n, 0:128], ownckv[:, kc, q0:q0 + own_n], pc(wk[g], kc, 1024, hh * 256 + 128), start=(kc == 0), stop=(kc == 3), r=wkeys + ['ock'], w=['ps%d' % b])
                        cp(alt(), vown[0:own_n, :], ps[b][0:own_n, 0:128], r=['ps%d' % b], w=['vown'])
                    tiles = []
                    for kt in range(nkb * 4):
                        if grp['diag'] and kt >= (nkb - 1) * 4:
                            j = kt - (nkb - 1) * 4
                            tiles.append((kt, 128, q0 + j * 128, nq - j * 128, True))
                        else:
                            tiles.append((kt, 128, q0, nq, False))
                    if own_n:
                        tiles.append((-1, own_n, q0, nq, False))
                    qn_ap = lambda c0, n: R1[:, 2 * h, c0:c0 + n]
                    qp_ap = lambda c0, n: R1[:, 2 * h + 1, c0:c0 + n]
                    qkeys = ['r%d' % (2 * h), 'r%d' % (2 * h + 1)]
                    def tile_ops(ti):
                        kt, nk, c0, n, dg = tiles[ti]
                        if kt >= 0:
                            kl = kexp[:, kt * 128:(kt + 1) * 128]
                            krl, krkeys = grp['kr'](kt * 128, 128)
                            vl = vexp[:, kt * 128:(kt + 1) * 128]
                            rk, rv = kk, vk
                        else:
                            kl = kown[:, 0:nk]
                            krl, krkeys = ownkr[:, q0:q0 + nk], ['okr']
                            vl = vown[0:nk, :]
                            rk, rv = ['kown'], ['vown']
                        return kt, nk, c0, n, dg, kl, krl, krkeys, vl, rk, rv

                    def emit_S(ti):
                        kt, nk, c0, n, dg, kl, krl, krkeys, vl, rk, rv = tile_ops(ti)
                        sb_ = 4 + (ti % 2)
                        mm(ps[sb_][0:nk, 0:n], kl, qn_ap(c0, n), start=True, stop=False, r=rk + qkeys, w=['ps%d' % sb_], inc=False)
                        mm(ps[sb_][0:nk, 0:n], krl, qp_ap(c0, n), start=False, stop=True, r=krkeys + qkeys, w=['ps%d' % sb_], inc=True)
                        pi = st['pt'] = (st['pt'] + 1) % 2
                        act(pT[0:nk, pi, 0:n], ps[sb_][0:nk, 0:n], AF.Exp, r=['ps%d' % sb_], w=['pT%d' % pi], scale=ATTN_SCALE)
                        if dg:
                            S.op('dve', lambda: nc.vector.memset(pT[64:128, pi, 0:64], 0.0), r=[], w=['pT%d' % pi])
                        return pi

                    def emit_PV(ti, pi):
                        kt, nk, c0, n, dg, kl, krl, krkeys, vl, rk, rv = tile_ops(ti)
                        first = (ti == 0)
                        last = (ti == len(tiles) - 1)
                        mm(ps[6][:, c0 - q0:c0 - q0 + n], vl, pT[0:nk, pi, 0:n], start=first, stop=last, r=rv + ['pT%d' % pi], w=['ps6'], inc=False)
                        mm(ps[7][:, c0 - q0:c0 - q0 + n], ones[0:nk, :], pT[0:nk, pi, 0:n], start=first, stop=last, r=['pT%d' % pi], w=['ps7'], inc=True)

                    pis = {0: emit_S(0)}
                    for ti in range(len(tiles)):
                        if ti + 1 < len(tiles):
                            pis[ti + 1] = emit_S(ti + 1)
                        emit_PV(ti, pis[ti])
                    f = fscr()
                    S.op('dve', lambda: nc.vector.reciprocal(out=fsc[:, f, 0:nq], in_=ps[7][:, 0:nq]), r=['ps7'], w=['f%d' % f])
                    tt(R1[:, 2 * h, q0:q0 + nq], ps[6][:, 0:nq], fsc[:, f, 0:nq], ALU.mult, r=['ps6', 'f%d' % f], w=['r%d' % (2 * h)])

            def layer_block(l, blk):
                T = blk['T']
                segs = blk['segs']
                tiles = blk['tiles']
                tok0 = blk['tok0']
                samp = blk['samp']
                b_idx = blk['b']
                Wl = w_in[l]
                st['cur_l'] = l
                S.dma('sp', kvt[:], kvn[l:l + 1, :].partition_broadcast(128), sem="kvt", w=['kvt'])
                if l == 0:
                    S.dma('sp', csT[:, 0:T], tabF[:, tok0:tok0 + T], sem="csT", w=['csT'])
                norm_mod(l, blk, MSH_A, MSC_A)
                chk('norm1')
                sA = load_std(Wl, 0, 8, 0, 512)
                sB = load_std(Wl, 8, 8, 0, 512)
                for m in range(4):
                    b = gbank()
                    for kc in range(16):
                        s_, k_ = (sA, kc) if kc < 8 else (sB, kc - 8)
                        mm(ps[b][:, 0:T], pc(s_, k_, 512, m * 128), R1[:, kc, 0:T], start=(kc == 0), stop=(kc == 15), r=['w%d' % s_, 'r%d' % kc], w=['ps%d' % b])
                    cp('dve', ql[:, m, 0:T], ps[b][:, 0:T], r=['ps%d' % b], w=['ql%d' % m])
                    ssq_accum(6, ql[:, m, 0:T], ['ql%d' % m], T, m == 0, m == 3)
                chk('qlat')
                sA = load_std(Wl, 0, 8, 512, 512)
                sB = load_std(Wl, 8, 8, 512, 512)
                for (r0, rows) in tiles:
                    b = gbank()
                    for kc in range(16):
                        s_, k_ = (sA, kc) if kc < 8 else (sB, kc - 8)
                        mm(ps[b][0:rows, 0:512], R1[:, kc, r0:r0 + rows], pc(s_, k_, 512, 0, 512), start=(kc == 0), stop=(kc == 15), r=['w%d' % s_, 'r%d' % kc], w=['ps%d' % b])
                    f = fscr()
                    S.op('dve', lambda: nc.vector.memset(sm[:, 0:1], 0.0), w=['sm'])
                    act(fsc[0:rows, f, 0:512], ps[b][0:rows, 0:512], AF.Square, r=['ps%d' % b], w=['f%d' % f, 'sm'], accum=sm[0:rows, 0:1])
                    act(sm[0:rows, 1:2], sm[0:rows, 0:1], AF.Ln, r=['sm'], w=['sm'], scale=1.0 / 512, bias=epsc[0:rows, 0:1])
                    act(sm[0:rows, 2:3], sm[0:rows, 1:2], AF.Exp, r=['sm'], w=['sm'], scale=-0.5)
                    stt(cst[0:rows, :], ps[b][0:rows, 0:512], sm[0:rows, 2:3], kvt[0:rows, :], ALU.mult, ALU.mult, r=['ps%d' % b, 'sm', 'kvt'], w=['cst'])
                    if samp:
                        S.dma('sp', o_ckvs[l, r0:r0 + rows, :], cst[0:rows, :], sem="cst_o", r=['cst'])
                    else:
                        S.dma('sp', o_ckvp[l, tok0 + r0:tok0 + r0 + rows, :], cst[0:rows, :], sem="cst_o", r=['cst'])
                    b2 = gbank()
                    for c in range(4):
                        tr(ps[b2][:, c * 128:c * 128 + rows], cst[0:rows, c * 128:(c + 1) * 128], rows, r=['cst'], w=['ps%d' % b2], inc=(c == 3))
                    for c in range(4):
                        cp(alt(), ownckv[:, c, r0:r0 + rows], ps[b2][:, c * 128:c * 128 + rows], r=['ps%d' % b2], w=['ock'])
                chk('kv')
                def fill_kpe(s):
                    S.dma('pool', slot(s)[:, 0:1024].rearrange("p (k n) -> p k n", k=16), Wl[:, 1024:1088].rearrange("(k p) n -> p k n", p=128), sem="w%d" % s, w=['w%d' % s])
                s = cached_piece(('kpe', l), fill_kpe)
                for (r0, rows) in tiles:
                    b = gbank()
                    S.dma('sp', ttab[0:rows, :], tabT[tok0 + r0:tok0 + r0 + rows, :], sem="ttab", w=['ttab'])
                    for kc in range(16):
                        mm(ps[b][0:rows, 0:64], R1[:, kc, r0:r0 + rows], pc(s, kc, 64, 0, 64), start=(kc == 0), stop=(kc == 15), r=['w%d' % s, 'r%d' % kc], w=['ps%d' % b])
                    tt(krt[0:rows, 0, 0:64], ps[b][0:rows, 0:64], ttab[0:rows, 0:64], ALU.mult, r=['ps%d' % b, 'ttab'], w=['krtA'])
                    tt(krt[0:rows, 1, 0:32], ps[b][0:rows, 32:64], ttab[0:rows, 64:96], ALU.mult, r=['ps%d' % b, 'ttab'], w=['krtB'])
                    tt(krt[0:rows, 1, 32:64], ps[b][0:rows, 0:32], ttab[0:rows, 96:128], ALU.mult, r=['ps%d' % b, 'ttab', 'krtB'], w=['krtB'])
                    tt(krt[0:rows, 2, 0:64], krt[0:rows, 0, 0:64], krt[0:rows, 1, 0:64], ALU.add, r=['krtA', 'krtB'], w=['krtC'])
                    tt(krt[0:rows, 2, 64:128], krt[0:rows, 0, 0:64], krt[0:rows, 1, 0:64], ALU.add, r=['krtA', 'krtB', 'krtC'], w=['krtC'])
                    if samp:
                        S.dma('sp', o_krs[l, r0:r0 + rows, :], krt[0:rows, 2, 0:64], sem="krt_o", r=['krtC'])
                    else:
                        S.dma('sp', o_krp[l, tok0 + r0:tok0 + r0 + rows, :], krt[0:rows, 2, 0:64], sem="krt_o", r=['krtC'])
                    b2 = gbank()
                    tr(ps[b2][:, 0:rows], krt[0:rows, 2, :], rows, r=['krtC'], w=['ps%d' % b2])
                    cp(alt(), ownkr[:, r0:r0 + rows], ps[b2][:, 0:rows], r=['ps%d' % b2], w=['okr'])
                chk('kpe')
                gcs = R46[:, 0:1024].bitcast(BF16)
                CUW = 516
                cu = R46[:, 1024:1024 + 4 * CUW]
                cukeys = ['g%d' % i for i in range(4, 13)]
                for half in range(2):
                    sA = load_std(Wl, 0, 8, 2112 + half * 512, 512)
                    sB = load_std(Wl, 8, 8, 2112 + half * 512, 512)
                    for j in range(4):
                        b = gbank()
                        for kc in range(16):
                            s_, k_ = (sA, kc) if kc < 8 else (sB, kc - 8)
                            mm(ps[b][:, 0:T], pc(s_, k_, 512, j * 128), R1[:, kc, 0:T], start=(kc == 0), stop=(kc == 15), r=['w%d' % s_, 'r%d' % kc], w=['ps%d' % b])
                        cp('act', gcs[:, j * 512:j * 512 + T], ps[b][:, 0:T], r=['ps%d' % b], w=['g%d' % j])
                    sA = load_std(Wl, 0, 8, 3136 + half * 512, 512)
                    sB = load_std(Wl, 8, 8, 3136 + half * 512, 512)
                    for j in range(4):
                        c = half * 4 + j
                        b = gbank()
                        for kc in range(16):
                            s_, k_ = (sA, kc) if kc < 8 else (sB, kc - 8)
                            mm(ps[b][:, 0:T], pc(s_, k_, 512, j * 128), R1[:, kc, 0:T], start=(kc == 0), stop=(kc == 15), r=['w%d' % s_, 'r%d' % kc], w=['ps%d' % b])
                        for si, sg in enumerate(segs):
                            so = j * CUW + si * (2 + sg.n)
                            if samp:
                                cp('dve', cu[:, so:so + 2], blk['halo'][si][:, :, c], r=['halo%d' % si], w=cukeys)
                            else:
                                cp('dve', cu[:, so:so + 2], hal[:, l, c, :], r=['hal'], w=cukeys)
                            tt(cu[:, so + 2:so + 2 + sg.n], ps[b][:, sg.c0:sg.c0 + sg.n], gcs[:, j * 512 + sg.c0:j * 512 + sg.c0 + sg.n], ALU.mult,
                               r=['ps%d' % b, 'g%d' % j], w=cukeys)
                            if samp or b_idx == 3:
                                cp('dve', cs2[:, :, c] if not samp else blk['cs2'][si][:, :, c], cu[:, so + sg.n:so + sg.n + 2], r=cukeys, w=['cs2_%d' % si])
                            if not samp:
                                cp('dve', hal[:, l, c, :], cu[:, so + sg.n:so + sg.n + 2], r=cukeys, w=['hal'])
                    sA = load_std(Wl, 0, 8, 1088 + half * 512, 512)
                    sB = load_std(Wl, 8, 8, 1088 + half * 512, 512)
                    for j in range(4):
                        c = half * 4 + j
                        b = gbank()
                        for kc in range(16):
                            s_, k_ = (sA, kc) if kc < 8 else (sB, kc - 8)
                            mm(ps[b][:, 0:T], pc(s_, k_, 512, j * 128), R1[:, kc, 0:T], start=(kc == 0), stop=(kc == 15), r=['w%d' % s_, 'r%d' % kc], w=['ps%d' % b])
                        f = fscr()
                        for si, sg in enumerate(segs):
                            so = j * CUW + si * (2 + sg.n)
                            wv = lambda k: vT[:, 464 + l * 24 + k * 8 + c:464 + l * 24 + k * 8 + c + 1]
                            y = fsc[:, f, sg.c0:sg.c0 + sg.n]
                            ts(y, cu[:, so:so + sg.n], wv(0), None, ALU.mult, None, r=cukeys, w=['f%d' % f])
                            stt(y, cu[:, so + 1:so + 1 + sg.n], wv(1), y, ALU.mult, ALU.add, r=cukeys + ['f%d' % f], w=['f%d' % f])
                            stt(y, cu[:, so + 2:so + 2 + sg.n], wv(2), y, ALU.mult, ALU.add, r=cukeys + ['f%d' % f], w=['f%d' % f])
                            tt(convn[:, c, sg.c0:sg.c0 + sg.n], ps[b][:, sg.c0:sg.c0 + sg.n], y, ALU.mult, r=['ps%d' % b, 'f%d' % f], w=['cn%d' % c])
                        ssq_accum(7, convn[:, c, 0:T], ['cn%d' % c], T, c == 0, c == 7)
                if samp or b_idx == 3:
                    for si, sg in enumerate(segs):
                        src = blk['cs2'][si] if samp else cs2
                        for t in range(2):
                            dst = (o_convs[l, si, t] if samp else o_convp[l, t]).rearrange("(c p) -> p c", p=128)
                            S.dma('sp', dst, src[:, t, :], sem="cs2_o%d" % si, r=['cs2_%d' % si], nonc=True)
                rstd_from(7, 1024, 1, T)
                for c in range(8):
                    stt(convn[:, c, 0:T], convn[:, c, 0:T], vT[:, 432 + l * 8 + c:432 + l * 8 + c + 1], rst[:, 1, 0:T], ALU.mult, ALU.mult,
                        r=['cn%d' % c, 'rst1'], w=['cn%d' % c])
                chk('conv')
                rstd_from(6, 512, 0, T)
                for m in range(4):
                    stt(ql[:, m, 0:T], ql[:, m, 0:T], vT[:, 384 + l * 4 + m:384 + l * 4 + m + 1], rst[:, 0, 0:T], ALU.mult, ALU.mult,
                        r=['ql%d' % m, 'rst0'], w=['ql%d' % m])
                for g in range(2):
                    def fill_q2(s, g=g):
                        dst = slot(s).rearrange("p (k h c) -> p k h c", k=4, h=4)
                        srcv = w_uq[l].rearrange("(k p) (h c) -> p k h c", p=128, c=192)
                        for kc in range(4):
                            S.dma('pool', dst[:, kc, :, 0:192], srcv[:, kc, 4 * g:4 * g + 4, :], sem="w%d" % s, w=['w%d' % s])
                            S.dma('pool', dst[:, kc, :, 192:224], srcv[:, kc, 4 * g:4 * g + 4, 160:192], sem="w%d" % s, w=['w%d' % s])
                            S.dma('pool', dst[:, kc, :, 224:256], srcv[:, kc, 4 * g:4 * g + 4, 128:160], sem="w%d" % s, w=['w%d' % s])
                    s = cached_piece(('q2', l, g), fill_q2)
                    for hh in range(4):
                        h = 4 * g + hh
                        for part in range(2):
                            b = gbank()
                            for kc in range(4):
                                mm(ps[b][:, 0:T], slot(s)[:, kc * 1024 + hh * 256 + part * 128:kc * 1024 + hh * 256 + part * 128 + 128], ql[:, kc, 0:T],
                                   start=(kc == 0), stop=(kc == 3), r=['w%d' % s, 'ql%d' % kc], w=['ps%d' % b])
                            if part == 0:
                                cp('act', R1[:, 2 * h, 0:T], ps[b][:, 0:T], r=['ps%d' % b], w=['r%d' % (2 * h)])
                            else:
                                tt(R1[:, 2 * h + 1, 0:T], ps[b][:, 0:T], csT[:, 0:T], ALU.mult, r=['ps%d' % b, 'csT'], w=['r%d' % (2 * h + 1)])
                chk('q')
                if not samp:
                    wk = [load_std(w_ukv[l], 0, 4, g * 1024, 1024) for g in range(2)]

                    def ckv_acc(kc, t0, n):
                        if t0 >= b_idx * 512:
                            return ownckv[:, kc, t0 - b_idx * 512:t0 - b_idx * 512 + n], ['ock']
                        o = (l * 4 + kc) * 1536 + t0
                        return cache[:, o:o + n], ['cache']

                    def kr_acc(t0, n):
                        if t0 >= b_idx * 512:
                            return ownkr[:, t0 - b_idx * 512:t0 - b_idx * 512 + n], ['okr']
                        return krc[:, l * 1536 + t0:l * 1536 + t0 + n], ['krc']
                    attention(l, blk, dict(q0=0, nq=512, nkb=b_idx + 1, ckv=ckv_acc, kr=kr_acc, own_n=0, diag=True), wk)
                    if b_idx < 3:
                        for kc in range(4):
                            o = (l * 4 + kc) * 1536 + b_idx * 512
                            cp(alt(), cache[:, o:o + 512], ownckv[:, kc, :], r=['ock'], w=['cache'])
                        cp(alt(), krc[:, l * 1536 + b_idx * 512:l * 1536 + b_idx * 512 + 512], ownkr[:, :], r=['okr'], w=['krc'])
                else:
                    for si, sg in enumerate(segs):
                        for q4 in range(4):
                            stg = R46[:, (q4 % 2) * 2048:(q4 % 2) * 2048 + 2048]
                            sk = ['g%d' % ((q4 % 2) * 8 + i) for i in range(8)]
                            S.dma('sp', stg.rearrange("p (t n) -> p t n", t=4), pkv[l, si, q4 * 512:(q4 + 1) * 512, :].rearrange("(t p) n -> p t n", p=128),
                                  sem="pst%d" % (q4 % 2), w=sk)
                            for t4 in range(4):
                                b = gbank()
                                for c in range(4):
                                    tr(ps[b][:, c * 128:(c + 1) * 128], stg[:, t4 * 512 + c * 128:t4 * 512 + (c + 1) * 128], 128, r=sk, w=['ps%d' % b], inc=(c == 3))
                                for c in range(4):
                                    o = c * 2048 + q4 * 512 + t4 * 128
                                    cp(alt(), cache[:, o:o + 128], ps[b][:, c * 128:(c + 1) * 128], r=['ps%d' % b], w=['cache'])
                        for q4 in range(4):
                            stg = R46[:, (q4 % 2) * 2048:(q4 % 2) * 2048 + 512]
                            sk = ['g%d' % ((q4 % 2) * 8 + i) for i in range(2)]
                            for dup in range(2):
                                S.dma('sp', stg.rearrange("p (t n) -> p t n", t=4)[:, :, dup * 64:(dup + 1) * 64],
                                      pkr[l, si, q4 * 512:(q4 + 1) * 512, :].rearrange("(t p) n -> p t n", p=128), sem="pst%d" % (q4 % 2), w=sk)
                            b = gbank()
                            for t4 in range(4):
                                tr(ps[b][:, t4 * 128:(t4 + 1) * 128], stg[:, t4 * 128:(t4 + 1) * 128], 128, r=sk, w=['ps%d' % b], inc=(t4 == 3))
                            cp(alt(), krc[:, q4 * 512:(q4 + 1) * 512], ps[b][:, 0:512], r=['ps%d' % b], w=['krc'])
                        wk = [load_std(w_ukv[l], 0, 4, g * 1024, 1024) for g in range(2)]

                        def ckv_acc(kc, t0, n):
                            return cache[:, kc * 2048 + t0:kc * 2048 + t0 + n], ['cache']

                        def kr_acc(t0, n):
                            return krc[:, t0:t0 + n], ['krc']
                        attention(l, blk, dict(q0=sg.c0, nq=sg.n, nkb=4, ckv=ckv_acc, kr=kr_acc, own_n=sg.n, diag=False), wk)
                chk('attn')
                b = gbank()
                for h in range(NH):
                    ssq_accum(b, R1[:, 2 * h, 0:T], ['r%d' % (2 * h)], T, h == 0, h == NH - 1)
                rstd_from(b, 1024, 0, T)
                for h in range(NH):
                    stt(R1[:, 2 * h, 0:T], R1[:, 2 * h, 0:T], vT[:, 400 + l * 8 + h:400 + l * 8 + h + 1], rst[:, 0, 0:T], ALU.mult, ALU.mult,
                        r=['r%d' % (2 * h), 'rst0'], w=['r%d' % (2 * h)])
                for cg in range(4):
                    sA = load_std(w_out[l], 0, 8, cg * 512, 512)
                    sB = load_std(w_out[l], 8, 8, cg * 512, 512)
                    for j in range(4):
                        m = cg * 4 + j
                        b = gbank()
                        for kc in range(16):
                            if kc < 8:
                                mm(ps[b][:, 0:T], pc(sA, kc, 512, j * 128), R1[:, 2 * kc, 0:T], start=(kc == 0), stop=False, r=['w%d' % sA, 'r%d' % (2 * kc)], w=['ps%d' % b])
                            else:
                                mm(ps[b][:, 0:T], pc(sB, kc - 8, 512, j * 128), convn[:, kc - 8, 0:T], start=False, stop=(kc == 15), r=['w%d' % sB, 'cn%d' % (kc - 8)], w=['ps%d' % b])
                        for sg in segs:
                            stt(xT[:, m, sg.c0:sg.c0 + sg.n], ps[b][:, sg.c0:sg.c0 + sg.n], modv(l, sg.e, MG_A, m), xT[:, m, sg.c0:sg.c0 + sg.n], ALU.mult, ALU.add,
                                r=['ps%d' % b, 'x%d' % m], w=['x%d' % m])
                chk('wout')
                norm_mod(l, blk, MSH_M, MSC_M)
                def mlp_bufs(g):
                    ub = (g % 2) * 2048
                    return R46[:, ub:ub + 2048].bitcast(BF16), ['g%d' % ((g % 2) * 8 + i) for i in range(8)]

                def emit_up(g):
                    up, uk = mlp_bufs(g)
                    for half in range(2):
                        sA = load_std(w_up[l], 0, 8, g * 1024 + half * 512, 512)
                        sB = load_std(w_up[l], 8, 8, g * 1024 + half * 512, 512)
                        for j in range(4):
                            b = gbank()
                            for kc in range(16):
                                s_, k_ = (sA, kc) if kc < 8 else (sB, kc - 8)
                                mm(ps[b][:, 0:T], pc(s_, k_, 512, j * 128), R1[:, kc, 0:T], start=(kc == 0), stop=(kc == 15), r=['w%d' % s_, 'r%d' % kc], w=['ps%d' % b])
                            act(rtmp[:, 0:T], ps[b][:, 0:T], AF.Relu, r=['ps%d' % b], w=['rtmp'])
                            jj = half * 4 + j
                            tt(up[:, jj * 512:jj * 512 + T], rtmp[:, 0:T], rtmp[:, 0:T], ALU.mult, r=['rtmp'], w=uk)

                def emit_down(g):
                    up, uk = mlp_bufs(g)
                    for cg in range(4):
                        s = load_std(w_down[l], g * 8, 8, cg * 512, 512)
                        for j in range(4):
                            m = cg * 4 + j
                            b = gbank()
                            for kc in range(8):
                                mm(ps[b][:, 0:T], pc(s, kc, 512, j * 128), up[:, kc * 512:kc * 512 + T], start=(kc == 0), stop=(kc == 7), r=['w%d' % s] + uk, w=['ps%d' % b])
                            for sg in segs:
                                stt(xT[:, m, sg.c0:sg.c0 + sg.n], ps[b][:, sg.c0:sg.c0 + sg.n], modv(l, sg.e, MG_M, m), xT[:, m, sg.c0:sg.c0 + sg.n], ALU.mult, ALU.add,
                                    r=['ps%d' % b, 'x%d' % m], w=['x%d' % m])

                emit_up(0)
                for g in range(8):
                    if g + 1 < 8:
                        emit_up(g + 1)
                    emit_down(g)

            def run_block(blk, xsrc, ydst):
                T = blk['T']
                tiles = blk['tiles']
                gk = ['g%d' % i for i in range(16)]
                for ti, (r0, rows) in enumerate(tiles):
                    stg = R46[:, (ti % 2) * 2048:(ti % 2) * 2048 + 2048]
                    sk = ['g%d' % ((ti % 2) * 8 + i) for i in range(8)]
                    S.dma('sp', stg[0:rows, :], xsrc[r0:r0 + rows, :], sem="xst%d" % (ti % 2), w=sk)
                    chk('xl1')
                    for c4 in range(4):
                        b = gbank()
                        if _CACHE.get('xv', 0) == 3:
                            b = 1
                        for c in range(4):
                            cc = c4 * 4 + c
                            tr(ps[b][:, c * 128:c * 128 + rows], stg[0:rows, cc * 128:(cc + 1) * 128], rows, r=sk, w=['ps%d' % b], inc=(c == 3))
                        chk('xl2')
                        chk('xt%d_%d' % (ti, c4))
                        xv = _CACHE.get('xv', 7)
                        for c in range(4):
                            cc = c4 * 4 + c
                            if xv == 1 and c4 == 1 and c > 0:
                                continue
                            if xv == 2:
                                cc = c
                            if xv == 5 and c4 == 1 and c != 1:
                                continue
                            en = alt()
                            if (xv == 4 and c4 == 1) or xv == 7:
                                en = 'act'
                            if xv == 5 and c4 == 1:
                                en = 'dve'
                            if xv == 6:
                                cp(en, fsc[:, 0, c * 128:c * 128 + rows], ps[b][:, c * 128:c * 128 + rows], r=['ps%d' % b], w=['x%d' % cc])
                            else:
                                cp(en, xT[:, cc, r0:r0 + rows], ps[b][:, c * 128:c * 128 + rows], r=['ps%d' % b], w=['x%d' % cc])
                        if _CACHE.get('serialtr'):
                            S._wait('pe', ('act', S.cnt['act']))
                            S._wait('pe', ('dve', S.cnt['dve']))
                        chk('xl3')
                        chk('xg%d_%d' % (ti, c4))
                chk('xload')
                for l in range(NL):
                    layer_block(l, blk)
                chk('layers')
                b = gbank()
                for kc in range(16):
                    ssq_accum(b, xT[:, kc, 0:T], ['x%d' % kc], T, kc == 0, kc == 15)
                rstd_from(b, D, 0, T)
                for kc in range(16):
                    stt(xT[:, kc, 0:T], xT[:, kc, 0:T], vT[:, 560 + kc:561 + kc], rst[:, 0, 0:T], ALU.mult, ALU.mult, r=['x%d' % kc, 'rst0'], w=['x%d' % kc])
                for ti, (r0, rows) in enumerate(tiles):
                    stg = R46[:, (ti % 2) * 2048:(ti % 2) * 2048 + 2048]
                    sk = ['g%d' % ((ti % 2) * 8 + i) for i in range(8)]
                    for c4 in range(4):
                        b = gbank()
                        for c in range(4):
                            cc = c4 * 4 + c
                            tr(ps[b][0:rows, c * 128:(c + 1) * 128], xT[:, cc, r0:r0 + rows], 128, r=['x%d' % cc], w=['ps%d' % b], inc=(c == 3))
                        cp(alt(), stg[0:rows, c4 * 512:(c4 + 1) * 512], ps[b][0:rows, 0:512], r=['ps%d' % b], w=sk)
                    S.dma('sp', ydst[r0:r0 + rows, :], stg[0:rows, :], sem="yst%d" % (ti % 2), r=sk)

            for bi in range(NBLK):
                blk = dict(T=512, segs=[Seg(0, 512, 0)], tiles=[(i * 128, 128) for i in range(4)], tok0=bi * 512, samp=False, b=bi)
                st['pass'] = bi
                run_block(blk, xp[bi * 512:(bi + 1) * 512, :], o_yp[bi * 512:(bi + 1) * 512, :])
                if bi == 0 and _CACHE.get('wcache', 1):
                    for s_ in range(4):
                        if ("wst%d" % s_) in S.cnt:
                            S._wait('pool', ("wst%d" % s_, S.cnt["wst%d" % s_]))
            halo_t = [sbt("halo%d" % i, [128, 2, 8], F32) for i in range(2)]
            cs2_t = [sbt("cs2s%d" % i, [128, 2, 8], F32) for i in range(2)]
            sblk = dict(T=32, segs=[Seg(0, 16, 1), Seg(16, 16, 2)], tiles=[(0, 32)], tok0=2048, samp=True, b=4, halo=halo_t, cs2=cs2_t)
            orig_layer_block = layer_block

            def layer_block_s(l, blk):
                for si in range(2):
                    for t in range(2):
                        S.dma('sp', halo_t[si][:, t, :], sconv[l, si, t].rearrange("(c p) -> p c", p=128), sem="halo%d" % si, w=['halo%d' % si], nonc=True)
                orig_layer_block(l, blk)
            layer_block = layer_block_s
            if SAMPLE:
                if _CACHE.get('nslots_s', 8) > 4:
                    for en_ in ('pe', 'act', 'dve'):
                        if S.cnt[en_] > 0:
                            S._wait('pool', (en_, S.cnt[en_]))
                    st['nslots'] = _CACHE.get('nslots_s', 8)
                st['pass'] = 4
                run_block(sblk, xs, o_ys)
        except _Stop:
            pass
        S.finish('sp')
        _CACHE['counts'] = dict(S.cnt)
        _CACHE['nwaits'] = S.nwaits
    return nc


_CACHE = {}


def _tables():
    half = 32
    inv = (10000.0 ** (-np.arange(half, dtype=np.float32) / half)).astype(np.float32)
    pos = np.concatenate([np.arange(2048), 2048 + np.arange(16), 2048 + np.arange(16)]).astype(np.float32)
    ang = pos[:, None] * inv[None, :]
    cos, sin = np.cos(ang).astype(np.float32), np.sin(ang).astype(np.float32)
    tabT = np.concatenate([cos, cos, -sin, sin], axis=1).astype(np.float32)
    tabF = np.ascontiguousarray(tabT.T)
    return tabF, np.ascontiguousarray(tabT)


def kernel(x_prompt, x_sample, c_prompt, c_sample, cache_kv_latent, cache_k_rope, state_conv,
           w_ada, b_ada, w_in, q_norm, w_uq, kv_norm, w_ukv, conv_w,
           attn_out_norm, conv_out_norm, w_out, w_up, w_down, final_norm):
    f = lambda a: np.ascontiguousarray(np.asarray(a, dtype=np.float32))
    x_prompt, x_sample, c_prompt, c_sample = f(x_prompt), f(x_sample), f(c_prompt), f(c_sample)
    cache_kv_latent, cache_k_rope, state_conv = f(cache_kv_latent), f(cache_k_rope), f(state_conv)
    w_ada, b_ada, w_in, q_norm, w_uq, kv_norm, w_ukv = f(w_ada), f(b_ada), f(w_in), f(q_norm), f(w_uq), f(kv_norm), f(w_ukv)
    conv_w, attn_out_norm, conv_out_norm, w_out, w_up, w_down, final_norm = f(conv_w), f(attn_out_norm), f(conv_out_norm), f(w_out), f(w_up), f(w_down), f(final_norm)
    if 'nc' not in _CACHE:
        _CACHE['nc'] = build_program()
    nc = _CACHE['nc']
    tabF, tabT = _tables()
    common = np.concatenate([b_ada.reshape(384, 128), q_norm.reshape(16, 128), attn_out_norm.reshape(32, 128),
                             conv_out_norm.reshape(32, 128), conv_w.reshape(96, 128), final_norm.reshape(16, 128)], axis=0)
    in_maps = []
    for c in range(8):
        vecs = np.zeros((640, 128), np.float32)
        vecs[0:576] = common
        vecs[576:592] = c_prompt[c].reshape(16, 128)
        vecs[592:608] = c_sample[2 * c].reshape(16, 128)
        vecs[608:624] = c_sample[2 * c + 1].reshape(16, 128)
        in_maps.append(dict(
            xp=x_prompt[c], xs=np.ascontiguousarray(x_sample[2 * c:2 * c + 2].reshape(32, D)), vecs=vecs, kvn=kv_norm,
            pkv=np.ascontiguousarray(cache_kv_latent[:, 2 * c:2 * c + 2]), pkr=np.ascontiguousarray(cache_k_rope[:, 2 * c:2 * c + 2]),
            sconv=np.ascontiguousarray(state_conv[:, 2 * c:2 * c + 2]), tabF=tabF, tabT=tabT,
            w_ada=w_ada, w_in=w_in, w_uq=w_uq, w_ukv=w_ukv, w_out=w_out, w_up=w_up, w_down=w_down))
    res = run_bass_kernel_spmd(nc, in_maps, core_ids=list(range(8)))
    R = res.results
    y_p = np.stack([R[c]["o_yp"] for c in range(8)], axis=0)
    y_s = np.concatenate([R[c]["o_ys"].reshape(2, 16, D) for c in range(8)], axis=0)
    ckv_p = np.stack([R[c]["o_ckvp"] for c in range(8)], axis=1)
    kr_p = np.stack([R[c]["o_krp"] for c in range(8)], axis=1)
    conv_p = np.stack([R[c]["o_convp"] for c in range(8)], axis=1)
    ckv_s = np.concatenate([R[c]["o_ckvs"].reshape(L, 2, 16, 512) for c in range(8)], axis=1)
    kr_s = np.concatenate([R[c]["o_krs"].reshape(L, 2, 16, 64) for c in range(8)], axis=1)
    conv_s = np.concatenate([R[c]["o_convs"] for c in range(8)], axis=1)
    return (y_p.astype(np.float32), y_s.astype(np.float32), ckv_p.astype(np.float32), kr_p.astype(np.float32),
            conv_p.astype(np.float32), ckv_s.astype(np.float32), kr_s.astype(np.float32), conv_s.astype(np.float32))
```
